# Optimizing a Trainium2 kernel written in Bass

```python
import jax, jax.numpy as jnp
from jax import lax
import numpy as np

D_MODEL = 1024
BATCH = 1
SEQ = 16384
DEPTH = 1
DEC_BATCH = 2
DEC_SEQ = 8192
PAST_LEN = 128

N_MEM = 256
GRID_W = 64
EPS = 1e-6

GLA_HEADS = 4
GLA_DK = 128
GLA_DV = 256
GLA_RANK = 16
GLA_TAU = 16.0
GLA_CHUNK = 64
GLA_QK = GLA_HEADS * GLA_DK
GLA_V = GLA_HEADS * GLA_DV

NAT_HEADS = 8
NAT_DH = 64
NAT_KH = 8
NAT_KW = 16
NAT_W = NAT_HEADS * NAT_DH

MEM_HEADS = 4
MEM_DH = 128
MEM_W = MEM_HEADS * MEM_DH

MIX_W = GLA_V + NAT_W + MEM_W
SPLIT_SIZES = (GLA_QK, GLA_QK, GLA_V, GLA_V, GLA_RANK, GLA_RANK,
               NAT_W, NAT_W, NAT_W, NAT_W, MEM_W, MEM_W)
IN_W = 6176

kernel_name = "hybrid_gla_natten_mem_encoder"


def rms_norm(x, g):
    xf = x.astype(jnp.float32)
    y = xf * lax.rsqrt(jnp.mean(xf * xf, axis=-1, keepdims=True) + EPS)
    return (y * g.astype(jnp.float32)).astype(x.dtype)


def _gla_scan(q, k, v, log_a):
    B, N, H, DK = q.shape
    DV = v.shape[-1]
    nc = N // GLA_CHUNK

    def to_chunks(t):
        t = t.astype(jnp.float32).reshape(B, nc, GLA_CHUNK, H, t.shape[-1])
        return jnp.moveaxis(t, 1, 0)

    qc, kc, vc, ac = to_chunks(q), to_chunks(k), to_chunks(v), to_chunks(log_a)
    causal = jnp.tril(jnp.ones((GLA_CHUNK, GLA_CHUNK), dtype=bool))[None, :, :, None, None]

    def step(S, inp):
        qi, ki, vi, ai = inp
        b = jnp.cumsum(ai, axis=1)
        diff = jnp.where(causal, b[:, :, None] - b[:, None, :], -jnp.inf)
        att = jnp.einsum('bihd,bjhd,bijhd->bhij', qi, ki, jnp.exp(diff))
        o_intra = jnp.einsum('bhij,bjhv->bihv', att, vi)
        o_inter = jnp.einsum('bihd,bhdv->bihv', qi * jnp.exp(b), S)
        b_last = b[:, -1]
        k_dec = ki * jnp.exp(b_last[:, None] - b)
        S_new = jnp.exp(b_last)[..., None] * S + jnp.einsum('bjhd,bjhv->bhdv', k_dec, vi)
        return S_new, o_intra + o_inter

    S0 = jnp.zeros((B, H, DK, DV), jnp.float32)
    _, o = lax.scan(step, S0, (qc, kc, vc, ac))
    return jnp.moveaxis(o, 0, 1).reshape(B, N, H, DV)


def gla_branch(q, k, v, g, lr_f, lr_b, w_f, b_f, w_b, b_b, norm_g):
    B, N, _ = q.shape
    log_a_f = jax.nn.log_sigmoid((lr_f @ w_f + b_f).astype(jnp.float32)) / GLA_TAU
    log_a_b = jax.nn.log_sigmoid((lr_b @ w_b + b_b).astype(jnp.float32)) / GLA_TAU
    qh = q.reshape(B, N, GLA_HEADS, GLA_DK) * (GLA_DK ** -0.5)
    kh = k.reshape(B, N, GLA_HEADS, GLA_DK)
    vh = v.reshape(B, N, GLA_HEADS, GLA_DV)
    af = log_a_f.reshape(B, N, GLA_HEADS, GLA_DK)
    ab = log_a_b.reshape(B, N, GLA_HEADS, GLA_DK)
    o_f = _gla_scan(qh, kh, vh, af)
    o_b = _gla_scan(qh[:, ::-1], kh[:, ::-1], vh[:, ::-1], ab[:, ::-1])[:, ::-1]
    o = rms_norm(o_f + o_b, norm_g)
    o = o.reshape(B, N, GLA_V) * jax.nn.silu(g.astype(jnp.float32))
    return o.astype(q.dtype)


def nat_branch(q, k, v, g, rpb):
    B, N, _ = q.shape
    rows = N // GRID_W
    kh = min(NAT_KH, rows)
    shp = (B, rows, GRID_W, NAT_HEADS, NAT_DH)
    qg = q.reshape(shp) * (NAT_DH ** -0.5)
    kg_all = k.reshape(shp)
    vg_all = v.reshape(shp)
    r = jnp.arange(rows)
    row_start = jnp.clip(r - kh // 2, 0, rows - kh)
    row_idx = row_start[:, None] + jnp.arange(kh)[None, :]
    kg = kg_all[:, row_idx]
    vg = vg_all[:, row_idx]
    c = jnp.arange(GRID_W)
    col_start = jnp.clip(c - NAT_KW // 2, 0, GRID_W - NAT_KW)
    col_in = (c[None, :] >= col_start[:, None]) & (c[None, :] < col_start[:, None] + NAT_KW)
    drow = row_idx - r[:, None] + (NAT_KH - 1)
    dcol = jnp.clip(c[None, :] - c[:, None] + (NAT_KW - 1), 0, 2 * NAT_KW - 2)
    bias = rpb[:, drow[:, None, :, None], dcol[None, :, None, :]]
    s = jnp.einsum('brqhd,brkwhd->bhrqkw', qg, kg).astype(jnp.float32)
    s = jnp.where(col_in[:, None, :], s + bias.astype(jnp.float32), -jnp.inf)
    p = jax.nn.softmax(s, axis=(-2, -1)).astype(v.dtype)
    o = jnp.einsum('bhrqkw,brkwhd->brqhd', p, vg).reshape(B, N, NAT_W)
    return (o.astype(jnp.float32) * jax.nn.silu(g.astype(jnp.float32))).astype(q.dtype)


def mem_branch(q, g, mem, mem_norm_g, w_mem_kv):
    B, N, _ = q.shape
    m = rms_norm(mem, mem_norm_g)
    mk, mv = jnp.split(m @ w_mem_kv, 2, axis=-1)
    M = mem.shape[1]
    qh = q.reshape(B, N, MEM_HEADS, MEM_DH) * (MEM_DH ** -0.5)
    mk = mk.reshape(B, M, MEM_HEADS, MEM_DH)
    mv = mv.reshape(B, M, MEM_HEADS, MEM_DH)
    s = jnp.einsum('bnhd,bmhd->bhnm', qh, mk).astype(jnp.float32)
    p = jax.nn.softmax(s, axis=-1).astype(mv.dtype)
    o = jnp.einsum('bhnm,bmhd->bnhd', p, mv).reshape(B, N, MEM_W)
    return (o.astype(jnp.float32) * jax.nn.silu(g.astype(jnp.float32))).astype(q.dtype)


def encoder_layer(x, mem, pre_g, w_in, gw_f, gb_f, gw_b, gb_b, gla_ng, rpb,
                  mem_ng, w_mem_kv, w_out, post_g):
    h = rms_norm(x, pre_g)
    proj = h @ w_in
    offs = np.cumsum(SPLIT_SIZES)[:-1].tolist()
    (gq, gk, gv, gg, glr_f, glr_b, nq, nk, nv, ng, mq, mg) = jnp.split(proj, offs, axis=-1)
    o_gla = gla_branch(gq, gk, gv, gg, glr_f, glr_b, gw_f, gb_f, gw_b, gb_b, gla_ng)
    o_nat = nat_branch(nq, nk, nv, ng, rpb)
    o_mem = mem_branch(mq, mg, mem, mem_ng, w_mem_kv)
    mixed = jnp.concatenate([o_gla, o_nat, o_mem], axis=-1)
    out = mixed @ w_out
    return (x + rms_norm(out, post_g)).astype(x.dtype)


def run_trunk(x, mem, pre_norm_g, w_in, gla_w_fwd, gla_b_fwd, gla_w_bwd, gla_b_bwd,
              gla_norm_g, nat_rpb, mem_norm_g, w_mem_kv, w_out, post_norm_g):
    for l in range(DEPTH):
        x = encoder_layer(x, mem, pre_norm_g[l], w_in[l], gla_w_fwd[l], gla_b_fwd[l],
                          gla_w_bwd[l], gla_b_bwd[l], gla_norm_g[l], nat_rpb[l],
                          mem_norm_g[l], w_mem_kv[l], w_out[l], post_norm_g[l])
    return x


def setup_inputs(seed: int = 0) -> dict:
    key = jax.random.key(seed)
    ks = jax.random.split(key, 20)
    f32 = jnp.float32
    nrm = lambda k, s, sc: jax.random.normal(k, s, f32) * sc
    return {
        "x_prompt": nrm(ks[0], (BATCH, SEQ, D_MODEL), 1.0),
        "x_sample": nrm(ks[1], (DEC_BATCH, DEC_SEQ, D_MODEL), 1.0),
        "mem_prompt": nrm(ks[2], (BATCH, N_MEM, D_MODEL), 1.0),
        "mem_sample": nrm(ks[3], (DEC_BATCH, N_MEM, D_MODEL), 1.0),
        "pre_norm_g": 1.0 + nrm(ks[4], (DEPTH, D_MODEL), 0.01),
        "w_in": nrm(ks[5], (DEPTH, D_MODEL, IN_W), D_MODEL ** -0.5),
        "gla_w_fwd": nrm(ks[6], (DEPTH, GLA_RANK, GLA_QK), GLA_RANK ** -0.5),
        "gla_b_fwd": nrm(ks[7], (DEPTH, GLA_QK), 0.1),
        "gla_w_bwd": nrm(ks[8], (DEPTH, GLA_RANK, GLA_QK), GLA_RANK ** -0.5),
        "gla_b_bwd": nrm(ks[9], (DEPTH, GLA_QK), 0.1),
        "gla_norm_g": 1.0 + nrm(ks[10], (DEPTH, GLA_DV), 0.01),
        "nat_rpb": nrm(ks[11], (DEPTH, NAT_HEADS, 2 * NAT_KH - 1, 2 * NAT_KW - 1), 0.02),
        "mem_norm_g": 1.0 + nrm(ks[12], (DEPTH, D_MODEL), 0.01),
        "w_mem_kv": nrm(ks[13], (DEPTH, D_MODEL, 2 * MEM_W), D_MODEL ** -0.5),
        "w_out": nrm(ks[14], (DEPTH, MIX_W, D_MODEL), MIX_W ** -0.5),
        "post_norm_g": 1.0 + nrm(ks[15], (DEPTH, D_MODEL), 0.01),
    }


def reference(x_prompt, x_sample, mem_prompt, mem_sample, pre_norm_g, w_in,
              gla_w_fwd, gla_b_fwd, gla_w_bwd, gla_b_bwd, gla_norm_g, nat_rpb,
              mem_norm_g, w_mem_kv, w_out, post_norm_g):
    y_prompt = run_trunk(x_prompt, mem_prompt, pre_norm_g, w_in, gla_w_fwd, gla_b_fwd,
                         gla_w_bwd, gla_b_bwd, gla_norm_g, nat_rpb, mem_norm_g,
                         w_mem_kv, w_out, post_norm_g)
    y_sample = run_trunk(x_sample, mem_sample, pre_norm_g, w_in, gla_w_fwd, gla_b_fwd,
                         gla_w_bwd, gla_b_bwd, gla_norm_g, nat_rpb, mem_norm_g,
                         w_mem_kv, w_out, post_norm_g)
    return (y_prompt, y_sample)
```

```python
import numpy as np
from contextlib import ExitStack
import concourse.bass as bass
import concourse.mybir as mybir
from concourse.bass_utils import run_bass_kernel_spmd

F32 = mybir.dt.float32
BF16 = mybir.dt.bfloat16
AF = mybir.ActivationFunctionType
ALU = mybir.AluOpType

PE, ACT, DVE, POOL, SP = "tensor", "scalar", "vector", "gpsimd", "sync"
ENGS = (PE, ACT, DVE, POOL, SP)

NTOK = 4096
NTILE = 32
NBLK = 8
GH = 512
NH = 256
NEXT = NTOK + 2 * NH
EPS = 1e-6
MASKV = -30000.0

C_GQ, C_GK, C_GV, C_GG, C_LRF, C_LRB = 0, 512, 1024, 2048, 3072, 3088
C_NQ, C_NK, C_NV, C_NG, C_MQ, C_MG = 3104, 3616, 4128, 4640, 5152, 5664
IN_W = 6176

SP_TILES = [(0, 0), (0, 1), (1, 0), (1, 1), (2, 0), (3, 0), (61, 3), (62, 3), (63, 2), (63, 3)]


class T:
    __slots__ = ("name", "w", "r")

    def __init__(self, name=""):
        self.name = name
        self.w = None
        self.r = {}


class Kern:
    def __init__(self, nc, stack):
        self.nc = nc
        self.stack = stack
        self.prog = {e: [] for e in ENGS}
        self.sems = {}
        self.cnt = {}
        self.seen = {e: {} for e in ENGS}
        for e in ENGS:
            self._mksem(e)
        self.n_dsem = 0

    def _mksem(self, key):
        h = self.stack.enter_context(self.nc.semaphore("s_" + str(key)))
        self.sems[key] = h
        self.cnt[key] = 0
        return key

    def dsem(self, name=None):
        self.n_dsem += 1
        return self._mksem(name or f"d{self.n_dsem}")

    def _collect(self, eng, reads, writes, same_ok):
        deps = {}

        def need(k, v):
            if k == eng and same_ok:
                return
            if v > deps.get(k, 0):
                deps[k] = v
        for t in reads:
            if t.w is not None:
                need(*t.w)
        for t in writes:
            if t.w is not None:
                need(*t.w)
            for k, v in t.r.items():
                need(k, v)
        seen = self.seen[eng]
        out = []
        for k, v in deps.items():
            if seen.get(k, 0) >= v:
                continue
            seen[k] = v
            out.append((k, v))
        return out

    def op(self, eng, fn, reads=(), writes=(), same_ok=None):
        if same_ok is None:
            same_ok = (eng == PE)
        waits = self._collect(eng, reads, writes, same_ok)
        self.cnt[eng] += 1
        v = self.cnt[eng]
        self.prog[eng].append((waits, fn, eng, 1))
        for t in writes:
            t.w = (eng, v)
            t.r = {}
        for t in reads:
            if t not in writes:
                t.r[eng] = v
        return v

    def dma(self, q, fn, sem, reads=(), writes=()):
        waits = self._collect(q, reads, writes, False)
        self.cnt[sem] += 16
        v = self.cnt[sem]
        self.prog[q].append((waits, fn, sem, 16))
        for t in writes:
            t.w = (sem, v)
            t.r = {}
        for t in reads:
            if t not in writes:
                t.r[sem] = v
        return v

    def barrier(self):
        snap = dict(self.cnt)
        for e in ENGS:
            waits = []
            for k, v in snap.items():
                if k == e or v == 0:
                    continue
                if self.seen[e].get(k, 0) >= v:
                    continue
                self.seen[e][k] = v
                waits.append((k, v))
            if waits:
                self.prog[e].append((waits, None, None, 0))

    def final_wait(self, eng=SP):
        waits = [(k, v) for k, v in self.cnt.items() if v > 0 and k != eng]
        self.prog[eng].append((waits, None, None, 0))

    def replay(self, block):
        sems = self.sems
        for e in ENGS:
            prog = self.prog[e]

            def body(h, prog=prog):
                for waits, fn, key, inc in prog:
                    for k, v in waits:
                        h.wait_ge(sems[k], v)
                    if fn is not None:
                        fn(h).then_inc(sems[key], inc)
            getattr(block, e)(body)


class Ring:
    def __init__(self, bufs, ts=None, banks=None):
        self.bufs = bufs
        self.ts = ts if ts is not None else [T() for _ in bufs]
        self.banks = banks
        self.i = 0

    def next_bank(self):
        j = self.i % len(self.bufs)
        self.i += 1
        return self.banks[j], self.ts[j]

    def next(self):
        j = self.i % len(self.bufs)
        self.i += 1
        return self.bufs[j], self.ts[j]


def build_nc(dbg=None):
    dbg = dbg or {}
    dbg_specs = dbg.get("outs", {})
    nc = bass.Bass("TRN2", target_bir_lowering=False)
    din = lambda n, s, d=F32: nc.dram_tensor(n, list(s), d, kind="ExternalInput").ap()
    x_main = din("x_main", [NTOK, 1024])
    x_gh = din("x_gh", [2 * GH, 1024])
    x_nh = din("x_nh", [2 * NH, 1024])
    mem_d = din("mem", [256, 1024])
    w_in = din("w_in", [1024, IN_W])
    w_out = din("w_out", [2048, 1024])
    w_mkv = din("w_mkv", [1024, 1024])
    pre_g_d = din("pre_g", [128, 8, 128])
    mem_g_d = din("mem_g", [128, 8, 128])
    post_g_d = din("post_g", [128, 1024])
    gnorm_d = din("gnorm", [128, 256])
    gnorm_fm_d = din("gnorm_fm", [128, 2])
    gw_f_d = din("gw_f", [16, 512])
    gw_b_d = din("gw_b", [16, 512])
    gb_d = din("gb", [128, 8])
    nat_gen_d = din("nat_gen", [128, 8, 512])
    nat_sp_d = din("nat_sp", [128, 10, 8, 64])
    ident_d = din("ident", [128, 128])
    cmask_d = din("cmask", [128, 256])
    rmask_d = din("rmask", [128, 512])
    y_out = nc.dram_tensor("y", [NTOK, 1024], F32, kind="ExternalOutput").ap()
    mixT_d = nc.dram_tensor("mixT_scr", [16, 128, NTOK], BF16, kind="Internal").ap()
    cs_d = nc.dram_tensor("cs_scr", [8, 128, NTOK], F32, kind="Internal").ap()
    wbf_d = nc.dram_tensor("wbf_scr", [128, 8, IN_W], BF16, kind="Internal").ap()
    wkvbf_d = nc.dram_tensor("wkvbf_scr", [128, 8, 1024], BF16, kind="Internal").ap()
    wobf_d = nc.dram_tensor("wobf_scr", [128, 16, 1024], BF16, kind="Internal").ap()
    dbg_out = {}
    for name, (shape, dt) in dbg_specs.items():
        dbg_out[name] = nc.dram_tensor("dbg_" + name, list(shape), dt, kind="ExternalOutput").ap()

    with ExitStack() as top:
        K = Kern(nc, top)

        def sbuf(st, n, s, d):
            return st.enter_context(nc.sbuf_tensor(n, list(s), d))

        def psum(st, n, s, d):
            return st.enter_context(nc.psum_tensor(n, list(s), d))

        pb = Ring([psum(top, f"pb{i}", [128, 512], F32) for i in range(3)])
        _phb = [psum(top, f"phb{i}", [128, 512], F32) for i in range(3)]
        ph = Ring([_phb[i][:, 0:256] for i in range(3)])
        _ptb = [psum(top, f"ptb{i}", [128, 8, 128], BF16) for i in range(2)]
        ptb = _ptb[0]
        pt = Ring([_ptb[i][:, 0, :] for i in range(2)], banks=_ptb)
        pt_full_T = [pt.ts[0]]

        hT = sbuf(top, "hT_sb", [128, 8, NTOK], BF16)
        t_hT = [T() for _ in range(NTILE)]
        ident = sbuf(top, "ident_sb", [128, 128], BF16)
        t_ident = T()
        cmask = sbuf(top, "cmask_sb", [128, 256], BF16)
        t_cmask = T()
        t_pre_g = T()
        S0 = sbuf(top, "S0_sb", [128, 8, 256], F32)
        t_S0 = [T() for _ in range(8)]
        cf32 = sbuf(top, "cf32", [128, 512], F32)
        t_cf32 = T()
        junk = sbuf(top, "junk_sb", [128, 1024], BF16)
        t_junk = T()
        wsem = [K.dsem() for _ in range(3)]
        wst_box = [None]
        cbf_box = [None]
        csem_n = [0]
        mix_stage = Ring([sbuf(top, f"mixst{i}", [128, 2, 512], BF16) for i in range(2)])
        mix_q = {id(t): [T() for _ in range(4)] for t in mix_stage.ts}
        mixsem = [K.dsem() for _ in range(2)]
        t_mixT = [[T() for _ in range(NBLK)] for _ in range(16)]
        dbgsem = K.dsem("dbg")

        def new_csem():
            csem_n[0] += 1
            return K.dsem(f"const{csem_n[0]}")

        def dbg_dump(name, ap, reads):
            if name in dbg_out:
                K.dma(SP, lambda e: e.dma_start(out=dbg_out[name], in_=ap), dbgsem, reads=reads)

        def load_const_bf16(dst, t_dst, src_ap, ncol):
            K.dma(SP, lambda e: e.dma_start(out=cf32[:, 0:ncol], in_=src_ap), new_csem(), writes=[t_cf32])
            K.op(DVE, lambda e: e.tensor_copy(out=dst, in_=cf32[:, 0:ncol]), reads=[t_cf32], writes=[t_dst])

        load_const_bf16(ident[:], t_ident, ident_d[:, :], 128)
        load_const_bf16(cmask[:], t_cmask, cmask_d[:, :], 256)

        OFF_G = lambda h: h * 768
        OFF_LR = 3072
        OFF_N = lambda j: 3104 + j * 512
        OFF_M = lambda h: 5152 + h * 256
        conv_list = []
        for h in range(4):
            conv_list += [("in", C_GQ + h * 128, 128, OFF_G(h)), ("in", C_GK + h * 128, 128, OFF_G(h) + 128),
                          ("in", C_GV + h * 256, 128, OFF_G(h) + 256), ("in", C_GV + h * 256 + 128, 128, OFF_G(h) + 384),
                          ("in", C_GG + h * 256, 128, OFF_G(h) + 512), ("in", C_GG + h * 256 + 128, 128, OFF_G(h) + 640)]
        conv_list.append(("in", C_LRF, 32, OFF_LR))
        for j in range(4):
            conv_list += [("in", C_NQ + j * 128, 128, OFF_N(j)), ("in", C_NK + j * 128, 128, OFF_N(j) + 128),
                          ("in", C_NV + j * 128, 128, OFF_N(j) + 256), ("in", C_NG + j * 128, 128, OFF_N(j) + 384)]
        for h in range(4):
            conv_list += [("in", C_MQ + h * 128, 128, OFF_M(h)), ("in", C_MG + h * 128, 128, OFF_M(h) + 128)]
        for c0 in range(0, 1024, 128):
            conv_list.append(("kv", c0, 128, c0))
        for kh in range(2):
            for c0 in range(0, 1024, 128):
                conv_list.append(("out", c0, 128, kh * 8 * 1024 + c0))
        def _prio(it):
            kind, sc, n, dc = it
            if kind == "in" and dc == OFF_LR:
                return 0
            if kind == "in" and dc < 3072:
                return 1 if 128 <= (dc % 768) < 512 else 6
            if kind == "in" and dc < 5152:
                return 2 if dc < OFF_N(1) else 3
            if kind == "kv":
                return 4
            if kind == "in":
                return 5
            return 7
        conv_list.sort(key=_prio)
        merged = []
        for it in conv_list:
            if merged:
                k0, s0_, n0, d0 = merged[-1]
                if k0 == it[0] and n0 == 128 and it[2] == 128 and s0_ + n0 == it[1] and d0 + n0 == it[3] and k0 != "out":
                    merged[-1] = (k0, s0_, 256, d0)
                    continue
            merged.append(it)
        conv_list = merged
        n_early = sum(1 for it in conv_list if _prio(it) <= 2)
        conv_limit = [n_early]
        conv_split = [((DVE, 0, 5), (POOL, 5, 8))]
        t_wbf = {}
        cbsem = [K.dsem() for _ in range(3)]
        conv_pos = [0]
        conv_g = {}

        conv_loaded = []

        def conv_load():
            if conv_pos[0] >= min(len(conv_list), conv_limit[0]):
                return
            kind, sc, n, dc = conv_list[conv_pos[0]]
            conv_pos[0] += 1
            wst = wst_box[0]
            buf, tb = wst.next()
            sem = wsem[(wst.i - 1) % 3]
            if kind == "in":
                src = w_in[:, sc:sc + n].rearrange("(c p) n -> p c n", p=128)
                dst = wbf_d[:, :, dc:dc + n]
                grep, t_g = conv_g["pre"]
            elif kind == "kv":
                src = w_mkv[:, sc:sc + n].rearrange("(c p) n -> p c n", p=128)
                dst = wkvbf_d[:, :, dc:dc + n]
                grep, t_g = conv_g["mem"]
            else:
                kh, c0 = dc // 8192, dc % 8192
                src = w_out[kh * 1024:(kh + 1) * 1024, c0:c0 + n].rearrange("(c p) n -> p c n", p=128)
                dst = wobf_d[:, kh * 8:(kh + 1) * 8, c0:c0 + n]
                grep, t_g = None, None
            K.dma(SP, lambda e: e.dma_start(out=buf[:, :, 0:n], in_=src), sem, writes=[tb])
            conv_loaded.append((kind, n, dc, buf, tb, dst, grep, t_g))

        def conv_finish():
            if not conv_loaded:
                return
            kind, n, dc, buf, tb, dst, grep, t_g = conv_loaded.pop(0)
            cbf = cbf_box[0]
            ob, tob = cbf.next()
            osem = cbsem[(cbf.i - 1) % 3]
            if conv_split[0] == "act":
                for c in range(8):
                    if grep is not None:
                        K.op(ACT, lambda e, c=c: e.activation(out=ob[:, c, 0:n], in_=buf[:, c, 0:n], func=AF.Copy,
                                                              scale=grep[:, c, 0:1]), reads=[tb, t_g], writes=[tob],
                             same_ok=(c > 0))
                    else:
                        K.op(ACT, lambda e, c=c: e.activation(out=ob[:, c, 0:n], in_=buf[:, c, 0:n], func=AF.Copy),
                             reads=[tb], writes=[tob], same_ok=(c > 0))
            for (eng, c0_, c1_) in (conv_split[0] if conv_split[0] != "act" else ()):
                for n0 in range(0, n, 128):
                    n1 = min(n, n0 + 128)
                    if grep is not None:
                        K.op(eng, lambda e, c0_=c0_, c1_=c1_, n0=n0, n1=n1: e.tensor_tensor(
                            out=ob[:, c0_:c1_, n0:n1], in0=buf[:, c0_:c1_, n0:n1], in1=grep[:, c0_:c1_, 0:n1 - n0],
                            op=ALU.mult), reads=[tb, t_g], writes=[tob])
                    else:
                        K.op(eng, lambda e, c0_=c0_, c1_=c1_, n0=n0, n1=n1: e.tensor_copy(
                            out=ob[:, c0_:c1_, n0:n1], in_=buf[:, c0_:c1_, n0:n1]), reads=[tb], writes=[tob])
            conv_cast.append((kind, n, dc, ob, tob, osem, dst))

        conv_cast = []

        def conv_store():
            if not conv_cast:
                return
            kind, n, dc, ob, tob, osem, dst = conv_cast.pop(0)
            tw = T()
            t_wbf[(kind, dc, n)] = tw
            K.dma(SP, lambda e: e.dma_start(out=dst, in_=ob[:, :, 0:n]), osem, reads=[tob], writes=[tw])

        def conv_some(k):
            lim = min(len(conv_list), conv_limit[0])
            for _ in range(k):
                if len(conv_cast) > 1 or (conv_pos[0] >= lim and not conv_loaded):
                    conv_store()
                conv_load()
                if len(conv_loaded) > 2 or conv_pos[0] >= lim:
                    conv_finish()

        def conv_flush():
            lim = min(len(conv_list), conv_limit[0])
            while conv_pos[0] < lim or conv_loaded or conv_cast:
                conv_some(1)

        lwsem = [K.dsem() for _ in range(8)]
        lw_i = [0]

        def load_wbf(dst_ap, t_dst, kind, col0, n):
            sem = lwsem[lw_i[0] % 8]
            lw_i[0] += 1
            if kind == "in":
                src = wbf_d[:, :, col0:col0 + n]
                deps = [t for (kd_, dc, nn), t in t_wbf.items() if kd_ == "in" and dc < col0 + n and dc + nn > col0]
            elif kind == "kv":
                src = wkvbf_d[:, :, col0:col0 + n]
                deps = [t for (kd_, dc, nn), t in t_wbf.items() if kd_ == "kv"]
            else:
                src = wobf_d[:, :, col0:col0 + n]
                deps = [t for (kd_, dc, nn), t in t_wbf.items() if kd_ == "out"]
            K.dma(SP, lambda e: e.dma_start(out=dst_ap, in_=src), sem, reads=deps, writes=[t_dst])

        def norm_tiles(st, src_d, ntiles, dstT, t_dst, tag, hook=None):
            xs = Ring([sbuf(st, f"xs_{tag}{i}", [128, 1024], F32) for i in range(3)])
            xsem = [K.dsem() for _ in range(3)]
            xn = Ring([sbuf(st, f"xn_{tag}{i}", [128, 1024], BF16) for i in range(3)])
            sm = Ring([sbuf(st, f"sm_{tag}{i}", [128, 4], F32) for i in range(3)])
            s1out = {}

            def S1(i):
                xb, tx = xs.next()
                sem = xsem[(xs.i - 1) % 3]
                K.dma(SP, lambda e: e.dma_start(out=xb[:], in_=src_d[i * 128:(i + 1) * 128, :]), sem, writes=[tx])
                smb, tsm = sm.next()
                K.op(ACT, lambda e: e.activation(out=junk[:], in_=xb[:], func=AF.Square, accum_out=smb[:, 0:1]),
                     reads=[tx], writes=[t_junk, tsm])
                K.op(ACT, lambda e: e.activation(out=smb[:, 1:2], in_=smb[:, 0:1], func=AF.Ln, scale=1.0 / 1024,
                                                 bias=EPS_AP[:, 0:1]), reads=[tsm, t_eps], writes=[tsm])
                K.op(ACT, lambda e: e.activation(out=smb[:, 2:3], in_=smb[:, 1:2], func=AF.Exp, scale=-0.5),
                     reads=[tsm], writes=[tsm])
                xnb, txn = xn.next()
                K.op(DVE, lambda e: e.tensor_scalar(out=xnb[:], in0=xb[:], scalar1=smb[:, 2:3], scalar2=None,
                                                    op0=ALU.mult), reads=[tx, tsm], writes=[txn])
                s1out[i] = (xnb, txn)

            def S2(i):
                xnb, txn = s1out.pop(i)
                pbk, tpbk = pt.next_bank()
                for c in range(8):
                    K.op(PE, lambda e, c=c: e.transpose(out=pbk[:, c, :], in_=xnb[:, c * 128:(c + 1) * 128],
                                                        identity=ident[:]), reads=[txn, t_ident], writes=[tpbk])
                K.op(ACT, lambda e: e.activation(out=dstT[:, :, i * 128:(i + 1) * 128], in_=pbk[:], func=AF.Copy),
                     reads=[tpbk], writes=[t_dst[i]])

            S1(0)
            for i in range(ntiles):
                if i + 1 < ntiles:
                    S1(i + 1)
                S2(i)
                if hook is not None:
                    hook()

        EPS_AP = sbuf(top, "eps_sb", [128, 2], F32)
        t_eps = T()
        K.op(POOL, lambda e: e.memset(EPS_AP[:, 0:1], EPS), writes=[t_eps])
        K.op(POOL, lambda e: e.memset(EPS_AP[:, 1:2], 1.0), writes=[t_eps])

        def mm_acc(out_ap, t_out, pairs, reads):
            n = len(pairs)
            for k, (l, r) in enumerate(pairs):
                K.op(PE, lambda e, l=l, r=r, k=k: e.matmul(out_ap, lhsT=l, rhs=r, start=(k == 0), stop=(k == n - 1)),
                     reads=reads, writes=[t_out])

        ev_rr = [0]

        def evac(out_ap, in_ap, reads, writes, eng=None):
            if eng is None:
                eng = ACT if ev_rr[0] % 2 == 0 else DVE
                ev_rr[0] += 1
            if eng == ACT:
                K.op(ACT, lambda e: e.activation(out=out_ap, in_=in_ap, func=AF.Copy), reads=reads, writes=writes)
            else:
                K.op(eng, lambda e: e.tensor_copy(out=out_ap, in_=in_ap), reads=reads, writes=writes)

        with ExitStack() as s1:
            hT_gh = sbuf(s1, "hT_gh", [128, 8, 2 * GH], BF16)
            t_hgh = [T() for _ in range(8)]
            wst_box[0] = Ring([sbuf(s1, f"wst{i}", [128, 8, 256], F32) for i in range(3)])
            cbf_box[0] = Ring([sbuf(s1, f"cbf{i}", [128, 8, 256], BF16) for i in range(3)])
            mem_gr = sbuf(s1, "mem_gr", [128, 8, 128], F32)
            t_mem_gr = T()
            K.dma(SP, lambda e: e.dma_start(out=mem_gr[:], in_=mem_g_d[:, :, :]), new_csem(), writes=[t_mem_gr])
            pre_g = sbuf(s1, "pre_g_sb", [128, 8, 128], F32)
            K.dma(SP, lambda e: e.dma_start(out=pre_g[:], in_=pre_g_d[:, :, :]), new_csem(), writes=[t_pre_g])
            conv_g["pre"] = (pre_g, t_pre_g)
            conv_g["mem"] = (mem_gr, t_mem_gr)
            if dbg.get("skip_p1"):
                conv_limit[0] = 10 ** 6
            with ExitStack() as s0:
                norm_tiles(s0, x_main, NTILE, hT, t_hT, "m", hook=lambda: conv_some(1))
                norm_tiles(s0, x_gh, 8, hT_gh, t_hgh, "g", hook=lambda: conv_some(1))
                conv_flush()
            K.barrier()
            conv_limit[0] = 10 ** 6
            conv_split[0] = ((DVE, 0, 8),)
            dbg_dump("hT", hT[:], t_hT)

            t_cs_d = [[T() for _ in range(NBLK)] for _ in range(8)]
            with ExitStack() as s2:
              if not dbg.get('skip_p1'):
                  wlr = sbuf(s2, "wlr", [128, 8, 32], BF16)
                  t_wlr = T()
                  load_wbf(wlr[:, :, 0:32], t_wlr, "in", OFF_LR, 32)
                  gw = sbuf(s2, "gw_sb", [16, 2, 512], BF16)
                  t_gw = T()
                  for d, src in enumerate((gw_f_d, gw_b_d)):
                      K.dma(SP, lambda e, src=src: e.dma_start(out=cf32[0:16, 0:512], in_=src[:, :]), new_csem(), writes=[t_cf32])
                      K.op(DVE, lambda e, d=d: e.tensor_copy(out=gw[:, d, :], in_=cf32[0:16, 0:512]),
                           reads=[t_cf32], writes=[t_gw])
                  negb = sbuf(s2, "negb", [128, 8], F32)
                  t_negb = T()
                  K.dma(SP, lambda e: e.dma_start(out=cf32[:, 0:8], in_=gb_d[:, :]), new_csem(), writes=[t_cf32])
                  K.op(DVE, lambda e: e.tensor_scalar(out=negb[:], in0=cf32[:, 0:8], scalar1=-1.0, scalar2=None,
                                                     op0=ALU.mult), reads=[t_cf32], writes=[t_negb])
                  rmask = sbuf(s2, "rmask_sb", [128, 512], F32)
                  t_rmask = T()
                  K.dma(SP, lambda e: e.dma_start(out=rmask[:], in_=rmask_d[:, :]), new_csem(), writes=[t_rmask])

                  lrT = [sbuf(s2, f"lrT{d}", [16, NTOK + GH], BF16) for d in range(2)]
                  t_lrT = [[T() for _ in range(9)] for _ in range(2)]

                  def tok_src(d, blk):
                      if d == 0:
                          if blk == 0:
                              return hT_gh[:, :, 0:GH], t_hgh[0:4]
                          return hT[:, :, (blk - 1) * 512:blk * 512], t_hT[(blk - 1) * 4:blk * 4]
                      if blk == 8:
                          return hT_gh[:, :, GH:2 * GH], t_hgh[4:8]
                      return hT[:, :, blk * 512:(blk + 1) * 512], t_hT[blk * 4:(blk + 1) * 4]

                  for d in range(2):
                      for blk in range(9):
                          src, ts = tok_src(d, blk)
                          pbuf, tp = pb.next()
                          mm_acc(pbuf[0:16, :], tp, [(wlr[:, c, d * 16:(d + 1) * 16], src[:, c, :]) for c in range(8)],
                                 reads=[t_wlr] + ts)
                          evac(lrT[d][:, blk * 512:(blk + 1) * 512], pbuf[0:16, :], [tp], [t_lrT[d][blk]])
                          conv_some(1)

                  wkv = sbuf(s2, "wkv", [128, 8, 384], BF16)
                  t_wkv = T()
                  kTh = sbuf(s2, "kTh", [128, 2 * GH], BF16)
                  t_kTh = T()
                  vh = sbuf(s2, "vh", [128, 8, 256], BF16)
                  t_vh = T()
                  sp_r = Ring([sbuf(s2, f"sp{i}", [128, 512], F32) for i in range(4)])
                  cs_r = Ring([sbuf(s2, f"cs{i}", [128, 512], F32) for i in range(3)])
                  cssem = [K.dsem() for _ in range(3)]
                  pfx_r = Ring([sbuf(s2, f"pfx{i}", [128, 512], F32) for i in range(2)])
                  cs_halo = [sbuf(s2, f"cs_halo{d}", [128, 512], F32) for d in range(2)]
                  t_cs_halo = [T(), T()]
                  cs_q = {id(t): [T() for _ in range(4)] for t in list(cs_r.ts) + t_cs_halo}
                  edec = Ring([sbuf(s2, f"edec{i}", [128, 128], F32) for i in range(4)])
                  kdT = Ring([sbuf(s2, f"kdT{i}", [128, 128], BF16) for i in range(4)])
                  kd = Ring([sbuf(s2, f"kd{i}", [128, 128], BF16) for i in range(4)])
                  sm1 = Ring([sbuf(s2, f"sm1_{i}", [128, 8], F32) for i in range(2)])

                  for h in range(4):
                      load_wbf(wkv[:, :, 0:384], t_wkv, "in", OFF_G(h) + 128, 384)
                      for half in range(2):
                          pbuf, tp = pb.next()
                          mm_acc(pbuf[:, :], tp, [(wkv[:, c, 0:128], hT_gh[:, c, half * GH:(half + 1) * GH])
                                                  for c in range(8)], reads=[t_wkv] + t_hgh[half * 4:half * 4 + 4])
                          evac(kTh[:, half * GH:(half + 1) * GH], pbuf[:, :], [tp], [t_kTh])
                      for ti in range(8):
                          phb, tph = ph.next()
                          mm_acc(phb[:, :], tph, [(hT_gh[:, c, ti * 128:(ti + 1) * 128], wkv[:, c, 128:384])
                                                  for c in range(8)], reads=[t_wkv, t_hgh[ti]])
                          evac(vh[:, ti, :], phb[:, :], [tph], [t_vh])
                      pend = [[], []]

                      def emit_block(d, blk, h=h):
                          u = d * 4 + h
                          is_halo = (blk == 0) if d == 0 else (blk == 8)
                          pbuf, tp = pb.next()
                          K.op(PE, lambda e: e.matmul(pbuf[:, :], lhsT=gw[:, d, h * 128:(h + 1) * 128],
                                                      rhs=lrT[d][:, blk * 512:(blk + 1) * 512], start=True, stop=True),
                               reads=[t_gw, t_lrT[d][blk]], writes=[tp])
                          spb, tsp = sp_r.next()
                          K.op(ACT, lambda e: e.activation(out=spb[:], in_=pbuf[:, :], func=AF.Exp, scale=-1.0,
                                                           bias=negb[:, u:u + 1]), reads=[tp, t_negb], writes=[tsp])
                          K.op(ACT, lambda e: e.activation(out=spb[:], in_=spb[:], func=AF.Ln, scale=1.0,
                                                           bias=EPS_AP[:, 1:2]), reads=[tsp, t_eps], writes=[tsp])
                          if is_halo:
                              csb, tcs = cs_halo[d], t_cs_halo[d]
                              csx = None
                          else:
                              csb, tcs = cs_r.next()
                              csx = cssem[(cs_r.i - 1) % 3]
                          tcq = cs_q[id(tcs)]
                          if d == 0:
                              K.op(DVE, lambda e: e.tensor_tensor_scan(out=csb[:], data0=rmask[:], data1=spb[:], initial=0.0,
                                                                       op0=ALU.mult, op1=ALU.add),
                                   reads=[tsp, t_rmask], writes=tcq)
                          else:
                              pfb, tpf = pfx_r.next()
                              K.op(DVE, lambda e: e.tensor_tensor_scan(out=pfb[:], data0=rmask[:], data1=spb[:], initial=0.0,
                                                                       op0=ALU.mult, op1=ALU.add),
                                   reads=[tsp, t_rmask], writes=[tpf])
                              for c4 in range(4):
                                  K.op(DVE, lambda e, c4=c4: e.scalar_tensor_tensor(
                                      out=csb[:, c4 * 128:(c4 + 1) * 128], in0=spb[:, c4 * 128:(c4 + 1) * 128],
                                      scalar=pfb[:, c4 * 128 + 127:c4 * 128 + 128], in1=pfb[:, c4 * 128:(c4 + 1) * 128],
                                      op0=ALU.add, op1=ALU.subtract), reads=[tsp, tpf], writes=[tcq[c4]])
                          if not is_halo:
                              mblk = blk - 1 if d == 0 else blk
                              K.dma(SP, lambda e: e.dma_start(out=cs_d[u, :, mblk * 512:(mblk + 1) * 512], in_=csb[:]),
                                    csx, reads=tcq, writes=[t_cs_d[u][mblk]])
                              return
                          K.op(POOL, lambda e: e.memset(S0[:, u, :], 0.0), writes=[t_S0[u]])
                          hoff = 0 if d == 0 else GH
                          corder = list(range(4)) if d == 0 else list(range(3, -1, -1))
                          lc0 = 127 if d == 0 else 0

                          def batch():
                              smb, tsm = sm1.next()
                              K.op(DVE, lambda e: e.tensor_scalar(
                                  out=smb[:, 0:4], in0=csb[:].rearrange("p (c t) -> p c t", c=4)[:, :, lc0],
                                  scalar1=-1.0 / 16, scalar2=None, op0=ALU.mult), reads=tcq, writes=[tsm])
                              K.op(ACT, lambda e: e.activation(out=smb[:, 4:8], in_=smb[:, 0:4], func=AF.Exp),
                                   reads=[tsm], writes=[tsm])
                              kds = []
                              for c4 in corder:
                                  edb, ted = edec.next()
                                  K.op(ACT, lambda e, c4=c4, edb=edb: e.activation(
                                      out=edb[:], in_=csb[:, c4 * 128:(c4 + 1) * 128], func=AF.Exp, scale=1.0 / 16,
                                      bias=smb[:, c4:c4 + 1]), reads=[tcq[c4], tsm], writes=[ted])
                                  kdTb, tkdT = kdT.next()
                                  K.op(POOL, lambda e, c4=c4, edb=edb, kdTb=kdTb: e.tensor_tensor(
                                      out=kdTb[:], in0=kTh[:, hoff + c4 * 128:hoff + (c4 + 1) * 128], in1=edb[:],
                                      op=ALU.mult), reads=[t_kTh, ted], writes=[tkdT])
                                  kds.append((c4, kdTb, tkdT))
                              us = []
                              for (c4, kdTb, tkdT) in kds:
                                  ptt, tpt = pt.next()
                                  K.op(PE, lambda e, ptt=ptt, kdTb=kdTb: e.transpose(out=ptt, in_=kdTb[:], identity=ident[:]),
                                       reads=[tkdT, t_ident], writes=[tpt])
                                  kdb, tkd = kd.next()
                                  evac(kdb[:], ptt, [tpt], [tkd], eng=ACT)
                                  phb, tph = ph.next()
                                  vti = (0 if d == 0 else 4) + c4
                                  K.op(PE, lambda e, phb=phb, kdb=kdb, vti=vti: e.matmul(
                                      phb[:, :], lhsT=kdb[:], rhs=vh[:, vti, :], start=True, stop=True),
                                      reads=[tkd, t_vh], writes=[tph])
                                  K.op(DVE, lambda e, c4=c4, phb=phb: e.scalar_tensor_tensor(
                                      out=S0[:, u, :], in0=S0[:, u, :], scalar=smb[:, 4 + c4:5 + c4], in1=phb[:, :],
                                      op0=ALU.mult, op1=ALU.add), reads=[tsm, tph], writes=[t_S0[u]])
                          pend[d].append(batch)

                      for i in range(9):
                          for d in range(2):
                              emit_block(d, i if d == 0 else 8 - i)
                          conv_some(1)
                      for d in range(2):
                          while pend[d]:
                              pend[d].pop(0)()
                  dbg_dump("S0", S0[:], t_S0)
              conv_flush()
            K.barrier()
        K.barrier()

        def mix_store(stb, tst, sem, ch0, nch, blk):
            K.dma(SP, lambda e: e.dma_start(
                out=mixT_d[ch0:ch0 + nch, :, blk * 512:(blk + 1) * 512].rearrange("c p t -> p c t"),
                in_=stb[:, 0:nch, :]), sem, reads=mix_q[id(tst)], writes=[t_mixT[ch0 + k][blk] for k in range(nch)])

        zsem = K.dsem("zfill")

        def mix_zero(ch0, nch):
            with ExitStack() as sz:
                zt = sbuf(sz, f"zt{ch0}", [128, 512], BF16)
                t_zt = T()
                K.op(POOL, lambda e: e.memset(zt[:], 0.0), writes=[t_zt])
                for ch in range(ch0, ch0 + nch):
                    for blk in range(NBLK):
                        K.dma(SP, lambda e, ch=ch, blk=blk: e.dma_start(out=mixT_d[ch, :, blk * 512:(blk + 1) * 512],
                                                                        in_=zt[:]), zsem, reads=[t_zt],
                              writes=[t_mixT[ch][blk]])
                K.barrier()

        if dbg.get("skip_nat"):
            mix_zero(8, 4)
        if dbg.get("skip_mem"):
            mix_zero(12, 4)
        if dbg.get("skip_gla"):
            mix_zero(0, 8)
        if not dbg.get("skip_nat"):
          with ExitStack() as sn:
            hTx = sbuf(sn, "hTx", [128, 8, 2 * NH], BF16)
            t_hTx = [T() for _ in range(4)]
            with ExitStack() as s0:
                norm_tiles(s0, x_nh, 4, hTx, t_hTx, "n")
            K.barrier()

            def ext_src(tok0, n):
                out = []
                t = tok0
                end = tok0 + n
                while t < end:
                    if t < NH:
                        e2 = min(end, NH)
                        out.append((hTx[:, :, t:e2], [t_hTx[k] for k in range(t // 128, (e2 - 1) // 128 + 1)], e2 - t))
                    elif t < NH + NTOK:
                        e2 = min(end, NH + NTOK)
                        a, b = t - NH, e2 - NH
                        out.append((hT[:, :, a:b], [t_hT[k] for k in range(a // 128, (b - 1) // 128 + 1)], e2 - t))
                    else:
                        e2 = end
                        a, b = t - NTOK - NH + NH, e2 - NTOK - NH + NH
                        out.append((hTx[:, :, a:b], [t_hTx[k] for k in range(a // 128, (b - 1) // 128 + 1)], e2 - t))
                    t = e2
                return out

            Egen = sbuf(sn, "Egen", [128, 8, 512], BF16)
            t_Egen = T()
            Esp = sbuf(sn, "Esp", [128, 10, 8, 64], BF16)
            t_Esp = T()
            with ExitStack() as stb:
                tabf = sbuf(stb, "tabf", [128, 8, 512], F32)
                t_tabf = T()
                K.dma(SP, lambda e: e.dma_start(out=tabf[:], in_=nat_gen_d[:, :, :]), new_csem(), writes=[t_tabf])
                K.op(ACT, lambda e: e.activation(out=Egen[:], in_=tabf[:], func=AF.Exp), reads=[t_tabf], writes=[t_Egen])
                tabs = sbuf(stb, "tabs", [128, 10, 8, 64], F32)
                t_tabs = T()
                K.dma(SP, lambda e: e.dma_start(out=tabs[:], in_=nat_sp_d[:, :, :, :]), new_csem(), writes=[t_tabs])
                K.op(ACT, lambda e: e.activation(out=Esp[:], in_=tabs[:], func=AF.Exp), reads=[t_tabs], writes=[t_Esp])
                K.barrier()

            wn = sbuf(sn, "wn", [128, 8, 512], BF16)
            t_wn = T()
            t_wng = T()
            qTn = sbuf(sn, "qTn", [128, NTOK], BF16)
            t_qTn = [T() for _ in range(NBLK)]
            kTn = sbuf(sn, "kTn", [128, NEXT], BF16)
            t_kTn = [T() for _ in range(9)]
            vTn = sbuf(sn, "vTn", [128, NEXT], BF16)
            t_vTn = [T() for _ in range(9)]
            Ve = sbuf(sn, "Ve", [128, 36, 2, 65], BF16)
            t_Ve = [T() for _ in range(36)]
            Vo = sbuf(sn, "Vo", [128, 36, 2, 65], BF16)
            t_Vo = [T() for _ in range(36)]
            K.op(POOL, lambda e: e.memset(Ve[:, :, :, 64:65], 1.0), writes=t_Ve)
            K.op(POOL, lambda e: e.memset(Vo[:, :, :, 64:65], 1.0), writes=t_Vo)
            pex = Ring([sbuf(sn, f"pex{i}", [128, 512], BF16) for i in range(3)])
            pT = Ring([sbuf(sn, f"pT{i}", [128, 2, 4, 64], BF16) for i in range(8)])
            gateN = sbuf(sn, "gateN", [128, NTOK], BF16)
            t_gN = [T() for _ in range(NBLK)]
            thn = Ring([sbuf(sn, f"thn{i}", [128, 512], F32) for i in range(2)])
            rden = Ring([sbuf(sn, f"rden{i}", [128, 2], F32) for i in range(3)])
            ogt = Ring([sbuf(sn, f"ogt{i}", [128, 128], BF16) for i in range(4)])
            og_q = {id(t): [T(), T()] for t in ogt.ts}
            pS = Ring([pb.bufs[0], pb.bufs[1], pb.bufs[2]], ts=[pb.ts[0], pb.ts[1], pb.ts[2]])
            pG = Ring([_phb[2]], ts=[ph.ts[2]])
            pV = Ring([_phb[0][:, 0:256], _phb[1][:, 0:256]], ts=[ph.ts[0], ph.ts[1]])

            load_wbf(wn[:, :, 0:384], t_wn, "in", OFF_N(0), 384)
            for j in range(4):
                load_wbf(wn[:, :, 384:512], t_wng, "in", OFF_N(j) + 384, 128)
                for blk in range(NBLK):
                    pbuf, tp = pS.next()
                    mm_acc(pbuf[:, :], tp, [(wn[:, c, 0:128], hT[:, c, blk * 512:(blk + 1) * 512]) for c in range(8)],
                           reads=[t_wn] + t_hT[blk * 4:blk * 4 + 4])
                    evac(qTn[:, blk * 512:(blk + 1) * 512], pbuf[:, :], [tp], [t_qTn[blk]])
                for (dstT, tdst, co) in ((kTn, t_kTn, 128), (vTn, t_vTn, 256)):
                    for blk in range(9):
                        pbuf, tp = pS.next()
                        pieces = ext_src(blk * 512, 512)
                        off = 0
                        for (src, ts, ln) in pieces:
                            mm_acc(pbuf[:, off:off + ln], tp, [(wn[:, c, co:co + 128], src[:, c, :]) for c in range(8)],
                                   reads=[t_wn] + ts)
                            off += ln
                        evac(dstT[:, blk * 512:(blk + 1) * 512], pbuf[:, :], [tp], [tdst[blk]])
                for (Vb, tV, n, base) in ((Ve, t_Ve, 36, 0), (Vo, t_Vo, 35, 64)):
                    for e0 in range(0, n, 4):
                        ne = min(4, n - e0)
                        pbk, tpbk = pt.next_bank()
                        for k in range(ne):
                            tok0 = base + (e0 + k) * 128
                            vb0 = tok0 // 512
                            vts = [t_vTn[vb0]] + ([t_vTn[vb0 + 1]] if (tok0 + 127) // 512 != vb0 else [])
                            K.op(PE, lambda e, pbk=pbk, k=k, tok0=tok0: e.transpose(
                                out=pbk[:, k, :], in_=vTn[:, tok0:tok0 + 128], identity=ident[:]),
                                reads=vts + [t_ident], writes=[tpbk])
                        evac(Vb[:, e0:e0 + ne, :, 0:64], pbk[:, 0:ne, :].rearrange("p e (a d) -> p e a d", a=2),
                             [tpbk], [tV[e0 + k] for k in range(ne)])

                for blk in range(NBLK):
                    pbuf, tp = pS.next()
                    mm_acc(pbuf[:, :], tp, [(wn[:, c, 384:512], hT[:, c, blk * 512:(blk + 1) * 512]) for c in range(8)],
                           reads=[t_wng] + t_hT[blk * 4:blk * 4 + 4])
                    thb, tth = thn.next()
                    K.op(ACT, lambda e, thb=thb, pbuf=pbuf: e.activation(out=thb[:], in_=pbuf[:, :], func=AF.Tanh, scale=0.5),
                         reads=[tp], writes=[tth])
                    K.op(DVE, lambda e, thb=thb, pbuf=pbuf, blk=blk: e.scalar_tensor_tensor(
                        out=gateN[:, blk * 512:(blk + 1) * 512], in0=thb[:], scalar=1.0, in1=pbuf[:, :],
                        op0=ALU.add, op1=ALU.mult), reads=[tth, tp], writes=[t_gN[blk]])
                if j + 1 < 4:
                    load_wbf(wn[:, :, 0:384], t_wn, "in", OFF_N(j + 1), 384)
                gate_blk = {}
                gate_done = set()

                nat_pending = []

                stA = {}
                stB = {}
                stage_buf = {}

                def natA(p, j=j):
                    if False:
                        for gb_ in (p // 4, p // 4 + 1):
                            if gb_ < NBLK and gb_ not in gate_done:
                                gate_done.add(gb_)
                                sts = nat_gate_steps(gb_)
                                if gb_ == 0:
                                    for st_ in sts:
                                        st_()
                                else:
                                    while nat_pending:
                                        nat_pending.pop(0)()
                                    nat_pending.extend(sts)
                    banks = [pS.next(), pS.next()]
                    for b in range(2):
                        for t in range(4):
                            k0 = (2 * p + b + 2 * t) * 64
                            kb = k0 // 512
                            kbs = [t_kTn[kb]] + ([t_kTn[kb + 1]] if (k0 + 127) // 512 != kb else [])
                            col = (b * 4 + t) * 64
                            for a in range(2):
                                pbuf, tp = banks[a]
                                pa = slice(a * 64, (a + 1) * 64)
                                K.op(PE, lambda e, pbuf=pbuf, pa=pa, k0=k0, col=col, b=b: e.matmul(
                                    pbuf[:, col:col + 64], lhsT=kTn[pa, k0:k0 + 128],
                                    rhs=qTn[pa, p * 128 + b * 64:p * 128 + b * 64 + 64], start=True, stop=True),
                                    reads=kbs + [t_qTn[p // 4]], writes=[tp])
                    outs = []
                    for a in range(2):
                        hd = 2 * j + a
                        pbuf, tp = banks[a]
                        pexb, tpex = pex.next()
                        K.op(ACT, lambda e, pexb=pexb, pbuf=pbuf: e.activation(out=pexb[:], in_=pbuf[:, :], func=AF.Exp,
                                                                               scale=0.125), reads=[tp], writes=[tpex])
                        pTb, tpT = pT.next()
                        spec = {}
                        for b in range(2):
                            for t in range(4):
                                if (2 * p + b, t) in SP_TILES:
                                    spec[(b, t)] = SP_TILES.index((2 * p + b, t))
                        if not spec:
                            pT2 = pTb[:].rearrange("p b t q -> p (b t q)")
                            if a == 0:
                                K.op(POOL, lambda e, pT2=pT2, pexb=pexb, hd=hd: e.tensor_tensor(
                                    out=pT2, in0=pexb[:], in1=Egen[:, hd, :], op=ALU.mult),
                                    reads=[tpex, t_Egen], writes=[tpT])
                            else:
                                K.op(DVE, lambda e, pT2=pT2, pexb=pexb, hd=hd: e.tensor_tensor(
                                    out=pT2, in0=pexb[:], in1=Egen[:, hd, :], op=ALU.mult),
                                    reads=[tpex, t_Egen], writes=[tpT])
                        else:
                            meng = POOL if a == 0 else DVE
                            for b in range(2):
                                for t in range(4):
                                    col = (b * 4 + t) * 64
                                    if (b, t) in spec:
                                        tab = Esp[:, spec[(b, t)], hd, :]
                                    else:
                                        tab = Egen[:, hd, col:col + 64]
                                    K.op(meng, lambda e, pTb=pTb, pexb=pexb, b=b, t=t, col=col, tab=tab: e.tensor_tensor(
                                        out=pTb[:, b, t, :], in0=pexb[:, col:col + 64], in1=tab, op=ALU.mult),
                                        reads=[tpex, t_Egen, t_Esp], writes=[tpT])
                        outs.append((pTb, tpT))
                    stA[p] = outs

                def natB(p):
                    outs = stA.pop(p)
                    pho, tpho = pV.next()
                    for b in range(2):
                        for a in range(2):
                            for t in range(4):
                                pTb, tpT = outs[a]
                                if b == 0:
                                    vt, tv = Ve[:, p + t, a, :], t_Ve[p + t]
                                else:
                                    vt, tv = Vo[:, p + t, a, :], t_Vo[p + t]
                                K.op(PE, lambda e, pho=pho, pTb=pTb, b=b, t=t, vt=vt, a=a: e.matmul(
                                    pho[b * 64:(b + 1) * 64, a * 65:(a + 1) * 65], lhsT=pTb[:, b, t, :], rhs=vt,
                                    start=(t == 0), stop=(t == 3)), reads=[tpT, tv], writes=[tpho])
                    rdb, trd = rden.next()
                    pho3 = pho[:, 0:130].rearrange("p (a d) -> p a d", a=2)
                    K.op(DVE, lambda e: e.reciprocal(out=rdb[:], in_=pho3[:, :, 64]), reads=[tpho], writes=[trd])
                    ogb, tog = ogt.next()
                    togh = og_q[id(tog)]
                    for a in range(2):
                        K.op(DVE, lambda e, a=a: e.tensor_scalar(out=ogb[:, a * 64:(a + 1) * 64], in0=pho3[:, a, 0:64],
                                                                 scalar1=rdb[:, a:a + 1], scalar2=0.5, op0=ALU.mult,
                                                                 op1=ALU.mult), reads=[tpho, trd], writes=[togh[a]])
                    stB[p] = (ogb, togh)

                def natC(p, j=j):
                    ogb, tog = stB.pop(p)
                    blk = p // 4
                    if p % 4 == 0:
                        stb, tst = mix_stage.next()
                        stage_buf[blk] = (stb, tst, mixsem[(mix_stage.i - 1) % 2])
                    stb, tst, msem = stage_buf[blk]
                    q0 = (p % 4) * 128
                    ptt, tpt = pt.next()
                    K.op(PE, lambda e: e.transpose(out=ptt, in_=ogb[:], identity=ident[:]), reads=list(tog) + [t_ident],
                         writes=[tpt])
                    K.op(DVE, lambda e: e.tensor_tensor(out=stb[:, 0, q0:q0 + 128], in0=ptt,
                                                        in1=gateN[:, p * 128:(p + 1) * 128], op=ALU.mult),
                         reads=[tpt, t_gN[blk]], writes=[mix_q[id(tst)][p % 4]])
                    if p % 4 == 3:
                        mix_store(stb, tst, msem, 8 + j, 1, blk)

                LA_, LB_ = 2, 1
                for p in range(LA_):
                    natA(p)
                for p in range(NTILE + LA_ + LB_):
                    if p < NTILE:
                        natB(p)
                    if 0 <= p - LB_ < NTILE:
                        natC(p - LB_)
                    if p + LA_ < NTILE:
                        natA(p + LA_)
                    if nat_pending:
                        nat_pending.pop(0)()
                while nat_pending:
                    nat_pending.pop(0)()
            K.barrier()
        K.barrier()

        if not dbg.get("skip_mem"):
          with ExitStack() as sm_:
            mT = sbuf(sm_, "mT", [128, 8, 256], BF16)
            t_mT = [T() for _ in range(2)]
            with ExitStack() as s0:
                norm_tiles(s0, mem_d, 2, mT, t_mT, "mm")
            K.barrier()
            mem_g = sbuf(sm_, "mem_g_sb", [128, 8, 128], F32)
            t_mem_g = T()
            K.dma(SP, lambda e: e.dma_start(out=mem_g[:], in_=mem_g_d[:, :, :]), new_csem(), writes=[t_mem_g])
            K.barrier()
            mkT = sbuf(sm_, "mkT", [128, 4, 256], BF16)
            t_mkT = T()
            mva = sbuf(sm_, "mva", [128, 2, 4, 129], BF16)
            t_mva = T()
            K.op(POOL, lambda e: e.memset(mva[:, :, :, 128:129], 1.0), writes=[t_mva])
            with ExitStack() as sk:
                wkvm = sbuf(sk, "wkvm", [128, 8, 1024], BF16)
                t_wkvm = T()
                load_wbf(wkvm[:, :, :], t_wkvm, "kv", 0, 1024)
                for h in range(4):
                    phb, tph = ph.next()
                    mm_acc(phb[:, :], tph, [(wkvm[:, c, h * 128:(h + 1) * 128], mT[:, c, :]) for c in range(8)],
                           reads=[t_wkvm] + t_mT)
                    evac(mkT[:, h, :], phb[:, :], [tph], [t_mkT])
                for mt in range(2):
                    pbuf, tp = pb.next()
                    mm_acc(pbuf[:, :], tp, [(mT[:, c, mt * 128:(mt + 1) * 128], wkvm[:, c, 512:1024]) for c in range(8)],
                           reads=[t_wkvm, t_mT[mt]])
                    evac(mva[:, mt, :, 0:128], pbuf[:, :].rearrange("p (h d) -> p h d", h=4), [tp], [t_mva])
                K.barrier()
            wm = sbuf(sm_, "wm", [128, 8, 256], BF16)
            t_wm = T()
            t_wmg = T()
            qTm = sbuf(sm_, "qTm", [128, NTOK], BF16)
            t_qTm = [T() for _ in range(NBLK)]
            pTm = Ring([sbuf(sm_, f"pTm{i}", [128, 2, 512], BF16) for i in range(3)])
            gateM = sbuf(sm_, "gateM", [128, NTOK], BF16)
            t_gM = [T() for _ in range(NBLK)]
            thm = Ring([sbuf(sm_, f"thm{i}", [128, 512], F32) for i in range(2)])
            rden = Ring([sbuf(sm_, f"rdenm{i}", [128, 2], F32) for i in range(3)])
            ogt = Ring([sbuf(sm_, f"ogtm{i}", [128, 128], BF16) for i in range(4)])
            pS = Ring([pb.bufs[0], pb.bufs[1], pb.bufs[2], _phb[2]], ts=[pb.ts[0], pb.ts[1], pb.ts[2], ph.ts[2]])
            pV = Ring([_phb[0][:, 0:256], _phb[1][:, 0:256]], ts=[ph.ts[0], ph.ts[1]])
            load_wbf(wm[:, :, 0:128], t_wm, "in", OFF_M(0), 128)
            for h in range(4):
                load_wbf(wm[:, :, 128:256], t_wmg, "in", OFF_M(h) + 128, 128)
                for blk in range(NBLK):
                    pbuf, tp = pS.next()
                    mm_acc(pbuf[:, :], tp, [(wm[:, c, 0:128], hT[:, c, blk * 512:(blk + 1) * 512]) for c in range(8)],
                           reads=[t_wm] + t_hT[blk * 4:blk * 4 + 4])
                    evac(qTm[:, blk * 512:(blk + 1) * 512], pbuf[:, :], [tp], [t_qTm[blk]])
                for blk in range(NBLK):
                    pbuf, tp = pS.next()
                    mm_acc(pbuf[:, :], tp, [(wm[:, c, 128:256], hT[:, c, blk * 512:(blk + 1) * 512]) for c in range(8)],
                           reads=[t_wmg] + t_hT[blk * 4:blk * 4 + 4])
                    thb, tth = thm.next()
                    K.op(ACT, lambda e, thb=thb, pbuf=pbuf: e.activation(out=thb[:], in_=pbuf[:, :], func=AF.Tanh, scale=0.5),
                         reads=[tp], writes=[tth])
                    K.op(DVE, lambda e, thb=thb, pbuf=pbuf, blk=blk: e.scalar_tensor_tensor(
                        out=gateM[:, blk * 512:(blk + 1) * 512], in0=thb[:], scalar=1.0, in1=pbuf[:, :],
                        op0=ALU.add, op1=ALU.mult), reads=[tth, tp], writes=[t_gM[blk]])
                if h + 1 < 4:
                    load_wbf(wm[:, :, 0:128], t_wm, "in", OFF_M(h + 1), 128)
                blkA = {}
                stB = {}
                stage_buf = {}

                def memA_steps(blk, h=h):
                    pTb, tpT = pTm.next()
                    blkA[blk] = (pTb, tpT, None, None)
                    steps = []
                    for mt in range(2):
                        def st(mt=mt):
                            pbuf, tp = pS.next()
                            K.op(PE, lambda e: e.matmul(
                                pbuf[:, :], lhsT=mkT[:, h, mt * 128:(mt + 1) * 128], rhs=qTm[:, blk * 512:(blk + 1) * 512],
                                start=True, stop=True), reads=[t_mkT, t_qTm[blk]], writes=[tp])
                            K.op(ACT, lambda e: e.activation(out=pTb[:, mt, :], in_=pbuf[:, :], func=AF.Exp,
                                                             scale=128.0 ** -0.5), reads=[tp], writes=[tpT])
                        steps.append(st)
                    return steps

                def memB(p, h=h):
                    blk = p // 4
                    pTb, tpT, gtb, tgt = blkA[blk]
                    q0 = (p % 4) * 128
                    for _ in range(dbg.get("mem_fill", 0)):
                        jb, tj = pS.next()
                        K.op(PE, lambda e, jb=jb: e.matmul(jb[:, :], lhsT=wm[:, 0, 0:128], rhs=qTm[:, 0:512],
                                                           start=True, stop=True), reads=[t_wm], writes=[tj])
                    pho, tpho = pV.next()
                    for mt in range(2):
                        K.op(PE, lambda e, mt=mt: e.matmul(pho[:, 0:129], lhsT=pTb[:, mt, q0:q0 + 128], rhs=mva[:, mt, h, :],
                                                           start=(mt == 0), stop=(mt == 1)), reads=[tpT, t_mva], writes=[tpho])
                    rdb, trd = rden.next()
                    K.op(DVE, lambda e: e.reciprocal(out=rdb[:, 0:1], in_=pho[:, 128:129]), reads=[tpho], writes=[trd])
                    ogb, tog = ogt.next()
                    K.op(DVE, lambda e: e.tensor_scalar(out=ogb[:], in0=pho[:, 0:128], scalar1=rdb[:, 0:1], scalar2=0.5,
                                                        op0=ALU.mult, op1=ALU.mult), reads=[tpho, trd], writes=[tog])
                    stB[p] = (ogb, tog)

                def memC(p, h=h):
                    ogb, tog = stB.pop(p)
                    blk = p // 4
                    if p % 4 == 0:
                        stb, tst = mix_stage.next()
                        stage_buf[blk] = (stb, tst, mixsem[(mix_stage.i - 1) % 2])
                    stb, tst, msem = stage_buf[blk]
                    pTb, tpT, gtb, tgt = blkA[blk]
                    q0 = (p % 4) * 128
                    ptt, tpt = pt.next()
                    K.op(PE, lambda e: e.transpose(out=ptt, in_=ogb[:], identity=ident[:]), reads=[tog, t_ident],
                         writes=[tpt])
                    K.op(DVE, lambda e: e.tensor_tensor(out=stb[:, 0, q0:q0 + 128], in0=ptt,
                                                        in1=gateM[:, p * 128:(p + 1) * 128], op=ALU.mult),
                         reads=[tpt, t_gM[blk]], writes=[mix_q[id(tst)][p % 4]])
                    if p % 4 == 3:
                        mix_store(stb, tst, msem, 12 + h, 1, blk)
                        del blkA[blk]

                for st in memA_steps(0):
                    st()
                pending = []
                for p in range(NTILE + 1):
                    if p % 4 == 0 and p // 4 + 1 < NBLK:
                        pending = memA_steps(p // 4 + 1)
                    if p < NTILE:
                        memB(p)
                    if p >= 1:
                        memC(p - 1)
                    k = 2 if p % 4 < 2 else 1
                    for _ in range(k):
                        if pending:
                            pending.pop(0)()
                    if p % 4 == 3:
                        while pending:
                            pending.pop(0)()
            K.barrier()
        K.barrier()

        if not dbg.get("skip_gla"):
          with ExitStack() as sg:
            wg = sbuf(sg, "wg", [128, 8, 768], BF16)
            t_wg = T()
            qTg = sbuf(sg, "qTg", [128, NTOK], BF16)
            kTg = sbuf(sg, "kTg", [128, NTOK], BF16)
            t_qTg = [T() for _ in range(NBLK)]
            t_kTg = [T() for _ in range(NBLK)]
            vg = sbuf(sg, "vg", [128, NTILE, 256], BF16)
            t_vg = [T() for _ in range(NTILE)]
            oacc = sbuf(sg, "oacc", [128, NTILE, 256], F32)
            t_oacc = [T() for _ in range(NTILE)]
            gnfm = sbuf(sg, "gnfm", [128, 2], F32)
            t_gnfm = T()
            K.dma(SP, lambda e: e.dma_start(out=cf32[:, 0:2], in_=gnorm_fm_d[:, :]), new_csem(), writes=[t_cf32])
            K.op(DVE, lambda e: e.tensor_scalar(out=gnfm[:], in0=cf32[:, 0:2], scalar1=0.5, scalar2=None, op0=ALU.mult),
                 reads=[t_cf32], writes=[t_gnfm])
            csl = [Ring([sbuf(sg, f"csl{d}{i}", [128, 512], F32) for i in range(2)]) for d in range(2)]
            cslsem = [[K.dsem() for _ in range(2)] for _ in range(2)]
            exr = Ring([sbuf(sg, f"exr{i}", [128, 512], F32) for i in range(2)])
            qB = [Ring([sbuf(sg, f"qB{d}{i}", [128, 512], BF16) for i in range(2)]) for d in range(2)]
            kB = [Ring([sbuf(sg, f"kB{d}{i}", [128, 512], BF16) for i in range(2)]) for d in range(2)]
            nbl = [Ring([sbuf(sg, f"nbl{d}{i}", [128, 8], F32) for i in range(2)]) for d in range(2)]
            edec = Ring([sbuf(sg, f"gedec{i}", [128, 128], F32) for i in range(3)])
            kdT = Ring([sbuf(sg, f"gkdT{i}", [128, 128], BF16) for i in range(6)])
            kd = Ring([sbuf(sg, f"gkd{i}", [128, 128], BF16) for i in range(6)])
            attm = Ring([sbuf(sg, f"attm{i}", [128, 128], BF16) for i in range(6)])
            Sf = [sbuf(sg, f"Sf{d}", [128, 256], F32) for d in range(2)]
            t_Sf = [T(), T()]
            Sb = [Ring([sbuf(sg, f"Sb{d}{i}", [128, 256], BF16) for i in range(2)]) for d in range(2)]
            ssall = sbuf(sg, "ssall", [128, 2, NTILE], F32)
            t_ssall = T()
            t_ss = [T() for _ in range(NTILE)]
            gateAll = sbuf(sg, "gateAll", [128, 2, NTOK], BF16)
            t_gate = [[T(), T()] for _ in range(NBLK)]
            onb = Ring([sbuf(sg, f"onb{i}", [128, 256], BF16) for i in range(4)])
            pU = Ring([_phb[0][:, 0:256], _phb[1][:, 0:256]], ts=[ph.ts[0], ph.ts[1]])
            pA = Ring([pb.bufs[2][:, 0:128], _phb[2][:, 0:128]], ts=[pb.ts[2], ph.ts[2]])
            pO = Ring([pb.bufs[0][:, 0:256], pb.bufs[1][:, 0:256]], ts=[pb.ts[0], pb.ts[1]])

            load_wbf(wg[:, :, :], t_wg, "in", OFF_G(0), 768)
            for h in range(4):
                for blk in range(NBLK):
                    for (dst, tds, co) in ((qTg, t_qTg, 0), (kTg, t_kTg, 128)):
                        pbuf, tp = pb.next()
                        mm_acc(pbuf[:, :], tp, [(wg[:, c, co:co + 128], hT[:, c, blk * 512:(blk + 1) * 512])
                                                for c in range(8)], reads=[t_wg] + t_hT[blk * 4:blk * 4 + 4])
                        evac(dst[:, blk * 512:(blk + 1) * 512], pbuf[:, :], [tp], [tds[blk]])
                for blk in range(NBLK):
                    for cc in range(2):
                        pbuf, tp = pb.next()
                        mm_acc(pbuf[:, :], tp, [(wg[:, c, 512 + cc * 128:512 + (cc + 1) * 128],
                                                 hT[:, c, blk * 512:(blk + 1) * 512]) for c in range(8)],
                               reads=[t_wg] + t_hT[blk * 4:blk * 4 + 4])
                        thb, tth = exr.next()
                        K.op(ACT, lambda e, thb=thb, pbuf=pbuf: e.activation(out=thb[:], in_=pbuf[:, :], func=AF.Tanh,
                                                                             scale=0.5), reads=[tp], writes=[tth])
                        K.op(DVE, lambda e, thb=thb, pbuf=pbuf: e.scalar_tensor_tensor(
                            out=thb[:], in0=thb[:], scalar=1.0, in1=pbuf[:, :], op0=ALU.add, op1=ALU.mult),
                            reads=[tth, tp], writes=[tth])
                        K.op(ACT, lambda e, thb=thb, cc=cc, blk=blk: e.activation(
                            out=gateAll[:, cc, blk * 512:(blk + 1) * 512], in_=thb[:], func=AF.Copy,
                            scale=gnfm[:, cc:cc + 1]), reads=[tth, t_gnfm], writes=[t_gate[blk][cc]])
                for p in range(0, NTILE, 2):
                    pbuf, tp = pb.next()
                    for k2 in range(2):
                        mm_acc(pbuf[:, k2 * 256:(k2 + 1) * 256], tp,
                               [(hT[:, c, (p + k2) * 128:(p + k2 + 1) * 128], wg[:, c, 256:512]) for c in range(8)],
                               reads=[t_wg, t_hT[p + k2]])
                    evac(vg[:, p:p + 2, :], pbuf[:, :].rearrange("p (a d) -> p a d", a=2), [tp], [t_vg[p], t_vg[p + 1]])
                if h + 1 < 4:
                    load_wbf(wg[:, :, :], t_wg, "in", OFF_G(h + 1), 768)
                cur_Sb = [None, None]
                for d in range(2):
                    u = d * 4 + h
                    K.op(POOL, lambda e, d=d, u=u: e.tensor_copy(out=Sf[d][:], in_=S0[:, u, :]), reads=[t_S0[u]],
                         writes=[t_Sf[d]])
                    sbb, tsb = Sb[d].next()
                    K.op(POOL, lambda e, sbb=sbb, u=u: e.tensor_copy(out=sbb[:], in_=S0[:, u, :]), reads=[t_S0[u]],
                         writes=[tsb])
                    cur_Sb[d] = (sbb, tsb)
                first_written = [False] * NTILE

                blkstate = {}

                def prep_block(d, blk, h=h):
                    u = d * 4 + h
                    csb, tcs = csl[d].next()
                    sem = cslsem[d][(csl[d].i - 1) % 2]
                    K.dma(SP, lambda e: e.dma_start(out=csb[:], in_=cs_d[u, :, blk * 512:(blk + 1) * 512]), sem,
                          reads=[t_cs_d[u][blk]], writes=[tcs])
                    qb, tqb = qB[d].next()
                    kb, tkb = kB[d].next()
                    nb, tnb = nbl[d].next()
                    ex0, tex0 = exr.next()
                    K.op(ACT, lambda e: e.activation(out=ex0[:], in_=csb[:], func=AF.Exp, scale=-1.0 / 16),
                         reads=[tcs], writes=[tex0])
                    K.op(DVE, lambda e: e.scalar_tensor_tensor(
                        out=qb[:], in0=qTg[:, blk * 512:(blk + 1) * 512], scalar=128.0 ** -0.5, in1=ex0[:],
                        op0=ALU.mult, op1=ALU.mult), reads=[t_qTg[blk], tex0], writes=[tqb])
                    ex1, tex1 = exr.next()
                    K.op(ACT, lambda e: e.activation(out=ex1[:], in_=csb[:], func=AF.Exp, scale=1.0 / 16),
                         reads=[tcs], writes=[tex1])
                    K.op(POOL, lambda e: e.tensor_tensor(
                        out=kb[:], in0=kTg[:, blk * 512:(blk + 1) * 512], in1=ex1[:], op=ALU.mult),
                        reads=[t_kTg[blk], tex1], writes=[tkb])
                    lc0 = 127 if d == 0 else 0
                    K.op(DVE, lambda e: e.tensor_scalar(
                        out=nb[:, 0:4], in0=csb[:].rearrange("p (c t) -> p c t", c=4)[:, :, lc0], scalar1=-1.0 / 16,
                        scalar2=None, op0=ALU.mult), reads=[tcs], writes=[tnb])
                    K.op(ACT, lambda e: e.activation(out=nb[:, 4:8], in_=nb[:, 0:4], func=AF.Exp), reads=[tnb], writes=[tnb])
                    blkstate[(d, blk)] = (csb, tcs, qb, tqb, kb, tkb, nb, tnb)

                items = []
                for step in range(NBLK):
                    for k4 in range(4):
                        items.append((0, step, k4))
                        items.append((1, NBLK - 1 - step, 3 - k4))
                pre_out = {}

                stA_out = {}

                def stageA(i):
                    d, blk, c4 = items[i]
                    if (d, blk) not in blkstate:
                        prep_block(d, blk)
                    csb, tcs, qb, tqb, kb, tkb, nb, tnb = blkstate[(d, blk)]
                    p = blk * 4 + c4
                    cs_ = slice(c4 * 128, (c4 + 1) * 128)
                    edb, ted = edec.next()
                    K.op(ACT, lambda e: e.activation(out=edb[:], in_=csb[:, cs_], func=AF.Exp, scale=1.0 / 16,
                                                     bias=nb[:, c4:c4 + 1]), reads=[tcs, tnb], writes=[ted])
                    kdTb, tkdT = kdT.next()
                    K.op(POOL, lambda e: e.tensor_tensor(out=kdTb[:], in0=kTg[:, p * 128:(p + 1) * 128], in1=edb[:],
                                                         op=ALU.mult), reads=[t_kTg[blk], ted], writes=[tkdT])
                    stA_out[i] = (kdTb, tkdT)

                def stageB(i):
                    d, blk, c4 = items[i]
                    csb, tcs, qb, tqb, kb, tkb, nb, tnb = blkstate[(d, blk)]
                    kdTb, tkdT = stA_out.pop(i)
                    cs_ = slice(c4 * 128, (c4 + 1) * 128)
                    ptt, tpt = pt.next()
                    K.op(PE, lambda e: e.transpose(out=ptt, in_=kdTb[:], identity=ident[:]),
                         reads=[tkdT, t_ident], writes=[tpt])
                    kdb, tkd = kd.next()
                    evac(kdb[:], ptt, [tpt], [tkd], eng=ACT)
                    pha, tpha = pA.next()
                    K.op(PE, lambda e: e.matmul(pha, lhsT=kb[:, cs_], rhs=qb[:, cs_], start=True, stop=True),
                         reads=[tkb, tqb], writes=[tpha])
                    atb, tat = attm.next()
                    K.op(DVE, lambda e: e.tensor_tensor(out=atb[:], in0=pha, in1=cmask[:, d * 128:(d + 1) * 128],
                                                        op=ALU.mult), reads=[tpha, t_cmask], writes=[tat])
                    pre_out[i] = (atb, tat, kdb, tkd)

                def post(i):
                    d, blk, c4 = items[i]
                    csb, tcs, qb, tqb, kb, tkb, nb, tnb = blkstate[(d, blk)]
                    atb, tat, kdb, tkd = pre_out.pop(i)
                    p = blk * 4 + c4
                    cs_ = slice(c4 * 128, (c4 + 1) * 128)
                    sbb, tsb = cur_Sb[d]
                    phu, tphu = pU.next()
                    K.op(PE, lambda e: e.matmul(phu, lhsT=kdb[:], rhs=vg[:, p, :], start=True, stop=True),
                         reads=[tkd, t_vg[p]], writes=[tphu])
                    pho, tpho = pO.next()
                    K.op(PE, lambda e: e.matmul(pho, lhsT=atb[:], rhs=vg[:, p, :], start=True, stop=False),
                         reads=[tat, t_vg[p]], writes=[tpho])
                    K.op(PE, lambda e: e.matmul(pho, lhsT=qb[:, cs_], rhs=sbb[:], start=False, stop=True),
                         reads=[tqb, tsb], writes=[tpho])
                    K.op(DVE, lambda e: e.scalar_tensor_tensor(
                        out=Sf[d][:], in0=Sf[d][:], scalar=nb[:, 4 + c4:5 + c4], in1=phu, op0=ALU.mult, op1=ALU.add),
                        reads=[tnb, tphu], writes=[t_Sf[d]])
                    nsb, tnsb = Sb[d].next()
                    K.op(DVE, lambda e: e.tensor_copy(out=nsb[:], in_=Sf[d][:]), reads=[t_Sf[d]], writes=[tnsb])
                    cur_Sb[d] = (nsb, tnsb)
                    if not first_written[p]:
                        K.op(ACT, lambda e: e.activation(out=oacc[:, p, :], in_=pho, func=AF.Copy),
                             reads=[tpho], writes=[t_oacc[p]])
                        first_written[p] = True
                    else:
                        K.op(DVE, lambda e: e.tensor_tensor(out=oacc[:, p, :], in0=pho, in1=oacc[:, p, :], op=ALU.add),
                             reads=[tpho, t_oacc[p]], writes=[t_oacc[p]])
                        K.op(ACT, lambda e: e.activation(out=junk[:, 0:256], in_=oacc[:, p, :], func=AF.Square,
                                                         accum_out=ssall[:, 0, p:p + 1]),
                             reads=[t_oacc[p]], writes=[t_junk, t_ss[p]])

                n_it = len(items)
                LA, LB = 5, 2
                for i in range(min(LA, n_it)):
                    stageA(i)
                for i in range(min(LB, n_it)):
                    stageB(i)
                for i in range(n_it):
                    if i + LA < n_it:
                        stageA(i + LA)
                    if i + LB < n_it:
                        stageB(i + LB)
                    post(i)

                K.op(ACT, lambda e: e.activation(out=ssall[:, 1, :], in_=ssall[:, 0, :], func=AF.Ln, scale=1.0 / 256,
                                                 bias=EPS_AP[:, 0:1]), reads=t_ss + [t_eps], writes=[t_ssall])
                K.op(ACT, lambda e: e.activation(out=ssall[:, 1, :], in_=ssall[:, 1, :], func=AF.Exp, scale=-0.5),
                     reads=[t_ssall], writes=[t_ssall])

                fA = {}
                fB = {}
                fstage = {}

                def finA(p):
                    onbb, tonb = onb.next()
                    K.op(ACT, lambda e: e.activation(out=onbb[:], in_=oacc[:, p, :], func=AF.Copy,
                                                     scale=ssall[:, 1, p:p + 1]), reads=[t_oacc[p], t_ssall], writes=[tonb])
                    fA[p] = (onbb, tonb)

                def finB(p):
                    onbb, tonb = fA.pop(p)
                    pbk, tpbk = pt.next_bank()
                    for cc in range(2):
                        K.op(PE, lambda e, cc=cc: e.transpose(out=pbk[:, cc, :], in_=onbb[:, cc * 128:(cc + 1) * 128],
                                                              identity=ident[:]), reads=[tonb, t_ident], writes=[tpbk])
                    fB[p] = (pbk, tpbk)

                def finC(p, h=h):
                    pbk, tpbk = fB.pop(p)
                    blk = p // 4
                    if p % 4 == 0:
                        stb, tst = mix_stage.next()
                        fstage[blk] = (stb, tst, mixsem[(mix_stage.i - 1) % 2])
                    stb, tst, msem = fstage[blk]
                    q0 = (p % 4) * 128
                    K.op(DVE, lambda e: e.tensor_tensor(out=stb[:, :, q0:q0 + 128], in0=pbk[:, 0:2, :],
                                                        in1=gateAll[:, :, p * 128:(p + 1) * 128], op=ALU.mult),
                         reads=[tpbk] + t_gate[blk], writes=[mix_q[id(tst)][p % 4]])
                    if p % 4 == 3:
                        mix_store(stb, tst, msem, 2 * h, 2, blk)

                finA(0)
                finA(1)
                finB(0)
                for p in range(NTILE):
                    if p + 2 < NTILE:
                        finA(p + 2)
                    if p + 1 < NTILE:
                        finB(p + 1)
                    finC(p)
            K.barrier()
        K.barrier()

        with ExitStack() as so:
          if dbg.get("skip_out"):
            K.final_wait(SP)
          else:
              wo = sbuf(so, "wo", [128, 16, 1024], BF16)
              t_wo = T()
              for kh in range(2):
                  load_wbf(wo[:, kh * 8:(kh + 1) * 8, :], t_wo, "out", 0, 1024) if False else None
              load_wbf(wo[:, :, :], t_wo, "out", 0, 1024)
              gB = sbuf(so, "gB", [128, 1024], F32)
              t_gB = T()
              K.dma(SP, lambda e: e.dma_start(out=gB[:], in_=post_g_d[:, :]), new_csem(), writes=[t_gB])
              mx = Ring([sbuf(so, f"mx{i}", [128, 16, 512], BF16) for i in range(2)])
              mxsem = [K.dsem() for _ in range(2)]
              xr = Ring([sbuf(so, f"xr{i}", [128, 1024], F32) for i in range(3)])
              xrsem = [K.dsem() for _ in range(3)]
              yt = Ring([sbuf(so, f"yt{i}", [128, 1024], F32) for i in range(3)])
              ysem = [K.dsem() for _ in range(3)]
              smo = Ring([sbuf(so, f"smo{i}", [128, 4], F32) for i in range(3)])
              pO6 = Ring([pb.bufs[0], pb.bufs[1], pb.bufs[2], _phb[0], _phb[1], _phb[2]],
                         ts=[pb.ts[0], pb.ts[1], pb.ts[2], ph.ts[0], ph.ts[1], ph.ts[2]])
              mxblk = {}
              oA = {}

              def load_mx(blk):
                  mxb, tmx = mx.next()
                  sem = mxsem[(mx.i - 1) % 2]
                  for kh in range(2):
                      K.dma(SP, lambda e, kh=kh: e.dma_start(
                          out=mxb[:, kh * 8:(kh + 1) * 8, :],
                          in_=mixT_d[kh * 8:(kh + 1) * 8, :, blk * 512:(blk + 1) * 512].rearrange("c p t -> p c t")), sem,
                          reads=[t_mixT[c][blk] for c in range(kh * 8, kh * 8 + 8)], writes=[tmx])
                  mxblk[blk] = (mxb, tmx)

              def outA(p):
                  blk = p // 4
                  if blk not in mxblk:
                      load_mx(blk)
                  if p % 4 == 0 and blk + 1 < NBLK and (blk + 1) not in mxblk:
                      load_mx(blk + 1)
                  mxb, tmx = mxblk[blk]
                  q0 = (p % 4) * 128
                  xb, tx = xr.next()
                  xsm = xrsem[(xr.i - 1) % 3]
                  K.dma(SP, lambda e: e.dma_start(out=xb[:], in_=x_main[p * 128:(p + 1) * 128, :]), xsm, writes=[tx])
                  smb, tsm = smo.next()
                  pbs = []
                  for hf in range(2):
                      pbuf, tp = pO6.next()
                      mm_acc(pbuf[:, :], tp, [(mxb[:, kc, q0:q0 + 128], wo[:, kc, hf * 512:(hf + 1) * 512])
                                              for kc in range(16)], reads=[tmx, t_wo])
                      K.op(ACT, lambda e, pbuf=pbuf, hf=hf: e.activation(
                          out=junk[:, 0:512], in_=pbuf[:, :], func=AF.Square, accum_out=smb[:, hf:hf + 1]),
                          reads=[tp], writes=[t_junk, tsm])
                      pbs.append((pbuf, tp))
                  oA[p] = (xb, tx, smb, tsm, pbs)

              def outB(p):
                  xb, tx, smb, tsm, pbs = oA.pop(p)
                  K.op(DVE, lambda e: e.tensor_tensor(out=smb[:, 2:3], in0=smb[:, 0:1], in1=smb[:, 1:2], op=ALU.add),
                       reads=[tsm], writes=[tsm])
                  K.op(ACT, lambda e: e.activation(out=smb[:, 3:4], in_=smb[:, 2:3], func=AF.Ln, scale=1.0 / 1024,
                                                   bias=EPS_AP[:, 0:1]), reads=[tsm, t_eps], writes=[tsm])
                  K.op(ACT, lambda e: e.activation(out=smb[:, 3:4], in_=smb[:, 3:4], func=AF.Exp, scale=-0.5),
                       reads=[tsm], writes=[tsm])
                  yb, ty = yt.next()
                  ysm = ysem[(yt.i - 1) % 3]
                  for hf, (pbuf, tp) in enumerate(pbs):
                      K.op(DVE, lambda e, pbuf=pbuf, hf=hf: e.scalar_tensor_tensor(
                          out=yb[:, hf * 512:(hf + 1) * 512], in0=pbuf[:, :], scalar=smb[:, 3:4],
                          in1=gB[:, hf * 512:(hf + 1) * 512], op0=ALU.mult, op1=ALU.mult),
                          reads=[tp, tsm, t_gB], writes=[ty])
                  K.op(POOL, lambda e: e.tensor_tensor(out=yb[:], in0=yb[:], in1=xb[:], op=ALU.add),
                       reads=[tx, ty], writes=[ty])
                  K.dma(SP, lambda e: e.dma_start(out=y_out[p * 128:(p + 1) * 128, :], in_=yb[:]), ysm, reads=[ty])

              outA(0)
              for p in range(NTILE):
                  if p + 1 < NTILE:
                      outA(p + 1)
                  outB(p)
              K.final_wait(SP)
        with nc.Block() as block:
            K.replay(block)
    return nc


def _nat_tables(rpb, first, last):
    kc = np.arange(64)[:, None]
    qc = np.arange(64)[None, :]
    cs = np.clip(qc - 8, 0, 48)
    col_in = (kc >= cs) & (kc < cs + 16)
    dcol = np.clip(kc - qc + 15, 0, 30)

    def blockfor(drow):
        if drow < 0 or drow > 14:
            return np.full((8, 64, 64), MASKV, np.float32)
        b = rpb[:, drow][:, dcol]
        return np.where(col_in[None], b, np.float32(MASKV)).astype(np.float32)

    gen = np.empty((128, 8, 2, 4, 64), np.float32)
    for t in range(4):
        for a in range(2):
            blk = blockfor(3 + 2 * t + a)
            for b in range(2):
                gen[a * 64:(a + 1) * 64, :, b, t, :] = blk.transpose(1, 0, 2)
    sp = np.empty((128, 10, 8, 64), np.float32)
    for i, (r, t) in enumerate(SP_TILES):
        for a in range(2):
            slot = r - 4 + 2 * t + a
            drow = 3 + 2 * t + a
            if first and slot < 0:
                drow = 2 * t + a + 11
            if last and slot >= 64:
                drow = 2 * t + a - 5 if slot <= 66 else -1
            sp[a * 64:(a + 1) * 64, i, :, :] = blockfor(drow).transpose(1, 0, 2)
    return gen.reshape(128, 8, 512), sp


def _core_inputs(c, inp, consts):
    if c < 4:
        xs, mem, seg, nseg = inp["x_prompt"][0], inp["mem_prompt"][0], c, 4
    else:
        b = (c - 4) // 2
        xs, mem, seg, nseg = inp["x_sample"][b], inp["mem_sample"][b], (c - 4) % 2, 2
    t0 = seg * NTOK
    first, last = seg == 0, seg == nseg - 1
    x_main = xs[t0:t0 + NTOK]
    x_gh = np.zeros((2 * GH, 1024), np.float32)
    if not first:
        x_gh[:GH] = xs[t0 - GH:t0]
    if not last:
        x_gh[GH:] = xs[t0 + NTOK:t0 + NTOK + GH]
    x_nh = np.zeros((2 * NH, 1024), np.float32)
    if first:
        x_nh[:NH] = x_main[4 * 64:8 * 64]
    else:
        x_nh[:NH] = xs[t0 - NH:t0]
    if last:
        x_nh[NH:NH + 192] = x_main[56 * 64:59 * 64]
    else:
        x_nh[NH:] = xs[t0 + NTOK:t0 + NTOK + NH]
    gen, sp = _nat_tables(inp["nat_rpb"][0], first, last)
    d = dict(consts)
    d.update(x_main=np.ascontiguousarray(x_main), x_gh=x_gh, x_nh=x_nh, mem=np.ascontiguousarray(mem),
             nat_gen=gen, nat_sp=sp)
    return d


def _consts(inp):
    f = np.float32
    j = np.arange(128)[:, None]
    i = np.arange(128)[None, :]
    cmask = np.concatenate([(j <= i).astype(f), (j >= i).astype(f)], axis=1)
    rmask = np.ones((128, 512), f)
    rmask[:, ::128] = 0.0
    gb = np.concatenate([inp["gla_b_fwd"][0].reshape(4, 128).T, inp["gla_b_bwd"][0].reshape(4, 128).T], axis=1)
    return dict(
        w_in=np.ascontiguousarray(inp["w_in"][0]), w_out=np.ascontiguousarray(inp["w_out"][0]),
        w_mkv=np.ascontiguousarray(inp["w_mem_kv"][0]),
        pre_g=np.ascontiguousarray(np.broadcast_to(inp["pre_norm_g"][0].reshape(8, 128).T[:, :, None], (128, 8, 128))),
        mem_g=np.ascontiguousarray(np.broadcast_to(inp["mem_norm_g"][0].reshape(8, 128).T[:, :, None], (128, 8, 128))),
        post_g=np.ascontiguousarray(np.broadcast_to(inp["post_norm_g"][0][None, :], (128, 1024))),
        gnorm=np.ascontiguousarray(np.broadcast_to(inp["gla_norm_g"][0][None, :], (128, 256))),
        gnorm_fm=np.ascontiguousarray(inp["gla_norm_g"][0].reshape(2, 128).T),
        gw_f=np.ascontiguousarray(inp["gla_w_fwd"][0]), gw_b=np.ascontiguousarray(inp["gla_w_bwd"][0]),
        gb=np.ascontiguousarray(gb.astype(f)),
        ident=np.eye(128, dtype=f), cmask=cmask, rmask=rmask)


def kernel(**inputs):
    inp = {k: np.asarray(v) for k, v in inputs.items()}
    consts = _consts(inp)
    in_maps = [_core_inputs(c, inp, consts) for c in range(8)]
    nc = build_nc()
    res = run_bass_kernel_spmd(nc, in_maps, core_ids=list(range(8)))
    ys = [np.asarray(r["y"], dtype=np.float32) for r in res.results]
    y_prompt = np.concatenate(ys[0:4], axis=0)[None]
    y_sample = np.stack([np.concatenate(ys[4:6], axis=0), np.concatenate(ys[6:8], axis=0)], axis=0)
    return (y_prompt, y_sample)
```

```python
import numpy as np
from contextlib import ExitStack
import concourse.bass as bass
import concourse.mybir as mybir
from concourse.bass_utils import run_bass_kernel_spmd

F32 = mybir.dt.float32
BF16 = mybir.dt.bfloat16
AF = mybir.ActivationFunctionType
ALU = mybir.AluOpType

PE, ACT, DVE, POOL, SP = "tensor", "scalar", "vector", "gpsimd", "sync"
ENGS = (PE, ACT, DVE, POOL, SP)

NTOK = 4096
NTILE = 32
NBLK = 8
GH = 512
NH = 256
NEXT = NTOK + 2 * NH
EPS = 1e-6
MASKV = -30000.0

C_GQ, C_GK, C_GV, C_GG, C_LRF, C_LRB = 0, 512, 1024, 2048, 3072, 3088
C_NQ, C_NK, C_NV, C_NG, C_MQ, C_MG = 3104, 3616, 4128, 4640, 5152, 5664
IN_W = 6176

SP_TILES = [(0, 0), (0, 1), (1, 0), (1, 1), (2, 0), (3, 0), (61, 3), (62, 3), (63, 2), (63, 3)]


class T:
    __slots__ = ("name", "w", "r")

    def __init__(self, name=""):
        self.name = name
        self.w = None
        self.r = {}


class Kern:
    def __init__(self, nc, stack):
        self.nc = nc
        self.stack = stack
        self.prog = {e: [] for e in ENGS}
        self.sems = {}
        self.cnt = {}
        self.seen = {e: {} for e in ENGS}
        for e in ENGS:
            self._mksem(e)
        self.n_dsem = 0

    def _mksem(self, key):
        h = self.stack.enter_context(self.nc.semaphore("s_" + str(key)))
        self.sems[key] = h
        self.cnt[key] = 0
        return key

    def dsem(self, name=None):
        self.n_dsem += 1
        return self._mksem(name or f"d{self.n_dsem}")

    def _collect(self, eng, reads, writes, same_ok):
        deps = {}

        def need(k, v):
            if k == eng and same_ok:
                return
            if v > deps.get(k, 0):
                deps[k] = v
        for t in reads:
            if t.w is not None:
                need(*t.w)
        for t in writes:
            if t.w is not None:
                need(*t.w)
            for k, v in t.r.items():
                need(k, v)
        seen = self.seen[eng]
        out = []
        for k, v in deps.items():
            if seen.get(k, 0) >= v:
                continue
            seen[k] = v
            out.append((k, v))
        return out

    def op(self, eng, fn, reads=(), writes=(), same_ok=None):
        if same_ok is None:
            same_ok = (eng == PE)
        waits = self._collect(eng, reads, writes, same_ok)
        self.cnt[eng] += 1
        v = self.cnt[eng]
        self.prog[eng].append((waits, fn, eng, 1))
        for t in writes:
            t.w = (eng, v)
            t.r = {}
        for t in reads:
            if t not in writes:
                t.r[eng] = v
        return v

    def dma(self, q, fn, sem, reads=(), writes=()):
        waits = self._collect(q, reads, writes, False)
        self.cnt[sem] += 16
        v = self.cnt[sem]
        self.prog[q].append((waits, fn, sem, 16))
        for t in writes:
            t.w = (sem, v)
            t.r = {}
        for t in reads:
            if t not in writes:
                t.r[sem] = v
        return v

    def barrier(self):
        snap = dict(self.cnt)
        for e in ENGS:
            waits = []
            for k, v in snap.items():
                if k == e or v == 0:
                    continue
                if self.seen[e].get(k, 0) >= v:
                    continue
                self.seen[e][k] = v
                waits.append((k, v))
            if waits:
                self.prog[e].append((waits, None, None, 0))

    def final_wait(self, eng=SP):
        waits = [(k, v) for k, v in self.cnt.items() if v > 0 and k != eng]
        self.prog[eng].append((waits, None, None, 0))

    def replay(self, block):
        sems = self.sems
        for e in ENGS:
            prog = self.prog[e]

            def body(h, prog=prog):
                for waits, fn, key, inc in prog:
                    for k, v in waits:
                        h.wait_ge(sems[k], v)
                    if fn is not None:
                        fn(h).then_inc(sems[key], inc)
            getattr(block, e)(body)


class Ring:
    def __init__(self, bufs, ts=None, banks=None):
        self.bufs = bufs
        self.ts = ts if ts is not None else [T() for _ in bufs]
        self.banks = banks
        self.i = 0

    def next_bank(self):
        j = self.i % len(self.bufs)
        self.i += 1
        return self.banks[j], self.ts[j]

    def next(self):
        j = self.i % len(self.bufs)
        self.i += 1
        return self.bufs[j], self.ts[j]


def build_nc(dbg=None):
    dbg = dbg or {}
    dbg_specs = dbg.get("outs", {})
    nc = bass.Bass("TRN2", target_bir_lowering=False)
    din = lambda n, s, d=F32: nc.dram_tensor(n, list(s), d, kind="ExternalInput").ap()
    x_main = din("x_main", [NTOK, 1024])
    x_gh = din("x_gh", [2 * GH, 1024])
    x_nh = din("x_nh", [2 * NH, 1024])
    mem_d = din("mem", [256, 1024])
    w_in = din("w_in", [1024, IN_W])
    w_out = din("w_out", [2048, 1024])
    w_mkv = din("w_mkv", [1024, 1024])
    pre_g_d = din("pre_g", [128, 8, 128])
    mem_g_d = din("mem_g", [128, 8, 128])
    post_g_d = din("post_g", [128, 1024])
    gnorm_d = din("gnorm", [128, 256])
    gnorm_fm_d = din("gnorm_fm", [128, 2])
    gw_f_d = din("gw_f", [16, 512])
    gw_b_d = din("gw_b", [16, 512])
    gb_d = din("gb", [128, 8])
    nat_gen_d = din("nat_gen", [128, 8, 512])
    nat_sp_d = din("nat_sp", [128, 10, 8, 64])
    ident_d = din("ident", [128, 128])
    cmask_d = din("cmask", [128, 256])
    rmask_d = din("rmask", [128, 512])
    y_out = nc.dram_tensor("y", [NTOK, 1024], F32, kind="ExternalOutput").ap()
    mixT_d = nc.dram_tensor("mixT_scr", [16, 128, NTOK], BF16, kind="Internal").ap()
    cs_d = nc.dram_tensor("cs_scr", [8, 128, NTOK], F32, kind="Internal").ap()
    wbf_d = nc.dram_tensor("wbf_scr", [128, 8, IN_W], BF16, kind="Internal").ap()
    wkvbf_d = nc.dram_tensor("wkvbf_scr", [128, 8, 1024], BF16, kind="Internal").ap()
    wobf_d = nc.dram_tensor("wobf_scr", [128, 16, 1024], BF16, kind="Internal").ap()
    dbg_out = {}
    for name, (shape, dt) in dbg_specs.items():
        dbg_out[name] = nc.dram_tensor("dbg_" + name, list(shape), dt, kind="ExternalOutput").ap()

    with ExitStack() as top:
        K = Kern(nc, top)

        def sbuf(st, n, s, d):
            return st.enter_context(nc.sbuf_tensor(n, list(s), d))

        def psum(st, n, s, d):
            return st.enter_context(nc.psum_tensor(n, list(s), d))

        pb = Ring([psum(top, f"pb{i}", [128, 512], F32) for i in range(3)])
        _phb = [psum(top, f"phb{i}", [128, 512], F32) for i in range(3)]
        ph = Ring([_phb[i][:, 0:256] for i in range(3)])
        _ptb = [psum(top, f"ptb{i}", [128, 8, 128], BF16) for i in range(2)]
        ptb = _ptb[0]
        pt = Ring([_ptb[i][:, 0, :] for i in range(2)], banks=_ptb)
        pt_full_T = [pt.ts[0]]

        hT = sbuf(top, "hT_sb", [128, 8, NTOK], BF16)
        t_hT = [T() for _ in range(NTILE)]
        ident = sbuf(top, "ident_sb", [128, 128], BF16)
        t_ident = T()
        cmask = sbuf(top, "cmask_sb", [128, 256], BF16)
        t_cmask = T()
        t_pre_g = T()
        S0 = sbuf(top, "S0_sb", [128, 8, 256], F32)
        t_S0 = [T() for _ in range(8)]
        cf32 = sbuf(top, "cf32", [128, 512], F32)
        t_cf32 = T()
        junk = sbuf(top, "junk_sb", [128, 1024], BF16)
        t_junk = T()
        wsem = [K.dsem() for _ in range(3)]
        wst_box = [None]
        cbf_box = [None]
        csem_n = [0]
        mix_stage = Ring([sbuf(top, f"mixst{i}", [128, 2, 512], BF16) for i in range(2)])
        mix_q = {id(t): [T() for _ in range(4)] for t in mix_stage.ts}
        mixsem = [K.dsem() for _ in range(2)]
        t_mixT = [[T() for _ in range(NBLK)] for _ in range(16)]
        dbgsem = K.dsem("dbg")

        def new_csem():
            csem_n[0] += 1
            return K.dsem(f"const{csem_n[0]}")

        def dbg_dump(name, ap, reads):
            if name in dbg_out:
                K.dma(SP, lambda e: e.dma_start(out=dbg_out[name], in_=ap), dbgsem, reads=reads)

        def load_const_bf16(dst, t_dst, src_ap, ncol):
            K.dma(SP, lambda e: e.dma_start(out=cf32[:, 0:ncol], in_=src_ap), new_csem(), writes=[t_cf32])
            K.op(DVE, lambda e: e.tensor_copy(out=dst, in_=cf32[:, 0:ncol]), reads=[t_cf32], writes=[t_dst])

        load_const_bf16(ident[:], t_ident, ident_d[:, :], 128)
        load_const_bf16(cmask[:], t_cmask, cmask_d[:, :], 256)

        OFF_G = lambda h: h * 768
        OFF_LR = 3072
        OFF_N = lambda j: 3104 + j * 512
        OFF_M = lambda h: 5152 + h * 256
        conv_list = []
        for h in range(4):
            conv_list += [("in", C_GQ + h * 128, 128, OFF_G(h)), ("in", C_GK + h * 128, 128, OFF_G(h) + 128),
                          ("in", C_GV + h * 256, 128, OFF_G(h) + 256), ("in", C_GV + h * 256 + 128, 128, OFF_G(h) + 384),
                          ("in", C_GG + h * 256, 128, OFF_G(h) + 512), ("in", C_GG + h * 256 + 128, 128, OFF_G(h) + 640)]
        conv_list.append(("in", C_LRF, 32, OFF_LR))
        for j in range(4):
            conv_list += [("in", C_NQ + j * 128, 128, OFF_N(j)), ("in", C_NK + j * 128, 128, OFF_N(j) + 128),
                          ("in", C_NV + j * 128, 128, OFF_N(j) + 256), ("in", C_NG + j * 128, 128, OFF_N(j) + 384)]
        for h in range(4):
            conv_list += [("in", C_MQ + h * 128, 128, OFF_M(h)), ("in", C_MG + h * 128, 128, OFF_M(h) + 128)]
        for c0 in range(0, 1024, 128):
            conv_list.append(("kv", c0, 128, c0))
        for kh in range(2):
            for c0 in range(0, 1024, 128):
                conv_list.append(("out", c0, 128, kh * 8 * 1024 + c0))
        def _prio(it):
            kind, sc, n, dc = it
            if kind == "in" and dc == OFF_LR:
                return 0
            if kind == "in" and dc < 3072:
                return 1 if 128 <= (dc % 768) < 512 else 6
            if kind == "in" and dc < 5152:
                return 2 if dc < OFF_N(1) else 3
            if kind == "kv":
                return 4
            if kind == "in":
                return 5
            return 7
        conv_list.sort(key=_prio)
        merged = []
        for it in conv_list:
            if merged:
                k0, s0_, n0, d0 = merged[-1]
                if k0 == it[0] and n0 == 128 and it[2] == 128 and s0_ + n0 == it[1] and d0 + n0 == it[3] and k0 != "out":
                    merged[-1] = (k0, s0_, 256, d0)
                    continue
            merged.append(it)
        conv_list = merged
        n_early = sum(1 for it in conv_list if _prio(it) <= 2)
        conv_limit = [n_early]
        conv_split = [((DVE, 0, 5), (POOL, 5, 8))]
        t_wbf = {}
        cbsem = [K.dsem() for _ in range(3)]
        conv_pos = [0]
        conv_g = {}

        conv_loaded = []

        def conv_load():
            if conv_pos[0] >= min(len(conv_list), conv_limit[0]):
                return
            kind, sc, n, dc = conv_list[conv_pos[0]]
            conv_pos[0] += 1
            wst = wst_box[0]
            buf, tb = wst.next()
            sem = wsem[(wst.i - 1) % 3]
            if kind == "in":
                src = w_in[:, sc:sc + n].rearrange("(c p) n -> p c n", p=128)
                dst = wbf_d[:, :, dc:dc + n]
                grep, t_g = conv_g["pre"]
            elif kind == "kv":
                src = w_mkv[:, sc:sc + n].rearrange("(c p) n -> p c n", p=128)
                dst = wkvbf_d[:, :, dc:dc + n]
                grep, t_g = conv_g["mem"]
            else:
                kh, c0 = dc // 8192, dc % 8192
                src = w_out[kh * 1024:(kh + 1) * 1024, c0:c0 + n].rearrange("(c p) n -> p c n", p=128)
                dst = wobf_d[:, kh * 8:(kh + 1) * 8, c0:c0 + n]
                grep, t_g = None, None
            K.dma(SP, lambda e: e.dma_start(out=buf[:, :, 0:n], in_=src), sem, writes=[tb])
            conv_loaded.append((kind, n, dc, buf, tb, dst, grep, t_g))

        def conv_finish():
            if not conv_loaded:
                return
            kind, n, dc, buf, tb, dst, grep, t_g = conv_loaded.pop(0)
            cbf = cbf_box[0]
            ob, tob = cbf.next()
            osem = cbsem[(cbf.i - 1) % 3]
            if conv_split[0] == "act":
                for c in range(8):
                    if grep is not None:
                        K.op(ACT, lambda e, c=c: e.activation(out=ob[:, c, 0:n], in_=buf[:, c, 0:n], func=AF.Copy,
                                                              scale=grep[:, c, 0:1]), reads=[tb, t_g], writes=[tob],
                             same_ok=(c > 0))
                    else:
                        K.op(ACT, lambda e, c=c: e.activation(out=ob[:, c, 0:n], in_=buf[:, c, 0:n], func=AF.Copy),
                             reads=[tb], writes=[tob], same_ok=(c > 0))
            for (eng, c0_, c1_) in (conv_split[0] if conv_split[0] != "act" else ()):
                for n0 in range(0, n, 128):
                    n1 = min(n, n0 + 128)
                    if grep is not None:
                        K.op(eng, lambda e, c0_=c0_, c1_=c1_, n0=n0, n1=n1: e.tensor_tensor(
                            out=ob[:, c0_:c1_, n0:n1], in0=buf[:, c0_:c1_, n0:n1], in1=grep[:, c0_:c1_, 0:n1 - n0],
                            op=ALU.mult), reads=[tb, t_g], writes=[tob])
                    else:
                        K.op(eng, lambda e, c0_=c0_, c1_=c1_, n0=n0, n1=n1: e.tensor_copy(
                            out=ob[:, c0_:c1_, n0:n1], in_=buf[:, c0_:c1_, n0:n1]), reads=[tb], writes=[tob])
            conv_cast.append((kind, n, dc, ob, tob, osem, dst))

        conv_cast = []

        def conv_store():
            if not conv_cast:
                return
            kind, n, dc, ob, tob, osem, dst = conv_cast.pop(0)
            tw = T()
            t_wbf[(kind, dc, n)] = tw
            K.dma(SP, lambda e: e.dma_start(out=dst, in_=ob[:, :, 0:n]), osem, reads=[tob], writes=[tw])

        def conv_some(k):
            lim = min(len(conv_list), conv_limit[0])
            for _ in range(k):
                if len(conv_cast) > 1 or (conv_pos[0] >= lim and not conv_loaded):
                    conv_store()
                conv_load()
                if len(conv_loaded) > 2 or conv_pos[0] >= lim:
                    conv_finish()

        def conv_flush():
            lim = min(len(conv_list), conv_limit[0])
            while conv_pos[0] < lim or conv_loaded or conv_cast:
                conv_some(1)

        lwsem = [K.dsem() for _ in range(8)]
        lw_i = [0]

        def load_wbf(dst_ap, t_dst, kind, col0, n):
            sem = lwsem[lw_i[0] % 8]
            lw_i[0] += 1
            if kind == "in":
                src = wbf_d[:, :, col0:col0 + n]
                deps = [t for (kd_, dc, nn), t in t_wbf.items() if kd_ == "in" and dc < col0 + n and dc + nn > col0]
            elif kind == "kv":
                src = wkvbf_d[:, :, col0:col0 + n]
                deps = [t for (kd_, dc, nn), t in t_wbf.items() if kd_ == "kv"]
            else:
                src = wobf_d[:, :, col0:col0 + n]
                deps = [t for (kd_, dc, nn), t in t_wbf.items() if kd_ == "out"]
            K.dma(SP, lambda e: e.dma_start(out=dst_ap, in_=src), sem, reads=deps, writes=[t_dst])

        def norm_tiles(st, src_d, ntiles, dstT, t_dst, tag, hook=None):
            xs = Ring([sbuf(st, f"xs_{tag}{i}", [128, 1024], F32) for i in range(3)])
            xsem = [K.dsem() for _ in range(3)]
            xn = Ring([sbuf(st, f"xn_{tag}{i}", [128, 1024], BF16) for i in range(3)])
            sm = Ring([sbuf(st, f"sm_{tag}{i}", [128, 4], F32) for i in range(3)])
            s1out = {}

            def S1(i):
                xb, tx = xs.next()
                sem = xsem[(xs.i - 1) % 3]
                K.dma(SP, lambda e: e.dma_start(out=xb[:], in_=src_d[i * 128:(i + 1) * 128, :]), sem, writes=[tx])
                smb, tsm = sm.next()
                K.op(ACT, lambda e: e.activation(out=junk[:], in_=xb[:], func=AF.Square, accum_out=smb[:, 0:1]),
                     reads=[tx], writes=[t_junk, tsm])
                K.op(ACT, lambda e: e.activation(out=smb[:, 1:2], in_=smb[:, 0:1], func=AF.Ln, scale=1.0 / 1024,
                                                 bias=EPS_AP[:, 0:1]), reads=[tsm, t_eps], writes=[tsm])
                K.op(ACT, lambda e: e.activation(out=smb[:, 2:3], in_=smb[:, 1:2], func=AF.Exp, scale=-0.5),
                     reads=[tsm], writes=[tsm])
                xnb, txn = xn.next()
                K.op(DVE, lambda e: e.tensor_scalar(out=xnb[:], in0=xb[:], scalar1=smb[:, 2:3], scalar2=None,
                                                    op0=ALU.mult), reads=[tx, tsm], writes=[txn])
                s1out[i] = (xnb, txn)

            def S2(i):
                xnb, txn = s1out.pop(i)
                pbk, tpbk = pt.next_bank()
                for c in range(8):
                    K.op(PE, lambda e, c=c: e.transpose(out=pbk[:, c, :], in_=xnb[:, c * 128:(c + 1) * 128],
                                                        identity=ident[:]), reads=[txn, t_ident], writes=[tpbk])
                K.op(ACT, lambda e: e.activation(out=dstT[:, :, i * 128:(i + 1) * 128], in_=pbk[:], func=AF.Copy),
                     reads=[tpbk], writes=[t_dst[i]])

            S1(0)
            for i in range(ntiles):
                if i + 1 < ntiles:
                    S1(i + 1)
                S2(i)
                if hook is not None:
                    hook()

        EPS_AP = sbuf(top, "eps_sb", [128, 2], F32)
        t_eps = T()
        K.op(POOL, lambda e: e.memset(EPS_AP[:, 0:1], EPS), writes=[t_eps])
        K.op(POOL, lambda e: e.memset(EPS_AP[:, 1:2], 1.0), writes=[t_eps])

        def mm_acc(out_ap, t_out, pairs, reads):
            n = len(pairs)
            for k, (l, r) in enumerate(pairs):
                K.op(PE, lambda e, l=l, r=r, k=k: e.matmul(out_ap, lhsT=l, rhs=r, start=(k == 0), stop=(k == n - 1)),
                     reads=reads, writes=[t_out])

        ev_rr = [0]

        def evac(out_ap, in_ap, reads, writes, eng=None):
            if eng is None:
                eng = ACT if ev_rr[0] % 2 == 0 else DVE
                ev_rr[0] += 1
            if eng == ACT:
                K.op(ACT, lambda e: e.activation(out=out_ap, in_=in_ap, func=AF.Copy), reads=reads, writes=writes)
            else:
                K.op(eng, lambda e: e.tensor_copy(out=out_ap, in_=in_ap), reads=reads, writes=writes)

        with ExitStack() as s1:
            hT_gh = sbuf(s1, "hT_gh", [128, 8, 2 * GH], BF16)
            t_hgh = [T() for _ in range(8)]
            wst_box[0] = Ring([sbuf(s1, f"wst{i}", [128, 8, 256], F32) for i in range(3)])
            cbf_box[0] = Ring([sbuf(s1, f"cbf{i}", [128, 8, 256], BF16) for i in range(3)])
            mem_gr = sbuf(s1, "mem_gr", [128, 8, 128], F32)
            t_mem_gr = T()
            K.dma(SP, lambda e: e.dma_start(out=mem_gr[:], in_=mem_g_d[:, :, :]), new_csem(), writes=[t_mem_gr])
            pre_g = sbuf(s1, "pre_g_sb", [128, 8, 128], F32)
            K.dma(SP, lambda e: e.dma_start(out=pre_g[:], in_=pre_g_d[:, :, :]), new_csem(), writes=[t_pre_g])
            conv_g["pre"] = (pre_g, t_pre_g)
            conv_g["mem"] = (mem_gr, t_mem_gr)
            if dbg.get("skip_p1"):
                conv_limit[0] = 10 ** 6
            with ExitStack() as s0:
                norm_tiles(s0, x_main, NTILE, hT, t_hT, "m", hook=lambda: conv_some(1))
                norm_tiles(s0, x_gh, 8, hT_gh, t_hgh, "g", hook=lambda: conv_some(1))
                conv_flush()
            K.barrier()
            conv_limit[0] = 10 ** 6
            conv_split[0] = ((DVE, 0, 8),)
            dbg_dump("hT", hT[:], t_hT)

            t_cs_d = [[T() for _ in range(NBLK)] for _ in range(8)]
            with ExitStack() as s2:
              if not dbg.get('skip_p1'):
                  wlr = sbuf(s2, "wlr", [128, 8, 32], BF16)
                  t_wlr = T()
                  load_wbf(wlr[:, :, 0:32], t_wlr, "in", OFF_LR, 32)
                  gw = sbuf(s2, "gw_sb", [16, 2, 512], BF16)
                  t_gw = T()
                  for d, src in enumerate((gw_f_d, gw_b_d)):
                      K.dma(SP, lambda e, src=src: e.dma_start(out=cf32[0:16, 0:512], in_=src[:, :]), new_csem(), writes=[t_cf32])
                      K.op(DVE, lambda e, d=d: e.tensor_copy(out=gw[:, d, :], in_=cf32[0:16, 0:512]),
                           reads=[t_cf32], writes=[t_gw])
                  negb = sbuf(s2, "negb", [128, 8], F32)
                  t_negb = T()
                  K.dma(SP, lambda e: e.dma_start(out=cf32[:, 0:8], in_=gb_d[:, :]), new_csem(), writes=[t_cf32])
                  K.op(DVE, lambda e: e.tensor_scalar(out=negb[:], in0=cf32[:, 0:8], scalar1=-1.0, scalar2=None,
                                                     op0=ALU.mult), reads=[t_cf32], writes=[t_negb])
                  rmask = sbuf(s2, "rmask_sb", [128, 512], F32)
                  t_rmask = T()
                  K.dma(SP, lambda e: e.dma_start(out=rmask[:], in_=rmask_d[:, :]), new_csem(), writes=[t_rmask])

                  lrT = [sbuf(s2, f"lrT{d}", [16, NTOK + GH], BF16) for d in range(2)]
                  t_lrT = [[T() for _ in range(9)] for _ in range(2)]

                  def tok_src(d, blk):
                      if d == 0:
                          if blk == 0:
                              return hT_gh[:, :, 0:GH], t_hgh[0:4]
                          return hT[:, :, (blk - 1) * 512:blk * 512], t_hT[(blk - 1) * 4:blk * 4]
                      if blk == 8:
                          return hT_gh[:, :, GH:2 * GH], t_hgh[4:8]
                      return hT[:, :, blk * 512:(blk + 1) * 512], t_hT[blk * 4:(blk + 1) * 4]

                  for d in range(2):
                      for blk in range(9):
                          src, ts = tok_src(d, blk)
                          pbuf, tp = pb.next()
                          mm_acc(pbuf[0:16, :], tp, [(wlr[:, c, d * 16:(d + 1) * 16], src[:, c, :]) for c in range(8)],
                                 reads=[t_wlr] + ts)
                          evac(lrT[d][:, blk * 512:(blk + 1) * 512], pbuf[0:16, :], [tp], [t_lrT[d][blk]])
                          conv_some(1)

                  wkv = sbuf(s2, "wkv", [128, 8, 384], BF16)
                  t_wkv = T()
                  kTh = sbuf(s2, "kTh", [128, 2 * GH], BF16)
                  t_kTh = T()
                  vh = sbuf(s2, "vh", [128, 8, 256], BF16)
                  t_vh = T()
                  sp_r = Ring([sbuf(s2, f"sp{i}", [128, 512], F32) for i in range(4)])
                  cs_r = Ring([sbuf(s2, f"cs{i}", [128, 512], F32) for i in range(3)])
                  cssem = [K.dsem() for _ in range(3)]
                  pfx_r = Ring([sbuf(s2, f"pfx{i}", [128, 512], F32) for i in range(2)])
                  cs_halo = [sbuf(s2, f"cs_halo{d}", [128, 512], F32) for d in range(2)]
                  t_cs_halo = [T(), T()]
                  cs_q = {id(t): [T() for _ in range(4)] for t in list(cs_r.ts) + t_cs_halo}
                  edec = Ring([sbuf(s2, f"edec{i}", [128, 128], F32) for i in range(4)])
                  kdT = Ring([sbuf(s2, f"kdT{i}", [128, 128], BF16) for i in range(4)])
                  kd = Ring([sbuf(s2, f"kd{i}", [128, 128], BF16) for i in range(4)])
                  sm1 = Ring([sbuf(s2, f"sm1_{i}", [128, 8], F32) for i in range(2)])

                  for h in range(4):
                      load_wbf(wkv[:, :, 0:384], t_wkv, "in", OFF_G(h) + 128, 384)
                      for half in range(2):
                          pbuf, tp = pb.next()
                          mm_acc(pbuf[:, :], tp, [(wkv[:, c, 0:128], hT_gh[:, c, half * GH:(half + 1) * GH])
                                                  for c in range(8)], reads=[t_wkv] + t_hgh[half * 4:half * 4 + 4])
                          evac(kTh[:, half * GH:(half + 1) * GH], pbuf[:, :], [tp], [t_kTh])
                      for ti in range(8):
                          phb, tph = ph.next()
                          mm_acc(phb[:, :], tph, [(hT_gh[:, c, ti * 128:(ti + 1) * 128], wkv[:, c, 128:384])
                                                  for c in range(8)], reads=[t_wkv, t_hgh[ti]])
                          evac(vh[:, ti, :], phb[:, :], [tph], [t_vh])
                      pend = [[], []]

                      def emit_block(d, blk, h=h):
                          u = d * 4 + h
                          is_halo = (blk == 0) if d == 0 else (blk == 8)
                          pbuf, tp = pb.next()
                          K.op(PE, lambda e: e.matmul(pbuf[:, :], lhsT=gw[:, d, h * 128:(h + 1) * 128],
                                                      rhs=lrT[d][:, blk * 512:(blk + 1) * 512], start=True, stop=True),
                               reads=[t_gw, t_lrT[d][blk]], writes=[tp])
                          spb, tsp = sp_r.next()
                          K.op(ACT, lambda e: e.activation(out=spb[:], in_=pbuf[:, :], func=AF.Exp, scale=-1.0,
                                                           bias=negb[:, u:u + 1]), reads=[tp, t_negb], writes=[tsp])
                          K.op(ACT, lambda e: e.activation(out=spb[:], in_=spb[:], func=AF.Ln, scale=1.0,
                                                           bias=EPS_AP[:, 1:2]), reads=[tsp, t_eps], writes=[tsp])
                          if is_halo:
                              csb, tcs = cs_halo[d], t_cs_halo[d]
                              csx = None
                          else:
                              csb, tcs = cs_r.next()
                              csx = cssem[(cs_r.i - 1) % 3]
                          tcq = cs_q[id(tcs)]
                          if d == 0:
                              K.op(DVE, lambda e: e.tensor_tensor_scan(out=csb[:], data0=rmask[:], data1=spb[:], initial=0.0,
                                                                       op0=ALU.mult, op1=ALU.add),
                                   reads=[tsp, t_rmask], writes=tcq)
                          else:
                              pfb, tpf = pfx_r.next()
                              K.op(DVE, lambda e: e.tensor_tensor_scan(out=pfb[:], data0=rmask[:], data1=spb[:], initial=0.0,
                                                                       op0=ALU.mult, op1=ALU.add),
                                   reads=[tsp, t_rmask], writes=[tpf])
                              for c4 in range(4):
                                  K.op(DVE, lambda e, c4=c4: e.scalar_tensor_tensor(
                                      out=csb[:, c4 * 128:(c4 + 1) * 128], in0=spb[:, c4 * 128:(c4 + 1) * 128],
                                      scalar=pfb[:, c4 * 128 + 127:c4 * 128 + 128], in1=pfb[:, c4 * 128:(c4 + 1) * 128],
                                      op0=ALU.add, op1=ALU.subtract), reads=[tsp, tpf], writes=[tcq[c4]])
                          if not is_halo:
                              mblk = blk - 1 if d == 0 else blk
                              K.dma(SP, lambda e: e.dma_start(out=cs_d[u, :, mblk * 512:(mblk + 1) * 512], in_=csb[:]),
                                    csx, reads=tcq, writes=[t_cs_d[u][mblk]])
                              return
                          K.op(POOL, lambda e: e.memset(S0[:, u, :], 0.0), writes=[t_S0[u]])
                          hoff = 0 if d == 0 else GH
                          corder = list(range(4)) if d == 0 else list(range(3, -1, -1))
                          lc0 = 127 if d == 0 else 0

                          def batch():
                              smb, tsm = sm1.next()
                              K.op(DVE, lambda e: e.tensor_scalar(
                                  out=smb[:, 0:4], in0=csb[:].rearrange("p (c t) -> p c t", c=4)[:, :, lc0],
                                  scalar1=-1.0 / 16, scalar2=None, op0=ALU.mult), reads=tcq, writes=[tsm])
                              K.op(ACT, lambda e: e.activation(out=smb[:, 4:8], in_=smb[:, 0:4], func=AF.Exp),
                                   reads=[tsm], writes=[tsm])
                              kds = []
                              for c4 in corder:
                                  edb, ted = edec.next()
                                  K.op(ACT, lambda e, c4=c4, edb=edb: e.activation(
                                      out=edb[:], in_=csb[:, c4 * 128:(c4 + 1) * 128], func=AF.Exp, scale=1.0 / 16,
                                      bias=smb[:, c4:c4 + 1]), reads=[tcq[c4], tsm], writes=[ted])
                                  kdTb, tkdT = kdT.next()
                                  K.op(POOL, lambda e, c4=c4, edb=edb, kdTb=kdTb: e.tensor_tensor(
                                      out=kdTb[:], in0=kTh[:, hoff + c4 * 128:hoff + (c4 + 1) * 128], in1=edb[:],
                                      op=ALU.mult), reads=[t_kTh, ted], writes=[tkdT])
                                  kds.append((c4, kdTb, tkdT))
                              us = []
                              for (c4, kdTb, tkdT) in kds:
                                  ptt, tpt = pt.next()
                                  K.op(PE, lambda e, ptt=ptt, kdTb=kdTb: e.transpose(out=ptt, in_=kdTb[:], identity=ident[:]),
                                       reads=[tkdT, t_ident], writes=[tpt])
                                  kdb, tkd = kd.next()
                                  evac(kdb[:], ptt, [tpt], [tkd], eng=ACT)
                                  phb, tph = ph.next()
                                  vti = (0 if d == 0 else 4) + c4
                                  K.op(PE, lambda e, phb=phb, kdb=kdb, vti=vti: e.matmul(
                                      phb[:, :], lhsT=kdb[:], rhs=vh[:, vti, :], start=True, stop=True),
                                      reads=[tkd, t_vh], writes=[tph])
                                  K.op(DVE, lambda e, c4=c4, phb=phb: e.scalar_tensor_tensor(
                                      out=S0[:, u, :], in0=S0[:, u, :], scalar=smb[:, 4 + c4:5 + c4], in1=phb[:, :],
                                      op0=ALU.mult, op1=ALU.add), reads=[tsm, tph], writes=[t_S0[u]])
                          pend[d].append(batch)

                      for i in range(9):
                          for d in range(2):
                              emit_block(d, i if d == 0 else 8 - i)
                          conv_some(1)
                      for d in range(2):
                          while pend[d]:
                              pend[d].pop(0)()
                  dbg_dump("S0", S0[:], t_S0)
              conv_flush()
            K.barrier()
        K.barrier()

        def mix_store(stb, tst, sem, ch0, nch, blk):
            K.dma(SP, lambda e: e.dma_start(
                out=mixT_d[ch0:ch0 + nch, :, blk * 512:(blk + 1) * 512].rearrange("c p t -> p c t"),
                in_=stb[:, 0:nch, :]), sem, reads=mix_q[id(tst)], writes=[t_mixT[ch0 + k][blk] for k in range(nch)])

        zsem = K.dsem("zfill")

        def mix_zero(ch0, nch):
            with ExitStack() as sz:
                zt = sbuf(sz, f"zt{ch0}", [128, 512], BF16)
                t_zt = T()
                K.op(POOL, lambda e: e.memset(zt[:], 0.0), writes=[t_zt])
                for ch in range(ch0, ch0 + nch):
                    for blk in range(NBLK):
                        K.dma(SP, lambda e, ch=ch, blk=blk: e.dma_start(out=mixT_d[ch, :, blk * 512:(blk + 1) * 512],
                                                                        in_=zt[:]), zsem, reads=[t_zt],
                              writes=[t_mixT[ch][blk]])
                K.barrier()

        if dbg.get("skip_nat"):
            mix_zero(8, 4)
        if dbg.get("skip_mem"):
            mix_zero(12, 4)
        if dbg.get("skip_gla"):
            mix_zero(0, 8)
        if not dbg.get("skip_nat"):
          with ExitStack() as sn:
            hTx = sbuf(sn, "hTx", [128, 8, 2 * NH], BF16)
            t_hTx = [T() for _ in range(4)]
            with ExitStack() as s0:
                norm_tiles(s0, x_nh, 4, hTx, t_hTx, "n")
            K.barrier()

            def ext_src(tok0, n):
                out = []
                t = tok0
                end = tok0 + n
                while t < end:
                    if t < NH:
                        e2 = min(end, NH)
                        out.append((hTx[:, :, t:e2], [t_hTx[k] for k in range(t // 128, (e2 - 1) // 128 + 1)], e2 - t))
                    elif t < NH + NTOK:
                        e2 = min(end, NH + NTOK)
                        a, b = t - NH, e2 - NH
                        out.append((hT[:, :, a:b], [t_hT[k] for k in range(a // 128, (b - 1) // 128 + 1)], e2 - t))
                    else:
                        e2 = end
                        a, b = t - NTOK - NH + NH, e2 - NTOK - NH + NH
                        out.append((hTx[:, :, a:b], [t_hTx[k] for k in range(a // 128, (b - 1) // 128 + 1)], e2 - t))
                    t = e2
                return out

            Egen = sbuf(sn, "Egen", [128, 8, 512], BF16)
            t_Egen = T()
            Esp = sbuf(sn, "Esp", [128, 10, 8, 64], BF16)
            t_Esp = T()
            with ExitStack() as stb:
                tabf = sbuf(stb, "tabf", [128, 8, 512], F32)
                t_tabf = T()
                K.dma(SP, lambda e: e.dma_start(out=tabf[:], in_=nat_gen_d[:, :, :]), new_csem(), writes=[t_tabf])
                K.op(ACT, lambda e: e.activation(out=Egen[:], in_=tabf[:], func=AF.Copy), reads=[t_tabf], writes=[t_Egen])
                tabs = sbuf(stb, "tabs", [128, 10, 8, 64], F32)
                t_tabs = T()
                K.dma(SP, lambda e: e.dma_start(out=tabs[:], in_=nat_sp_d[:, :, :, :]), new_csem(), writes=[t_tabs])
                K.op(ACT, lambda e: e.activation(out=Esp[:], in_=tabs[:], func=AF.Copy), reads=[t_tabs], writes=[t_Esp])
                K.barrier()

            wn = sbuf(sn, "wn", [128, 8, 512], BF16)
            t_wn = T()
            t_wng = T()
            qTn = sbuf(sn, "qTn", [128, NTOK], BF16)
            t_qTn = [T() for _ in range(NBLK)]
            kTn = sbuf(sn, "kTn", [128, NEXT], BF16)
            t_kTn = [T() for _ in range(9)]
            vTn = sbuf(sn, "vTn", [128, NEXT], BF16)
            t_vTn = [T() for _ in range(9)]
            Ve = sbuf(sn, "Ve", [128, 36, 2, 65], BF16)
            t_Ve = [T() for _ in range(36)]
            Vo = sbuf(sn, "Vo", [128, 36, 2, 65], BF16)
            t_Vo = [T() for _ in range(36)]
            K.op(POOL, lambda e: e.memset(Ve[:, :, :, 64:65], 1.0), writes=t_Ve)
            K.op(POOL, lambda e: e.memset(Vo[:, :, :, 64:65], 1.0), writes=t_Vo)
            pex = Ring([sbuf(sn, f"pex{i}", [128, 512], F32) for i in range(3)])
            pT = Ring([sbuf(sn, f"pT{i}", [128, 2, 4, 64], BF16) for i in range(8)])
            gateTn = Ring([sbuf(sn, f"gateTn{i}", [128, 512], BF16) for i in range(3)])
            thn = Ring([sbuf(sn, f"thn{i}", [128, 512], F32) for i in range(2)])
            rden = Ring([sbuf(sn, f"rden{i}", [128, 2], F32) for i in range(3)])
            ogt = Ring([sbuf(sn, f"ogt{i}", [128, 128], BF16) for i in range(4)])
            og_q = {id(t): [T(), T()] for t in ogt.ts}
            pS = Ring([pb.bufs[0], pb.bufs[1], pb.bufs[2]], ts=[pb.ts[0], pb.ts[1], pb.ts[2]])
            pG = Ring([_phb[2]], ts=[ph.ts[2]])
            pV = Ring([_phb[0][:, 0:256], _phb[1][:, 0:256]], ts=[ph.ts[0], ph.ts[1]])

            load_wbf(wn[:, :, 0:384], t_wn, "in", OFF_N(0), 384)
            for j in range(4):
                load_wbf(wn[:, :, 384:512], t_wng, "in", OFF_N(j) + 384, 128)
                for blk in range(NBLK):
                    pbuf, tp = pS.next()
                    mm_acc(pbuf[:, :], tp, [(wn[:, c, 0:128], hT[:, c, blk * 512:(blk + 1) * 512]) for c in range(8)],
                           reads=[t_wn] + t_hT[blk * 4:blk * 4 + 4])
                    evac(qTn[:, blk * 512:(blk + 1) * 512], pbuf[:, :], [tp], [t_qTn[blk]])
                for (dstT, tdst, co) in ((kTn, t_kTn, 128), (vTn, t_vTn, 256)):
                    for blk in range(9):
                        pbuf, tp = pS.next()
                        pieces = ext_src(blk * 512, 512)
                        off = 0
                        for (src, ts, ln) in pieces:
                            mm_acc(pbuf[:, off:off + ln], tp, [(wn[:, c, co:co + 128], src[:, c, :]) for c in range(8)],
                                   reads=[t_wn] + ts)
                            off += ln
                        evac(dstT[:, blk * 512:(blk + 1) * 512], pbuf[:, :], [tp], [tdst[blk]])
                for (Vb, tV, n, base) in ((Ve, t_Ve, 36, 0), (Vo, t_Vo, 35, 64)):
                    for e0 in range(0, n, 4):
                        ne = min(4, n - e0)
                        pbk, tpbk = pt.next_bank()
                        for k in range(ne):
                            tok0 = base + (e0 + k) * 128
                            vb0 = tok0 // 512
                            vts = [t_vTn[vb0]] + ([t_vTn[vb0 + 1]] if (tok0 + 127) // 512 != vb0 else [])
                            K.op(PE, lambda e, pbk=pbk, k=k, tok0=tok0: e.transpose(
                                out=pbk[:, k, :], in_=vTn[:, tok0:tok0 + 128], identity=ident[:]),
                                reads=vts + [t_ident], writes=[tpbk])
                        evac(Vb[:, e0:e0 + ne, :, 0:64], pbk[:, 0:ne, :].rearrange("p e (a d) -> p e a d", a=2),
                             [tpbk], [tV[e0 + k] for k in range(ne)])

                if j + 1 < 4:
                    load_wbf(wn[:, :, 0:384], t_wn, "in", OFF_N(j + 1), 384)
                gate_blk = {}
                gate_done = set()

                def nat_gate_steps(blk):
                    thb, tth = thn.next()
                    gtb, tgt = gateTn.next()
                    gate_blk[blk] = (gtb, tgt)
                    gbank = {}
                    steps = []
                    for c0 in range(0, 8, 2):
                        def sg_(c0=c0):
                            if c0 == 0:
                                gbank["b"] = pG.next()
                            pbuf, tp = gbank["b"]
                            for c in (c0, c0 + 1):
                                K.op(PE, lambda e, c=c: e.matmul(pbuf[:, :], lhsT=wn[:, c, 384:512],
                                                                 rhs=hT[:, c, blk * 512:(blk + 1) * 512], start=(c == 0),
                                                                 stop=(c == 7)), reads=[t_wng] + t_hT[blk * 4:blk * 4 + 4],
                                     writes=[tp])
                            if c0 == 6:
                                K.op(ACT, lambda e: e.activation(out=thb[:], in_=pbuf[:, :], func=AF.Tanh, scale=0.5),
                                     reads=[tp], writes=[tth])
                                K.op(DVE, lambda e: e.scalar_tensor_tensor(out=gtb[:], in0=thb[:], scalar=1.0,
                                                                           in1=pbuf[:, :], op0=ALU.add, op1=ALU.mult),
                                     reads=[tth, tp], writes=[tgt])
                        steps.append(sg_)
                    return steps

                nat_pending = []

                stA = {}
                stB = {}
                stage_buf = {}

                def natA(p, j=j):
                    if p % 4 == 0:
                        for gb_ in (p // 4, p // 4 + 1):
                            if gb_ < NBLK and gb_ not in gate_done:
                                gate_done.add(gb_)
                                sts = nat_gate_steps(gb_)
                                if gb_ == 0:
                                    for st_ in sts:
                                        st_()
                                else:
                                    while nat_pending:
                                        nat_pending.pop(0)()
                                    nat_pending.extend(sts)
                    banks = [pS.next(), pS.next()]
                    for b in range(2):
                        for t in range(4):
                            k0 = (2 * p + b + 2 * t) * 64
                            kb = k0 // 512
                            kbs = [t_kTn[kb]] + ([t_kTn[kb + 1]] if (k0 + 127) // 512 != kb else [])
                            col = (b * 4 + t) * 64
                            for a in range(2):
                                pbuf, tp = banks[a]
                                pa = slice(a * 64, (a + 1) * 64)
                                K.op(PE, lambda e, pbuf=pbuf, pa=pa, k0=k0, col=col, b=b: e.matmul(
                                    pbuf[:, col:col + 64], lhsT=kTn[pa, k0:k0 + 128],
                                    rhs=qTn[pa, p * 128 + b * 64:p * 128 + b * 64 + 64], start=True, stop=True),
                                    reads=kbs + [t_qTn[p // 4]], writes=[tp])
                    outs = []
                    for a in range(2):
                        hd = 2 * j + a
                        pbuf, tp = banks[a]
                        pexb, tpex = pex.next()
                        pTb, tpT = pT.next()
                        spec = {}
                        for b in range(2):
                            for t in range(4):
                                if (2 * p + b, t) in SP_TILES:
                                    spec[(b, t)] = SP_TILES.index((2 * p + b, t))
                        if not spec:
                            K.op(DVE, lambda e, pexb=pexb, pbuf=pbuf, hd=hd: e.scalar_tensor_tensor(
                                out=pexb[:], in0=pbuf[:, :], scalar=0.125, in1=Egen[:, hd, :], op0=ALU.mult, op1=ALU.add),
                                reads=[tp, t_Egen], writes=[tpex])
                        else:
                            first = True
                            for b in range(2):
                                for t in range(4):
                                    col = (b * 4 + t) * 64
                                    if (b, t) in spec:
                                        tab = Esp[:, spec[(b, t)], hd, :]
                                    else:
                                        tab = Egen[:, hd, col:col + 64]
                                    K.op(DVE, lambda e, pexb=pexb, pbuf=pbuf, col=col, tab=tab: e.scalar_tensor_tensor(
                                        out=pexb[:, col:col + 64], in0=pbuf[:, col:col + 64], scalar=0.125, in1=tab,
                                        op0=ALU.mult, op1=ALU.add), reads=[tp, t_Egen, t_Esp], writes=[tpex],
                                        same_ok=(not first))
                                    first = False
                        K.op(ACT, lambda e, pexb=pexb, pTb=pTb: e.activation(
                            out=pTb[:].rearrange("p b t q -> p (b t q)"), in_=pexb[:], func=AF.Exp),
                            reads=[tpex], writes=[tpT])
                        outs.append((pTb, tpT))
                    stA[p] = outs

                def natB(p):
                    outs = stA.pop(p)
                    pho, tpho = pV.next()
                    for b in range(2):
                        for a in range(2):
                            for t in range(4):
                                pTb, tpT = outs[a]
                                if b == 0:
                                    vt, tv = Ve[:, p + t, a, :], t_Ve[p + t]
                                else:
                                    vt, tv = Vo[:, p + t, a, :], t_Vo[p + t]
                                K.op(PE, lambda e, pho=pho, pTb=pTb, b=b, t=t, vt=vt, a=a: e.matmul(
                                    pho[b * 64:(b + 1) * 64, a * 65:(a + 1) * 65], lhsT=pTb[:, b, t, :], rhs=vt,
                                    start=(t == 0), stop=(t == 3)), reads=[tpT, tv], writes=[tpho])
                    rdb, trd = rden.next()
                    pho3 = pho[:, 0:130].rearrange("p (a d) -> p a d", a=2)
                    K.op(DVE, lambda e: e.reciprocal(out=rdb[:], in_=pho3[:, :, 64]), reads=[tpho], writes=[trd])
                    ogb, tog = ogt.next()
                    togh = og_q[id(tog)]
                    for a in range(2):
                        K.op(DVE, lambda e, a=a: e.tensor_scalar(out=ogb[:, a * 64:(a + 1) * 64], in0=pho3[:, a, 0:64],
                                                                 scalar1=rdb[:, a:a + 1], scalar2=0.5, op0=ALU.mult,
                                                                 op1=ALU.mult), reads=[tpho, trd], writes=[togh[a]])
                    stB[p] = (ogb, togh)

                def natC(p, j=j):
                    ogb, tog = stB.pop(p)
                    blk = p // 4
                    if p % 4 == 0:
                        stb, tst = mix_stage.next()
                        stage_buf[blk] = (stb, tst, mixsem[(mix_stage.i - 1) % 2])
                    stb, tst, msem = stage_buf[blk]
                    gtb, tgt = gate_blk[blk]
                    q0 = (p % 4) * 128
                    ptt, tpt = pt.next()
                    K.op(PE, lambda e: e.transpose(out=ptt, in_=ogb[:], identity=ident[:]), reads=list(tog) + [t_ident],
                         writes=[tpt])
                    K.op(DVE, lambda e: e.tensor_tensor(out=stb[:, 0, q0:q0 + 128], in0=ptt, in1=gtb[:, q0:q0 + 128],
                                                        op=ALU.mult), reads=[tpt, tgt], writes=[mix_q[id(tst)][p % 4]])
                    if p % 4 == 3:
                        mix_store(stb, tst, msem, 8 + j, 1, blk)
                        del gate_blk[blk]

                LA_, LB_ = 2, 1
                for p in range(LA_):
                    natA(p)
                for p in range(NTILE + LA_ + LB_):
                    if p < NTILE:
                        natB(p)
                    if 0 <= p - LB_ < NTILE:
                        natC(p - LB_)
                    if p + LA_ < NTILE:
                        natA(p + LA_)
                    if nat_pending:
                        nat_pending.pop(0)()
                while nat_pending:
                    nat_pending.pop(0)()
            K.barrier()
        K.barrier()

        if not dbg.get("skip_mem"):
          with ExitStack() as sm_:
            mT = sbuf(sm_, "mT", [128, 8, 256], BF16)
            t_mT = [T() for _ in range(2)]
            with ExitStack() as s0:
                norm_tiles(s0, mem_d, 2, mT, t_mT, "mm")
            K.barrier()
            mem_g = sbuf(sm_, "mem_g_sb", [128, 8, 128], F32)
            t_mem_g = T()
            K.dma(SP, lambda e: e.dma_start(out=mem_g[:], in_=mem_g_d[:, :, :]), new_csem(), writes=[t_mem_g])
            K.barrier()
            mkT = sbuf(sm_, "mkT", [128, 4, 256], BF16)
            t_mkT = T()
            mva = sbuf(sm_, "mva", [128, 2, 4, 129], BF16)
            t_mva = T()
            K.op(POOL, lambda e: e.memset(mva[:, :, :, 128:129], 1.0), writes=[t_mva])
            with ExitStack() as sk:
                wkvm = sbuf(sk, "wkvm", [128, 8, 1024], BF16)
                t_wkvm = T()
                load_wbf(wkvm[:, :, :], t_wkvm, "kv", 0, 1024)
                for h in range(4):
                    phb, tph = ph.next()
                    mm_acc(phb[:, :], tph, [(wkvm[:, c, h * 128:(h + 1) * 128], mT[:, c, :]) for c in range(8)],
                           reads=[t_wkvm] + t_mT)
                    evac(mkT[:, h, :], phb[:, :], [tph], [t_mkT])
                for mt in range(2):
                    pbuf, tp = pb.next()
                    mm_acc(pbuf[:, :], tp, [(mT[:, c, mt * 128:(mt + 1) * 128], wkvm[:, c, 512:1024]) for c in range(8)],
                           reads=[t_wkvm, t_mT[mt]])
                    evac(mva[:, mt, :, 0:128], pbuf[:, :].rearrange("p (h d) -> p h d", h=4), [tp], [t_mva])
                K.barrier()
            wm = sbuf(sm_, "wm", [128, 8, 256], BF16)
            t_wm = T()
            t_wmg = T()
            qTm = sbuf(sm_, "qTm", [128, NTOK], BF16)
            t_qTm = [T() for _ in range(NBLK)]
            pTm = Ring([sbuf(sm_, f"pTm{i}", [128, 2, 512], BF16) for i in range(3)])
            gateTm = Ring([sbuf(sm_, f"gateTm{i}", [128, 512], BF16) for i in range(3)])
            thm = Ring([sbuf(sm_, f"thm{i}", [128, 512], F32) for i in range(2)])
            rden = Ring([sbuf(sm_, f"rdenm{i}", [128, 2], F32) for i in range(3)])
            ogt = Ring([sbuf(sm_, f"ogtm{i}", [128, 128], BF16) for i in range(4)])
            pS = Ring([pb.bufs[0], pb.bufs[1], pb.bufs[2], _phb[2]], ts=[pb.ts[0], pb.ts[1], pb.ts[2], ph.ts[2]])
            pV = Ring([_phb[0][:, 0:256], _phb[1][:, 0:256]], ts=[ph.ts[0], ph.ts[1]])
            load_wbf(wm[:, :, 0:128], t_wm, "in", OFF_M(0), 128)
            for h in range(4):
                load_wbf(wm[:, :, 128:256], t_wmg, "in", OFF_M(h) + 128, 128)
                for blk in range(NBLK):
                    pbuf, tp = pS.next()
                    mm_acc(pbuf[:, :], tp, [(wm[:, c, 0:128], hT[:, c, blk * 512:(blk + 1) * 512]) for c in range(8)],
                           reads=[t_wm] + t_hT[blk * 4:blk * 4 + 4])
                    evac(qTm[:, blk * 512:(blk + 1) * 512], pbuf[:, :], [tp], [t_qTm[blk]])
                if h + 1 < 4:
                    load_wbf(wm[:, :, 0:128], t_wm, "in", OFF_M(h + 1), 128)
                blkA = {}
                stB = {}
                stage_buf = {}

                def memA_steps(blk, h=h):
                    pTb, tpT = pTm.next()
                    gtb, tgt = gateTm.next()
                    thb, tth = thm.next()
                    blkA[blk] = (pTb, tpT, gtb, tgt)
                    steps = []
                    for mt in range(2):
                        def st(mt=mt):
                            pbuf, tp = pS.next()
                            K.op(PE, lambda e: e.matmul(
                                pbuf[:, :], lhsT=mkT[:, h, mt * 128:(mt + 1) * 128], rhs=qTm[:, blk * 512:(blk + 1) * 512],
                                start=True, stop=True), reads=[t_mkT, t_qTm[blk]], writes=[tp])
                            K.op(ACT, lambda e: e.activation(out=pTb[:, mt, :], in_=pbuf[:, :], func=AF.Exp,
                                                             scale=128.0 ** -0.5), reads=[tp], writes=[tpT])
                        steps.append(st)
                    gbank = {}
                    for c0 in range(0, 8, 2):
                        def sg_(c0=c0):
                            if c0 == 0:
                                gbank["b"] = pS.next()
                            pbuf, tp = gbank["b"]
                            for c in (c0, c0 + 1):
                                K.op(PE, lambda e, c=c: e.matmul(pbuf[:, :], lhsT=wm[:, c, 128:256],
                                                                 rhs=hT[:, c, blk * 512:(blk + 1) * 512], start=(c == 0),
                                                                 stop=(c == 7)), reads=[t_wmg] + t_hT[blk * 4:blk * 4 + 4],
                                     writes=[tp])
                            if c0 == 6:
                                K.op(ACT, lambda e: e.activation(out=thb[:], in_=pbuf[:, :], func=AF.Tanh, scale=0.5),
                                     reads=[tp], writes=[tth])
                                K.op(DVE, lambda e: e.scalar_tensor_tensor(out=gtb[:], in0=thb[:], scalar=1.0,
                                                                           in1=pbuf[:, :], op0=ALU.add, op1=ALU.mult),
                                     reads=[tth, tp], writes=[tgt])
                        steps.append(sg_)
                    return steps

                def memB(p, h=h):
                    blk = p // 4
                    pTb, tpT, gtb, tgt = blkA[blk]
                    q0 = (p % 4) * 128
                    for _ in range(dbg.get("mem_fill", 0)):
                        jb, tj = pS.next()
                        K.op(PE, lambda e, jb=jb: e.matmul(jb[:, :], lhsT=wm[:, 0, 0:128], rhs=qTm[:, 0:512],
                                                           start=True, stop=True), reads=[t_wm], writes=[tj])
                    pho, tpho = pV.next()
                    for mt in range(2):
                        K.op(PE, lambda e, mt=mt: e.matmul(pho[:, 0:129], lhsT=pTb[:, mt, q0:q0 + 128], rhs=mva[:, mt, h, :],
                                                           start=(mt == 0), stop=(mt == 1)), reads=[tpT, t_mva], writes=[tpho])
                    rdb, trd = rden.next()
                    K.op(DVE, lambda e: e.reciprocal(out=rdb[:, 0:1], in_=pho[:, 128:129]), reads=[tpho], writes=[trd])
                    ogb, tog = ogt.next()
                    K.op(DVE, lambda e: e.tensor_scalar(out=ogb[:], in0=pho[:, 0:128], scalar1=rdb[:, 0:1], scalar2=0.5,
                                                        op0=ALU.mult, op1=ALU.mult), reads=[tpho, trd], writes=[tog])
                    stB[p] = (ogb, tog)

                def memC(p, h=h):
                    ogb, tog = stB.pop(p)
                    blk = p // 4
                    if p % 4 == 0:
                        stb, tst = mix_stage.next()
                        stage_buf[blk] = (stb, tst, mixsem[(mix_stage.i - 1) % 2])
                    stb, tst, msem = stage_buf[blk]
                    pTb, tpT, gtb, tgt = blkA[blk]
                    q0 = (p % 4) * 128
                    ptt, tpt = pt.next()
                    K.op(PE, lambda e: e.transpose(out=ptt, in_=ogb[:], identity=ident[:]), reads=[tog, t_ident],
                         writes=[tpt])
                    K.op(DVE, lambda e: e.tensor_tensor(out=stb[:, 0, q0:q0 + 128], in0=ptt, in1=gtb[:, q0:q0 + 128],
                                                        op=ALU.mult), reads=[tpt, tgt], writes=[mix_q[id(tst)][p % 4]])
                    if p % 4 == 3:
                        mix_store(stb, tst, msem, 12 + h, 1, blk)
                        del blkA[blk]

                for st in memA_steps(0):
                    st()
                pending = []
                for p in range(NTILE + 1):
                    if p % 4 == 0 and p // 4 + 1 < NBLK:
                        pending = memA_steps(p // 4 + 1)
                    if p < NTILE:
                        memB(p)
                    if p >= 1:
                        memC(p - 1)
                    k = 2 if p % 4 < 2 else 1
                    for _ in range(k):
                        if pending:
                            pending.pop(0)()
                    if p % 4 == 3:
                        while pending:
                            pending.pop(0)()
            K.barrier()
        K.barrier()

        if not dbg.get("skip_gla"):
          with ExitStack() as sg:
            wg = sbuf(sg, "wg", [128, 8, 768], BF16)
            t_wg = T()
            qTg = sbuf(sg, "qTg", [128, NTOK], BF16)
            kTg = sbuf(sg, "kTg", [128, NTOK], BF16)
            t_qTg = [T() for _ in range(NBLK)]
            t_kTg = [T() for _ in range(NBLK)]
            vg = sbuf(sg, "vg", [128, NTILE, 256], BF16)
            t_vg = [T() for _ in range(NTILE)]
            oacc = sbuf(sg, "oacc", [128, NTILE, 256], F32)
            t_oacc = [T() for _ in range(NTILE)]
            gnfm = sbuf(sg, "gnfm", [128, 2], F32)
            t_gnfm = T()
            K.dma(SP, lambda e: e.dma_start(out=cf32[:, 0:2], in_=gnorm_fm_d[:, :]), new_csem(), writes=[t_cf32])
            K.op(DVE, lambda e: e.tensor_scalar(out=gnfm[:], in0=cf32[:, 0:2], scalar1=0.5, scalar2=None, op0=ALU.mult),
                 reads=[t_cf32], writes=[t_gnfm])
            csl = [Ring([sbuf(sg, f"csl{d}{i}", [128, 512], F32) for i in range(2)]) for d in range(2)]
            cslsem = [[K.dsem() for _ in range(2)] for _ in range(2)]
            exr = Ring([sbuf(sg, f"exr{i}", [128, 512], F32) for i in range(2)])
            qB = [Ring([sbuf(sg, f"qB{d}{i}", [128, 512], BF16) for i in range(2)]) for d in range(2)]
            kB = [Ring([sbuf(sg, f"kB{d}{i}", [128, 512], BF16) for i in range(2)]) for d in range(2)]
            nbl = [Ring([sbuf(sg, f"nbl{d}{i}", [128, 8], F32) for i in range(2)]) for d in range(2)]
            edec = Ring([sbuf(sg, f"gedec{i}", [128, 128], F32) for i in range(3)])
            kdT = Ring([sbuf(sg, f"gkdT{i}", [128, 128], BF16) for i in range(6)])
            kd = Ring([sbuf(sg, f"gkd{i}", [128, 128], BF16) for i in range(6)])
            attm = Ring([sbuf(sg, f"attm{i}", [128, 128], BF16) for i in range(6)])
            Sf = [sbuf(sg, f"Sf{d}", [128, 256], F32) for d in range(2)]
            t_Sf = [T(), T()]
            Sb = [Ring([sbuf(sg, f"Sb{d}{i}", [128, 256], BF16) for i in range(2)]) for d in range(2)]
            ssall = sbuf(sg, "ssall", [128, 2, NTILE], F32)
            t_ssall = T()
            t_ss = [T() for _ in range(NTILE)]
            gateAll = sbuf(sg, "gateAll", [128, 2, NTOK], BF16)
            t_gate = [[T(), T()] for _ in range(NBLK)]
            onb = Ring([sbuf(sg, f"onb{i}", [128, 256], BF16) for i in range(4)])
            pU = Ring([_phb[0][:, 0:256], _phb[1][:, 0:256]], ts=[ph.ts[0], ph.ts[1]])
            pA = Ring([pb.bufs[2][:, 0:128], _phb[2][:, 0:128]], ts=[pb.ts[2], ph.ts[2]])
            pO = Ring([pb.bufs[0][:, 0:256], pb.bufs[1][:, 0:256]], ts=[pb.ts[0], pb.ts[1]])

            load_wbf(wg[:, :, :], t_wg, "in", OFF_G(0), 768)
            for h in range(4):
                for blk in range(NBLK):
                    for (dst, tds, co) in ((qTg, t_qTg, 0), (kTg, t_kTg, 128)):
                        pbuf, tp = pb.next()
                        mm_acc(pbuf[:, :], tp, [(wg[:, c, co:co + 128], hT[:, c, blk * 512:(blk + 1) * 512])
                                                for c in range(8)], reads=[t_wg] + t_hT[blk * 4:blk * 4 + 4])
                        evac(dst[:, blk * 512:(blk + 1) * 512], pbuf[:, :], [tp], [tds[blk]])
                for blk in range(NBLK):
                    for cc in range(2):
                        pbuf, tp = pb.next()
                        mm_acc(pbuf[:, :], tp, [(wg[:, c, 512 + cc * 128:512 + (cc + 1) * 128],
                                                 hT[:, c, blk * 512:(blk + 1) * 512]) for c in range(8)],
                               reads=[t_wg] + t_hT[blk * 4:blk * 4 + 4])
                        thb, tth = exr.next()
                        K.op(ACT, lambda e, thb=thb, pbuf=pbuf: e.activation(out=thb[:], in_=pbuf[:, :], func=AF.Tanh,
                                                                             scale=0.5), reads=[tp], writes=[tth])
                        K.op(DVE, lambda e, thb=thb, pbuf=pbuf: e.scalar_tensor_tensor(
                            out=thb[:], in0=thb[:], scalar=1.0, in1=pbuf[:, :], op0=ALU.add, op1=ALU.mult),
                            reads=[tth, tp], writes=[tth])
                        K.op(ACT, lambda e, thb=thb, cc=cc, blk=blk: e.activation(
                            out=gateAll[:, cc, blk * 512:(blk + 1) * 512], in_=thb[:], func=AF.Copy,
                            scale=gnfm[:, cc:cc + 1]), reads=[tth, t_gnfm], writes=[t_gate[blk][cc]])
                for p in range(0, NTILE, 2):
                    pbuf, tp = pb.next()
                    for k2 in range(2):
                        mm_acc(pbuf[:, k2 * 256:(k2 + 1) * 256], tp,
                               [(hT[:, c, (p + k2) * 128:(p + k2 + 1) * 128], wg[:, c, 256:512]) for c in range(8)],
                               reads=[t_wg, t_hT[p + k2]])
                    evac(vg[:, p:p + 2, :], pbuf[:, :].rearrange("p (a d) -> p a d", a=2), [tp], [t_vg[p], t_vg[p + 1]])
                if h + 1 < 4:
                    load_wbf(wg[:, :, :], t_wg, "in", OFF_G(h + 1), 768)
                cur_Sb = [None, None]
                for d in range(2):
                    u = d * 4 + h
                    K.op(POOL, lambda e, d=d, u=u: e.tensor_copy(out=Sf[d][:], in_=S0[:, u, :]), reads=[t_S0[u]],
                         writes=[t_Sf[d]])
                    sbb, tsb = Sb[d].next()
                    K.op(POOL, lambda e, sbb=sbb, u=u: e.tensor_copy(out=sbb[:], in_=S0[:, u, :]), reads=[t_S0[u]],
                         writes=[tsb])
                    cur_Sb[d] = (sbb, tsb)
                first_written = [False] * NTILE

                blkstate = {}

                def prep_block(d, blk, h=h):
                    u = d * 4 + h
                    csb, tcs = csl[d].next()
                    sem = cslsem[d][(csl[d].i - 1) % 2]
                    K.dma(SP, lambda e: e.dma_start(out=csb[:], in_=cs_d[u, :, blk * 512:(blk + 1) * 512]), sem,
                          reads=[t_cs_d[u][blk]], writes=[tcs])
                    qb, tqb = qB[d].next()
                    kb, tkb = kB[d].next()
                    nb, tnb = nbl[d].next()
                    ex0, tex0 = exr.next()
                    K.op(ACT, lambda e: e.activation(out=ex0[:], in_=csb[:], func=AF.Exp, scale=-1.0 / 16),
                         reads=[tcs], writes=[tex0])
                    K.op(DVE, lambda e: e.scalar_tensor_tensor(
                        out=qb[:], in0=qTg[:, blk * 512:(blk + 1) * 512], scalar=128.0 ** -0.5, in1=ex0[:],
                        op0=ALU.mult, op1=ALU.mult), reads=[t_qTg[blk], tex0], writes=[tqb])
                    ex1, tex1 = exr.next()
                    K.op(ACT, lambda e: e.activation(out=ex1[:], in_=csb[:], func=AF.Exp, scale=1.0 / 16),
                         reads=[tcs], writes=[tex1])
                    K.op(POOL, lambda e: e.tensor_tensor(
                        out=kb[:], in0=kTg[:, blk * 512:(blk + 1) * 512], in1=ex1[:], op=ALU.mult),
                        reads=[t_kTg[blk], tex1], writes=[tkb])
                    lc0 = 127 if d == 0 else 0
                    K.op(DVE, lambda e: e.tensor_scalar(
                        out=nb[:, 0:4], in0=csb[:].rearrange("p (c t) -> p c t", c=4)[:, :, lc0], scalar1=-1.0 / 16,
                        scalar2=None, op0=ALU.mult), reads=[tcs], writes=[tnb])
                    K.op(ACT, lambda e: e.activation(out=nb[:, 4:8], in_=nb[:, 0:4], func=AF.Exp), reads=[tnb], writes=[tnb])
                    blkstate[(d, blk)] = (csb, tcs, qb, tqb, kb, tkb, nb, tnb)

                items = []
                for step in range(NBLK):
                    for k4 in range(4):
                        items.append((0, step, k4))
                        items.append((1, NBLK - 1 - step, 3 - k4))
                pre_out = {}

                stA_out = {}

                def stageA(i):
                    d, blk, c4 = items[i]
                    if (d, blk) not in blkstate:
                        prep_block(d, blk)
                    csb, tcs, qb, tqb, kb, tkb, nb, tnb = blkstate[(d, blk)]
                    p = blk * 4 + c4
                    cs_ = slice(c4 * 128, (c4 + 1) * 128)
                    edb, ted = edec.next()
                    K.op(ACT, lambda e: e.activation(out=edb[:], in_=csb[:, cs_], func=AF.Exp, scale=1.0 / 16,
                                                     bias=nb[:, c4:c4 + 1]), reads=[tcs, tnb], writes=[ted])
                    kdTb, tkdT = kdT.next()
                    K.op(POOL, lambda e: e.tensor_tensor(out=kdTb[:], in0=kTg[:, p * 128:(p + 1) * 128], in1=edb[:],
                                                         op=ALU.mult), reads=[t_kTg[blk], ted], writes=[tkdT])
                    stA_out[i] = (kdTb, tkdT)

                def stageB(i):
                    d, blk, c4 = items[i]
                    csb, tcs, qb, tqb, kb, tkb, nb, tnb = blkstate[(d, blk)]
                    kdTb, tkdT = stA_out.pop(i)
                    cs_ = slice(c4 * 128, (c4 + 1) * 128)
                    ptt, tpt = pt.next()
                    K.op(PE, lambda e: e.transpose(out=ptt, in_=kdTb[:], identity=ident[:]),
                         reads=[tkdT, t_ident], writes=[tpt])
                    kdb, tkd = kd.next()
                    evac(kdb[:], ptt, [tpt], [tkd], eng=ACT)
                    pha, tpha = pA.next()
                    K.op(PE, lambda e: e.matmul(pha, lhsT=kb[:, cs_], rhs=qb[:, cs_], start=True, stop=True),
                         reads=[tkb, tqb], writes=[tpha])
                    atb, tat = attm.next()
                    K.op(DVE, lambda e: e.tensor_tensor(out=atb[:], in0=pha, in1=cmask[:, d * 128:(d + 1) * 128],
                                                        op=ALU.mult), reads=[tpha, t_cmask], writes=[tat])
                    pre_out[i] = (atb, tat, kdb, tkd)

                def post(i):
                    d, blk, c4 = items[i]
                    csb, tcs, qb, tqb, kb, tkb, nb, tnb = blkstate[(d, blk)]
                    atb, tat, kdb, tkd = pre_out.pop(i)
                    p = blk * 4 + c4
                    cs_ = slice(c4 * 128, (c4 + 1) * 128)
                    sbb, tsb = cur_Sb[d]
                    phu, tphu = pU.next()
                    K.op(PE, lambda e: e.matmul(phu, lhsT=kdb[:], rhs=vg[:, p, :], start=True, stop=True),
                         reads=[tkd, t_vg[p]], writes=[tphu])
                    pho, tpho = pO.next()
                    K.op(PE, lambda e: e.matmul(pho, lhsT=atb[:], rhs=vg[:, p, :], start=True, stop=False),
                         reads=[tat, t_vg[p]], writes=[tpho])
                    K.op(PE, lambda e: e.matmul(pho, lhsT=qb[:, cs_], rhs=sbb[:], start=False, stop=True),
                         reads=[tqb, tsb], writes=[tpho])
                    K.op(DVE, lambda e: e.scalar_tensor_tensor(
                        out=Sf[d][:], in0=Sf[d][:], scalar=nb[:, 4 + c4:5 + c4], in1=phu, op0=ALU.mult, op1=ALU.add),
                        reads=[tnb, tphu], writes=[t_Sf[d]])
                    nsb, tnsb = Sb[d].next()
                    K.op(DVE, lambda e: e.tensor_copy(out=nsb[:], in_=Sf[d][:]), reads=[t_Sf[d]], writes=[tnsb])
                    cur_Sb[d] = (nsb, tnsb)
                    if not first_written[p]:
                        K.op(ACT, lambda e: e.activation(out=oacc[:, p, :], in_=pho, func=AF.Copy),
                             reads=[tpho], writes=[t_oacc[p]])
                        first_written[p] = True
                    else:
                        K.op(DVE, lambda e: e.tensor_tensor(out=oacc[:, p, :], in0=pho, in1=oacc[:, p, :], op=ALU.add),
                             reads=[tpho, t_oacc[p]], writes=[t_oacc[p]])
                        K.op(ACT, lambda e: e.activation(out=junk[:, 0:256], in_=oacc[:, p, :], func=AF.Square,
                                                         accum_out=ssall[:, 0, p:p + 1]),
                             reads=[t_oacc[p]], writes=[t_junk, t_ss[p]])

                n_it = len(items)
                LA, LB = 5, 2
                for i in range(min(LA, n_it)):
                    stageA(i)
                for i in range(min(LB, n_it)):
                    stageB(i)
                for i in range(n_it):
                    if i + LA < n_it:
                        stageA(i + LA)
                    if i + LB < n_it:
                        stageB(i + LB)
                    post(i)

                K.op(ACT, lambda e: e.activation(out=ssall[:, 1, :], in_=ssall[:, 0, :], func=AF.Ln, scale=1.0 / 256,
                                                 bias=EPS_AP[:, 0:1]), reads=t_ss + [t_eps], writes=[t_ssall])
                K.op(ACT, lambda e: e.activation(out=ssall[:, 1, :], in_=ssall[:, 1, :], func=AF.Exp, scale=-0.5),
                     reads=[t_ssall], writes=[t_ssall])

                fA = {}
                fB = {}
                fstage = {}

                def finA(p):
                    onbb, tonb = onb.next()
                    K.op(ACT, lambda e: e.activation(out=onbb[:], in_=oacc[:, p, :], func=AF.Copy,
                                                     scale=ssall[:, 1, p:p + 1]), reads=[t_oacc[p], t_ssall], writes=[tonb])
                    fA[p] = (onbb, tonb)

                def finB(p):
                    onbb, tonb = fA.pop(p)
                    pbk, tpbk = pt.next_bank()
                    for cc in range(2):
                        K.op(PE, lambda e, cc=cc: e.transpose(out=pbk[:, cc, :], in_=onbb[:, cc * 128:(cc + 1) * 128],
                                                              identity=ident[:]), reads=[tonb, t_ident], writes=[tpbk])
                    fB[p] = (pbk, tpbk)

                def finC(p, h=h):
                    pbk, tpbk = fB.pop(p)
                    blk = p // 4
                    if p % 4 == 0:
                        stb, tst = mix_stage.next()
                        fstage[blk] = (stb, tst, mixsem[(mix_stage.i - 1) % 2])
                    stb, tst, msem = fstage[blk]
                    q0 = (p % 4) * 128
                    K.op(DVE, lambda e: e.tensor_tensor(out=stb[:, :, q0:q0 + 128], in0=pbk[:, 0:2, :],
                                                        in1=gateAll[:, :, p * 128:(p + 1) * 128], op=ALU.mult),
                         reads=[tpbk] + t_gate[blk], writes=[mix_q[id(tst)][p % 4]])
                    if p % 4 == 3:
                        mix_store(stb, tst, msem, 2 * h, 2, blk)

                finA(0)
                finA(1)
                finB(0)
                for p in range(NTILE):
                    if p + 2 < NTILE:
                        finA(p + 2)
                    if p + 1 < NTILE:
                        finB(p + 1)
                    finC(p)
            K.barrier()
        K.barrier()

        with ExitStack() as so:
          if dbg.get("skip_out"):
            K.final_wait(SP)
          else:
              wo = sbuf(so, "wo", [128, 16, 1024], BF16)
              t_wo = T()
              for kh in range(2):
                  load_wbf(wo[:, kh * 8:(kh + 1) * 8, :], t_wo, "out", 0, 1024) if False else None
              load_wbf(wo[:, :, :], t_wo, "out", 0, 1024)
              gB = sbuf(so, "gB", [128, 1024], F32)
              t_gB = T()
              K.dma(SP, lambda e: e.dma_start(out=gB[:], in_=post_g_d[:, :]), new_csem(), writes=[t_gB])
              mx = Ring([sbuf(so, f"mx{i}", [128, 16, 512], BF16) for i in range(2)])
              mxsem = [K.dsem() for _ in range(2)]
              xr = Ring([sbuf(so, f"xr{i}", [128, 1024], F32) for i in range(3)])
              xrsem = [K.dsem() for _ in range(3)]
              yt = Ring([sbuf(so, f"yt{i}", [128, 1024], F32) for i in range(3)])
              ysem = [K.dsem() for _ in range(3)]
              smo = Ring([sbuf(so, f"smo{i}", [128, 4], F32) for i in range(3)])
              pO6 = Ring([pb.bufs[0], pb.bufs[1], pb.bufs[2], _phb[0], _phb[1], _phb[2]],
                         ts=[pb.ts[0], pb.ts[1], pb.ts[2], ph.ts[0], ph.ts[1], ph.ts[2]])
              mxblk = {}
              oA = {}

              def load_mx(blk):
                  mxb, tmx = mx.next()
                  sem = mxsem[(mx.i - 1) % 2]
                  for kh in range(2):
                      K.dma(SP, lambda e, kh=kh: e.dma_start(
                          out=mxb[:, kh * 8:(kh + 1) * 8, :],
                          in_=mixT_d[kh * 8:(kh + 1) * 8, :, blk * 512:(blk + 1) * 512].rearrange("c p t -> p c t")), sem,
                          reads=[t_mixT[c][blk] for c in range(kh * 8, kh * 8 + 8)], writes=[tmx])
                  mxblk[blk] = (mxb, tmx)

              def outA(p):
                  blk = p // 4
                  if blk not in mxblk:
                      load_mx(blk)
                  if p % 4 == 0 and blk + 1 < NBLK and (blk + 1) not in mxblk:
                      load_mx(blk + 1)
                  mxb, tmx = mxblk[blk]
                  q0 = (p % 4) * 128
                  xb, tx = xr.next()
                  xsm = xrsem[(xr.i - 1) % 3]
                  K.dma(SP, lambda e: e.dma_start(out=xb[:], in_=x_main[p * 128:(p + 1) * 128, :]), xsm, writes=[tx])
                  smb, tsm = smo.next()
                  pbs = []
                  for hf in range(2):
                      pbuf, tp = pO6.next()
                      mm_acc(pbuf[:, :], tp, [(mxb[:, kc, q0:q0 + 128], wo[:, kc, hf * 512:(hf + 1) * 512])
                                              for kc in range(16)], reads=[tmx, t_wo])
                      K.op(ACT, lambda e, pbuf=pbuf, hf=hf: e.activation(
                          out=junk[:, 0:512], in_=pbuf[:, :], func=AF.Square, accum_out=smb[:, hf:hf + 1]),
                          reads=[tp], writes=[t_junk, tsm])
                      pbs.append((pbuf, tp))
                  oA[p] = (xb, tx, smb, tsm, pbs)

              def outB(p):
                  xb, tx, smb, tsm, pbs = oA.pop(p)
                  K.op(DVE, lambda e: e.tensor_tensor(out=smb[:, 2:3], in0=smb[:, 0:1], in1=smb[:, 1:2], op=ALU.add),
                       reads=[tsm], writes=[tsm])
                  K.op(ACT, lambda e: e.activation(out=smb[:, 3:4], in_=smb[:, 2:3], func=AF.Ln, scale=1.0 / 1024,
                                                   bias=EPS_AP[:, 0:1]), reads=[tsm, t_eps], writes=[tsm])
                  K.op(ACT, lambda e: e.activation(out=smb[:, 3:4], in_=smb[:, 3:4], func=AF.Exp, scale=-0.5),
                       reads=[tsm], writes=[tsm])
                  yb, ty = yt.next()
                  ysm = ysem[(yt.i - 1) % 3]
                  for hf, (pbuf, tp) in enumerate(pbs):
                      K.op(DVE, lambda e, pbuf=pbuf, hf=hf: e.scalar_tensor_tensor(
                          out=yb[:, hf * 512:(hf + 1) * 512], in0=pbuf[:, :], scalar=smb[:, 3:4],
                          in1=gB[:, hf * 512:(hf + 1) * 512], op0=ALU.mult, op1=ALU.mult),
                          reads=[tp, tsm, t_gB], writes=[ty])
                  K.op(POOL, lambda e: e.tensor_tensor(out=yb[:], in0=yb[:], in1=xb[:], op=ALU.add),
                       reads=[tx, ty], writes=[ty])
                  K.dma(SP, lambda e: e.dma_start(out=y_out[p * 128:(p + 1) * 128, :], in_=yb[:]), ysm, reads=[ty])

              outA(0)
              for p in range(NTILE):
                  if p + 1 < NTILE:
                      outA(p + 1)
                  outB(p)
              K.final_wait(SP)
        with nc.Block() as block:
            K.replay(block)
    return nc


def _nat_tables(rpb, first, last):
    kc = np.arange(64)[:, None]
    qc = np.arange(64)[None, :]
    cs = np.clip(qc - 8, 0, 48)
    col_in = (kc >= cs) & (kc < cs + 16)
    dcol = np.clip(kc - qc + 15, 0, 30)

    def blockfor(drow):
        if drow < 0 or drow > 14:
            return np.full((8, 64, 64), MASKV, np.float32)
        b = rpb[:, drow][:, dcol]
        return np.where(col_in[None], b, np.float32(MASKV)).astype(np.float32)

    gen = np.empty((128, 8, 2, 4, 64), np.float32)
    for t in range(4):
        for a in range(2):
            blk = blockfor(3 + 2 * t + a)
            for b in range(2):
                gen[a * 64:(a + 1) * 64, :, b, t, :] = blk.transpose(1, 0, 2)
    sp = np.empty((128, 10, 8, 64), np.float32)
    for i, (r, t) in enumerate(SP_TILES):
        for a in range(2):
            slot = r - 4 + 2 * t + a
            drow = 3 + 2 * t + a
            if first and slot < 0:
                drow = 2 * t + a + 11
            if last and slot >= 64:
                drow = 2 * t + a - 5 if slot <= 66 else -1
            sp[a * 64:(a + 1) * 64, i, :, :] = blockfor(drow).transpose(1, 0, 2)
    return gen.reshape(128, 8, 512), sp


def _core_inputs(c, inp, consts):
    if c < 4:
        xs, mem, seg, nseg = inp["x_prompt"][0], inp["mem_prompt"][0], c, 4
    else:
        b = (c - 4) // 2
        xs, mem, seg, nseg = inp["x_sample"][b], inp["mem_sample"][b], (c - 4) % 2, 2
    t0 = seg * NTOK
    first, last = seg == 0, seg == nseg - 1
    x_main = xs[t0:t0 + NTOK]
    x_gh = np.zeros((2 * GH, 1024), np.float32)
    if not first:
        x_gh[:GH] = xs[t0 - GH:t0]
    if not last:
        x_gh[GH:] = xs[t0 + NTOK:t0 + NTOK + GH]
    x_nh = np.zeros((2 * NH, 1024), np.float32)
    if first:
        x_nh[:NH] = x_main[4 * 64:8 * 64]
    else:
        x_nh[:NH] = xs[t0 - NH:t0]
    if last:
        x_nh[NH:NH + 192] = x_main[56 * 64:59 * 64]
    else:
        x_nh[NH:] = xs[t0 + NTOK:t0 + NTOK + NH]
    gen, sp = _nat_tables(inp["nat_rpb"][0], first, last)
    d = dict(consts)
    d.update(x_main=np.ascontiguousarray(x_main), x_gh=x_gh, x_nh=x_nh, mem=np.ascontiguousarray(mem),
             nat_gen=gen, nat_sp=sp)
    return d


def _consts(inp):
    f = np.float32
    j = np.arange(128)[:, None]
    i = np.arange(128)[None, :]
    cmask = np.concatenate([(j <= i).astype(f), (j >= i).astype(f)], axis=1)
    rmask = np.ones((128, 512), f)
    rmask[:, ::128] = 0.0
    gb = np.concatenate([inp["gla_b_fwd"][0].reshape(4, 128).T, inp["gla_b_bwd"][0].reshape(4, 128).T], axis=1)
    return dict(
        w_in=np.ascontiguousarray(inp["w_in"][0]), w_out=np.ascontiguousarray(inp["w_out"][0]),
        w_mkv=np.ascontiguousarray(inp["w_mem_kv"][0]),
        pre_g=np.ascontiguousarray(np.broadcast_to(inp["pre_norm_g"][0].reshape(8, 128).T[:, :, None], (128, 8, 128))),
        mem_g=np.ascontiguousarray(np.broadcast_to(inp["mem_norm_g"][0].reshape(8, 128).T[:, :, None], (128, 8, 128))),
        post_g=np.ascontiguousarray(np.broadcast_to(inp["post_norm_g"][0][None, :], (128, 1024))),
        gnorm=np.ascontiguousarray(np.broadcast_to(inp["gla_norm_g"][0][None, :], (128, 256))),
        gnorm_fm=np.ascontiguousarray(inp["gla_norm_g"][0].reshape(2, 128).T),
        gw_f=np.ascontiguousarray(inp["gla_w_fwd"][0]), gw_b=np.ascontiguousarray(inp["gla_w_bwd"][0]),
        gb=np.ascontiguousarray(gb.astype(f)),
        ident=np.eye(128, dtype=f), cmask=cmask, rmask=rmask)


def kernel(**inputs):
    inp = {k: np.asarray(v) for k, v in inputs.items()}
    consts = _consts(inp)
    in_maps = [_core_inputs(c, inp, consts) for c in range(8)]
    nc = build_nc()
    res = run_bass_kernel_spmd(nc, in_maps, core_ids=list(range(8)))
    ys = [np.asarray(r["y"], dtype=np.float32) for r in res.results]
    y_prompt = np.concatenate(ys[0:4], axis=0)[None]
    y_sample = np.stack([np.concatenate(ys[4:6], axis=0), np.concatenate(ys[6:8], axis=0)], axis=0)
    return (y_prompt, y_sample)
```

```python
import numpy as np
from contextlib import ExitStack
import concourse.bass as bass
import concourse.mybir as mybir
from concourse.bass_utils import run_bass_kernel_spmd

F32 = mybir.dt.float32
BF16 = mybir.dt.bfloat16
AF = mybir.ActivationFunctionType
ALU = mybir.AluOpType

PE, ACT, DVE, POOL, SP = "tensor", "scalar", "vector", "gpsimd", "sync"
ENGS = (PE, ACT, DVE, POOL, SP)

NTOK = 4096
NTILE = 32
NBLK = 8
GH = 512
NH = 256
NEXT = NTOK + 2 * NH
EPS = 1e-6
MASKV = -30000.0

C_GQ, C_GK, C_GV, C_GG, C_LRF, C_LRB = 0, 512, 1024, 2048, 3072, 3088
C_NQ, C_NK, C_NV, C_NG, C_MQ, C_MG = 3104, 3616, 4128, 4640, 5152, 5664
IN_W = 6176

SP_TILES = [(0, 0), (0, 1), (1, 0), (1, 1), (2, 0), (3, 0), (61, 3), (62, 3), (63, 2), (63, 3)]


class T:
    __slots__ = ("name", "w", "r")

    def __init__(self, name=""):
        self.name = name
        self.w = None
        self.r = {}


class Kern:
    def __init__(self, nc, stack):
        self.nc = nc
        self.stack = stack
        self.prog = {e: [] for e in ENGS}
        self.sems = {}
        self.cnt = {}
        self.seen = {e: {} for e in ENGS}
        for e in ENGS:
            self._mksem(e)
        self.n_dsem = 0

    def _mksem(self, key):
        h = self.stack.enter_context(self.nc.semaphore("s_" + str(key)))
        self.sems[key] = h
        self.cnt[key] = 0
        return key

    def dsem(self, name=None):
        self.n_dsem += 1
        return self._mksem(name or f"d{self.n_dsem}")

    def _collect(self, eng, reads, writes, same_ok):
        deps = {}

        def need(k, v):
            if k == eng and same_ok:
                return
            if v > deps.get(k, 0):
                deps[k] = v
        for t in reads:
            if t.w is not None:
                need(*t.w)
        for t in writes:
            if t.w is not None:
                need(*t.w)
            for k, v in t.r.items():
                need(k, v)
        seen = self.seen[eng]
        out = []
        for k, v in deps.items():
            if seen.get(k, 0) >= v:
                continue
            seen[k] = v
            out.append((k, v))
        return out

    def op(self, eng, fn, reads=(), writes=(), same_ok=None):
        if same_ok is None:
            same_ok = (eng == PE)
        waits = self._collect(eng, reads, writes, same_ok)
        self.cnt[eng] += 1
        v = self.cnt[eng]
        self.prog[eng].append((waits, fn, eng, 1))
        for t in writes:
            t.w = (eng, v)
            t.r = {}
        for t in reads:
            if t not in writes:
                t.r[eng] = v
        return v

    def dma(self, q, fn, sem, reads=(), writes=()):
        waits = self._collect(q, reads, writes, False)
        self.cnt[sem] += 16
        v = self.cnt[sem]
        self.prog[q].append((waits, fn, sem, 16))
        for t in writes:
            t.w = (sem, v)
            t.r = {}
        for t in reads:
            if t not in writes:
                t.r[sem] = v
        return v

    def barrier(self):
        snap = dict(self.cnt)
        for e in ENGS:
            waits = []
            for k, v in snap.items():
                if k == e or v == 0:
                    continue
                if self.seen[e].get(k, 0) >= v:
                    continue
                self.seen[e][k] = v
                waits.append((k, v))
            if waits:
                self.prog[e].append((waits, None, None, 0))

    def final_wait(self, eng=SP):
        waits = [(k, v) for k, v in self.cnt.items() if v > 0 and k != eng]
        self.prog[eng].append((waits, None, None, 0))

    def replay(self, block):
        sems = self.sems
        for e in ENGS:
            prog = self.prog[e]

            def body(h, prog=prog):
                for waits, fn, key, inc in prog:
                    for k, v in waits:
                        h.wait_ge(sems[k], v)
                    if fn is not None:
                        fn(h).then_inc(sems[key], inc)
            getattr(block, e)(body)


class Ring:
    def __init__(self, bufs, ts=None, banks=None):
        self.bufs = bufs
        self.ts = ts if ts is not None else [T() for _ in bufs]
        self.banks = banks
        self.i = 0

    def next_bank(self):
        j = self.i % len(self.bufs)
        self.i += 1
        return self.banks[j], self.ts[j]

    def next(self):
        j = self.i % len(self.bufs)
        self.i += 1
        return self.bufs[j], self.ts[j]


def build_nc(dbg=None):
    dbg = dbg or {}
    dbg_specs = dbg.get("outs", {})
    nc = bass.Bass("TRN2", target_bir_lowering=False)
    din = lambda n, s, d=F32: nc.dram_tensor(n, list(s), d, kind="ExternalInput").ap()
    x_main = din("x_main", [NTOK, 1024])
    x_gh = din("x_gh", [2 * GH, 1024])
    x_nh = din("x_nh", [2 * NH, 1024])
    mem_d = din("mem", [256, 1024])
    w_in = din("w_in", [1024, IN_W])
    w_out = din("w_out", [2048, 1024])
    w_mkv = din("w_mkv", [1024, 1024])
    pre_g_d = din("pre_g", [128, 8, 128])
    mem_g_d = din("mem_g", [128, 8, 128])
    post_g_d = din("post_g", [128, 1024])
    gnorm_d = din("gnorm", [128, 256])
    gnorm_fm_d = din("gnorm_fm", [128, 2])
    gw_f_d = din("gw_f", [16, 512])
    gw_b_d = din("gw_b", [16, 512])
    gb_d = din("gb", [128, 8])
    nat_gen_d = din("nat_gen", [128, 8, 512])
    nat_sp_d = din("nat_sp", [128, 10, 8, 64])
    ident_d = din("ident", [128, 128])
    cmask_d = din("cmask", [128, 256])
    rmask_d = din("rmask", [128, 512])
    y_out = nc.dram_tensor("y", [NTOK, 1024], F32, kind="ExternalOutput").ap()
    mixT_d = nc.dram_tensor("mixT_scr", [16, 128, NTOK], BF16, kind="Internal").ap()
    cs_d = nc.dram_tensor("cs_scr", [8, 128, NTOK], F32, kind="Internal").ap()
    wbf_d = nc.dram_tensor("wbf_scr", [128, 8, IN_W], BF16, kind="Internal").ap()
    wkvbf_d = nc.dram_tensor("wkvbf_scr", [128, 8, 1024], BF16, kind="Internal").ap()
    wobf_d = nc.dram_tensor("wobf_scr", [128, 16, 1024], BF16, kind="Internal").ap()
    dbg_out = {}
    for name, (shape, dt) in dbg_specs.items():
        dbg_out[name] = nc.dram_tensor("dbg_" + name, list(shape), dt, kind="ExternalOutput").ap()

    with ExitStack() as top:
        K = Kern(nc, top)

        def sbuf(st, n, s, d):
            return st.enter_context(nc.sbuf_tensor(n, list(s), d))

        def psum(st, n, s, d):
            return st.enter_context(nc.psum_tensor(n, list(s), d))

        pb = Ring([psum(top, f"pb{i}", [128, 512], F32) for i in range(3)])
        _phb = [psum(top, f"phb{i}", [128, 512], F32) for i in range(3)]
        ph = Ring([_phb[i][:, 0:256] for i in range(3)])
        _ptb = [psum(top, f"ptb{i}", [128, 8, 128], BF16) for i in range(2)]
        ptb = _ptb[0]
        pt = Ring([_ptb[i][:, 0, :] for i in range(2)], banks=_ptb)
        pt_full_T = [pt.ts[0]]

        hT = sbuf(top, "hT_sb", [128, 8, NTOK], BF16)
        t_hT = [T() for _ in range(NTILE)]
        ident = sbuf(top, "ident_sb", [128, 128], BF16)
        t_ident = T()
        cmask = sbuf(top, "cmask_sb", [128, 256], BF16)
        t_cmask = T()
        t_pre_g = T()
        S0 = sbuf(top, "S0_sb", [128, 8, 256], F32)
        t_S0 = [T() for _ in range(8)]
        cf32 = sbuf(top, "cf32", [128, 512], F32)
        t_cf32 = T()
        junk = sbuf(top, "junk_sb", [128, 1024], BF16)
        t_junk = T()
        wsem = [K.dsem() for _ in range(3)]
        wst_box = [None]
        cbf_box = [None]
        csem_n = [0]
        mix_stage = Ring([sbuf(top, f"mixst{i}", [128, 2, 512], BF16) for i in range(2)])
        mix_q = {id(t): [T() for _ in range(4)] for t in mix_stage.ts}
        mixsem = [K.dsem() for _ in range(2)]
        t_mixT = [[T() for _ in range(NBLK)] for _ in range(16)]
        dbgsem = K.dsem("dbg")

        def new_csem():
            csem_n[0] += 1
            return K.dsem(f"const{csem_n[0]}")

        def dbg_dump(name, ap, reads):
            if name in dbg_out:
                K.dma(SP, lambda e: e.dma_start(out=dbg_out[name], in_=ap), dbgsem, reads=reads)

        def load_const_bf16(dst, t_dst, src_ap, ncol):
            K.dma(SP, lambda e: e.dma_start(out=cf32[:, 0:ncol], in_=src_ap), new_csem(), writes=[t_cf32])
            K.op(DVE, lambda e: e.tensor_copy(out=dst, in_=cf32[:, 0:ncol]), reads=[t_cf32], writes=[t_dst])

        load_const_bf16(ident[:], t_ident, ident_d[:, :], 128)
        load_const_bf16(cmask[:], t_cmask, cmask_d[:, :], 256)

        OFF_G = lambda h: h * 768
        OFF_LR = 3072
        OFF_N = lambda j: 3104 + j * 512
        OFF_M = lambda h: 5152 + h * 256
        conv_list = []
        for h in range(4):
            conv_list += [("in", C_GQ + h * 128, 128, OFF_G(h)), ("in", C_GK + h * 128, 128, OFF_G(h) + 128),
                          ("in", C_GV + h * 256, 128, OFF_G(h) + 256), ("in", C_GV + h * 256 + 128, 128, OFF_G(h) + 384),
                          ("in", C_GG + h * 256, 128, OFF_G(h) + 512), ("in", C_GG + h * 256 + 128, 128, OFF_G(h) + 640)]
        conv_list.append(("in", C_LRF, 32, OFF_LR))
        for j in range(4):
            conv_list += [("in", C_NQ + j * 128, 128, OFF_N(j)), ("in", C_NK + j * 128, 128, OFF_N(j) + 128),
                          ("in", C_NV + j * 128, 128, OFF_N(j) + 256), ("in", C_NG + j * 128, 128, OFF_N(j) + 384)]
        for h in range(4):
            conv_list += [("in", C_MQ + h * 128, 128, OFF_M(h)), ("in", C_MG + h * 128, 128, OFF_M(h) + 128)]
        for c0 in range(0, 1024, 128):
            conv_list.append(("kv", c0, 128, c0))
        for kh in range(2):
            for c0 in range(0, 1024, 128):
                conv_list.append(("out", c0, 128, kh * 8 * 1024 + c0))
        def _prio(it):
            kind, sc, n, dc = it
            if kind == "in" and dc == OFF_LR:
                return 0
            if kind == "in" and dc < 3072:
                return 1 if 128 <= (dc % 768) < 512 else 6
            if kind == "in" and dc < 5152:
                return 2 if dc < OFF_N(1) else 3
            if kind == "kv":
                return 4
            if kind == "in":
                return 5
            return 7
        conv_list.sort(key=_prio)
        merged = []
        for it in conv_list:
            if merged:
                k0, s0_, n0, d0 = merged[-1]
                if k0 == it[0] and n0 == 128 and it[2] == 128 and s0_ + n0 == it[1] and d0 + n0 == it[3] and k0 != "out":
                    merged[-1] = (k0, s0_, 256, d0)
                    continue
            merged.append(it)
        conv_list = merged
        n_early = sum(1 for it in conv_list if _prio(it) <= 2)
        conv_limit = [n_early]
        conv_split = [((DVE, 0, 5), (POOL, 5, 8))]
        t_wbf = {}
        cbsem = [K.dsem() for _ in range(3)]
        conv_pos = [0]
        conv_g = {}

        conv_loaded = []

        def conv_load():
            if conv_pos[0] >= min(len(conv_list), conv_limit[0]):
                return
            kind, sc, n, dc = conv_list[conv_pos[0]]
            conv_pos[0] += 1
            wst = wst_box[0]
            buf, tb = wst.next()
            sem = wsem[(wst.i - 1) % 3]
            if kind == "in":
                src = w_in[:, sc:sc + n].rearrange("(c p) n -> p c n", p=128)
                dst = wbf_d[:, :, dc:dc + n]
                grep, t_g = conv_g["pre"]
            elif kind == "kv":
                src = w_mkv[:, sc:sc + n].rearrange("(c p) n -> p c n", p=128)
                dst = wkvbf_d[:, :, dc:dc + n]
                grep, t_g = conv_g["mem"]
            else:
                kh, c0 = dc // 8192, dc % 8192
                src = w_out[kh * 1024:(kh + 1) * 1024, c0:c0 + n].rearrange("(c p) n -> p c n", p=128)
                dst = wobf_d[:, kh * 8:(kh + 1) * 8, c0:c0 + n]
                grep, t_g = None, None
            K.dma(SP, lambda e: e.dma_start(out=buf[:, :, 0:n], in_=src), sem, writes=[tb])
            conv_loaded.append((kind, n, dc, buf, tb, dst, grep, t_g))

        def conv_finish():
            if not conv_loaded:
                return
            kind, n, dc, buf, tb, dst, grep, t_g = conv_loaded.pop(0)
            cbf = cbf_box[0]
            ob, tob = cbf.next()
            osem = cbsem[(cbf.i - 1) % 3]
            if conv_split[0] == "act":
                for c in range(8):
                    if grep is not None:
                        K.op(ACT, lambda e, c=c: e.activation(out=ob[:, c, 0:n], in_=buf[:, c, 0:n], func=AF.Copy,
                                                              scale=grep[:, c, 0:1]), reads=[tb, t_g], writes=[tob],
                             same_ok=(c > 0))
                    else:
                        K.op(ACT, lambda e, c=c: e.activation(out=ob[:, c, 0:n], in_=buf[:, c, 0:n], func=AF.Copy),
                             reads=[tb], writes=[tob], same_ok=(c > 0))
            for (eng, c0_, c1_) in (conv_split[0] if conv_split[0] != "act" else ()):
                for n0 in range(0, n, 128):
                    n1 = min(n, n0 + 128)
                    if grep is not None:
                        K.op(eng, lambda e, c0_=c0_, c1_=c1_, n0=n0, n1=n1: e.tensor_tensor(
                            out=ob[:, c0_:c1_, n0:n1], in0=buf[:, c0_:c1_, n0:n1], in1=grep[:, c0_:c1_, 0:n1 - n0],
                            op=ALU.mult), reads=[tb, t_g], writes=[tob])
                    else:
                        K.op(eng, lambda e, c0_=c0_, c1_=c1_, n0=n0, n1=n1: e.tensor_copy(
                            out=ob[:, c0_:c1_, n0:n1], in_=buf[:, c0_:c1_, n0:n1]), reads=[tb], writes=[tob])
            conv_cast.append((kind, n, dc, ob, tob, osem, dst))

        conv_cast = []

        def conv_store():
            if not conv_cast:
                return
            kind, n, dc, ob, tob, osem, dst = conv_cast.pop(0)
            tw = T()
            t_wbf[(kind, dc, n)] = tw
            K.dma(POOL, lambda e: e.dma_start(out=dst, in_=ob[:, :, 0:n]), osem, reads=[tob], writes=[tw])

        def conv_some(k):
            lim = min(len(conv_list), conv_limit[0])
            for _ in range(k):
                if len(conv_cast) > 1 or (conv_pos[0] >= lim and not conv_loaded):
                    conv_store()
                conv_load()
                if len(conv_loaded) > 2 or conv_pos[0] >= lim:
                    conv_finish()

        def conv_flush():
            lim = min(len(conv_list), conv_limit[0])
            while conv_pos[0] < lim or conv_loaded or conv_cast:
                conv_some(1)

        lwsem = [K.dsem() for _ in range(8)]
        lw_i = [0]

        def load_wbf(dst_ap, t_dst, kind, col0, n):
            sem = lwsem[lw_i[0] % 8]
            lw_i[0] += 1
            if kind == "in":
                src = wbf_d[:, :, col0:col0 + n]
                deps = [t for (kd_, dc, nn), t in t_wbf.items() if kd_ == "in" and dc < col0 + n and dc + nn > col0]
            elif kind == "kv":
                src = wkvbf_d[:, :, col0:col0 + n]
                deps = [t for (kd_, dc, nn), t in t_wbf.items() if kd_ == "kv"]
            else:
                src = wobf_d[:, :, col0:col0 + n]
                deps = [t for (kd_, dc, nn), t in t_wbf.items() if kd_ == "out"]
            K.dma(SP, lambda e: e.dma_start(out=dst_ap, in_=src), sem, reads=deps, writes=[t_dst])

        def norm_tiles(st, src_d, ntiles, dstT, t_dst, tag, hook=None):
            xs = Ring([sbuf(st, f"xs_{tag}{i}", [128, 1024], F32) for i in range(3)])
            xsem = [K.dsem() for _ in range(3)]
            xn = Ring([sbuf(st, f"xn_{tag}{i}", [128, 1024], BF16) for i in range(3)])
            sm = Ring([sbuf(st, f"sm_{tag}{i}", [128, 4], F32) for i in range(3)])
            s1out = {}

            def S1(i):
                xb, tx = xs.next()
                sem = xsem[(xs.i - 1) % 3]
                K.dma(SP, lambda e: e.dma_start(out=xb[:], in_=src_d[i * 128:(i + 1) * 128, :]), sem, writes=[tx])
                smb, tsm = sm.next()
                K.op(ACT, lambda e: e.activation(out=junk[:], in_=xb[:], func=AF.Square, accum_out=smb[:, 0:1]),
                     reads=[tx], writes=[t_junk, tsm])
                K.op(ACT, lambda e: e.activation(out=smb[:, 1:2], in_=smb[:, 0:1], func=AF.Ln, scale=1.0 / 1024,
                                                 bias=EPS_AP[:, 0:1]), reads=[tsm, t_eps], writes=[tsm])
                K.op(ACT, lambda e: e.activation(out=smb[:, 2:3], in_=smb[:, 1:2], func=AF.Exp, scale=-0.5),
                     reads=[tsm], writes=[tsm])
                xnb, txn = xn.next()
                K.op(DVE, lambda e: e.tensor_scalar(out=xnb[:], in0=xb[:], scalar1=smb[:, 2:3], scalar2=None,
                                                    op0=ALU.mult), reads=[tx, tsm], writes=[txn])
                s1out[i] = (xnb, txn)

            def S2(i):
                xnb, txn = s1out.pop(i)
                pbk, tpbk = pt.next_bank()
                for c in range(8):
                    K.op(PE, lambda e, c=c: e.transpose(out=pbk[:, c, :], in_=xnb[:, c * 128:(c + 1) * 128],
                                                        identity=ident[:]), reads=[txn, t_ident], writes=[tpbk])
                K.op(ACT, lambda e: e.activation(out=dstT[:, :, i * 128:(i + 1) * 128], in_=pbk[:], func=AF.Copy),
                     reads=[tpbk], writes=[t_dst[i]])

            S1(0)
            for i in range(ntiles):
                if i + 1 < ntiles:
                    S1(i + 1)
                S2(i)
                if hook is not None:
                    hook()

        EPS_AP = sbuf(top, "eps_sb", [128, 2], F32)
        t_eps = T()
        K.op(POOL, lambda e: e.memset(EPS_AP[:, 0:1], EPS), writes=[t_eps])
        K.op(POOL, lambda e: e.memset(EPS_AP[:, 1:2], 1.0), writes=[t_eps])

        def mm_acc(out_ap, t_out, pairs, reads):
            n = len(pairs)
            for k, (l, r) in enumerate(pairs):
                K.op(PE, lambda e, l=l, r=r, k=k: e.matmul(out_ap, lhsT=l, rhs=r, start=(k == 0), stop=(k == n - 1)),
                     reads=reads, writes=[t_out])

        ev_rr = [0]

        def evac(out_ap, in_ap, reads, writes, eng=None):
            if eng is None:
                eng = ACT if ev_rr[0] % 2 == 0 else DVE
                ev_rr[0] += 1
            if eng == ACT:
                K.op(ACT, lambda e: e.activation(out=out_ap, in_=in_ap, func=AF.Copy), reads=reads, writes=writes)
            else:
                K.op(eng, lambda e: e.tensor_copy(out=out_ap, in_=in_ap), reads=reads, writes=writes)

        with ExitStack() as s1:
            hT_gh = sbuf(s1, "hT_gh", [128, 8, 2 * GH], BF16)
            t_hgh = [T() for _ in range(8)]
            wst_box[0] = Ring([sbuf(s1, f"wst{i}", [128, 8, 256], F32) for i in range(3)])
            cbf_box[0] = Ring([sbuf(s1, f"cbf{i}", [128, 8, 256], BF16) for i in range(3)])
            mem_gr = sbuf(s1, "mem_gr", [128, 8, 128], F32)
            t_mem_gr = T()
            K.dma(SP, lambda e: e.dma_start(out=mem_gr[:], in_=mem_g_d[:, :, :]), new_csem(), writes=[t_mem_gr])
            pre_g = sbuf(s1, "pre_g_sb", [128, 8, 128], F32)
            K.dma(SP, lambda e: e.dma_start(out=pre_g[:], in_=pre_g_d[:, :, :]), new_csem(), writes=[t_pre_g])
            conv_g["pre"] = (pre_g, t_pre_g)
            conv_g["mem"] = (mem_gr, t_mem_gr)
            if dbg.get("skip_p1"):
                conv_limit[0] = 10 ** 6
            with ExitStack() as s0:
                norm_tiles(s0, x_main, NTILE, hT, t_hT, "m", hook=lambda: conv_some(1))
                norm_tiles(s0, x_gh, 8, hT_gh, t_hgh, "g", hook=lambda: conv_some(1))
                conv_flush()
            K.barrier()
            conv_limit[0] = 10 ** 6
            conv_split[0] = ((DVE, 0, 8),)
            dbg_dump("hT", hT[:], t_hT)

            t_cs_d = [[T() for _ in range(NBLK)] for _ in range(8)]
            with ExitStack() as s2:
              if not dbg.get('skip_p1'):
                  wlr = sbuf(s2, "wlr", [128, 8, 32], BF16)
                  t_wlr = T()
                  load_wbf(wlr[:, :, 0:32], t_wlr, "in", OFF_LR, 32)
                  gw = sbuf(s2, "gw_sb", [16, 2, 512], BF16)
                  t_gw = T()
                  for d, src in enumerate((gw_f_d, gw_b_d)):
                      K.dma(SP, lambda e, src=src: e.dma_start(out=cf32[0:16, 0:512], in_=src[:, :]), new_csem(), writes=[t_cf32])
                      K.op(DVE, lambda e, d=d: e.tensor_copy(out=gw[:, d, :], in_=cf32[0:16, 0:512]),
                           reads=[t_cf32], writes=[t_gw])
                  negb = sbuf(s2, "negb", [128, 8], F32)
                  t_negb = T()
                  K.dma(SP, lambda e: e.dma_start(out=cf32[:, 0:8], in_=gb_d[:, :]), new_csem(), writes=[t_cf32])
                  K.op(DVE, lambda e: e.tensor_scalar(out=negb[:], in0=cf32[:, 0:8], scalar1=-1.0, scalar2=None,
                                                     op0=ALU.mult), reads=[t_cf32], writes=[t_negb])
                  rmask = sbuf(s2, "rmask_sb", [128, 512], F32)
                  t_rmask = T()
                  K.dma(SP, lambda e: e.dma_start(out=rmask[:], in_=rmask_d[:, :]), new_csem(), writes=[t_rmask])

                  lrT = [sbuf(s2, f"lrT{d}", [16, NTOK + GH], BF16) for d in range(2)]
                  t_lrT = [[T() for _ in range(9)] for _ in range(2)]

                  def tok_src(d, blk):
                      if d == 0:
                          if blk == 0:
                              return hT_gh[:, :, 0:GH], t_hgh[0:4]
                          return hT[:, :, (blk - 1) * 512:blk * 512], t_hT[(blk - 1) * 4:blk * 4]
                      if blk == 8:
                          return hT_gh[:, :, GH:2 * GH], t_hgh[4:8]
                      return hT[:, :, blk * 512:(blk + 1) * 512], t_hT[blk * 4:(blk + 1) * 4]

                  for d in range(2):
                      for blk in range(9):
                          src, ts = tok_src(d, blk)
                          pbuf, tp = pb.next()
                          mm_acc(pbuf[0:16, :], tp, [(wlr[:, c, d * 16:(d + 1) * 16], src[:, c, :]) for c in range(8)],
                                 reads=[t_wlr] + ts)
                          evac(lrT[d][:, blk * 512:(blk + 1) * 512], pbuf[0:16, :], [tp], [t_lrT[d][blk]])
                          conv_some(1)

                  wkv = sbuf(s2, "wkv", [128, 8, 384], BF16)
                  t_wkv = T()
                  kTh = sbuf(s2, "kTh", [128, 2 * GH], BF16)
                  t_kTh = T()
                  vh = sbuf(s2, "vh", [128, 8, 256], BF16)
                  t_vh = T()
                  sp_r = Ring([sbuf(s2, f"sp{i}", [128, 512], F32) for i in range(4)])
                  cs_r = Ring([sbuf(s2, f"cs{i}", [128, 512], F32) for i in range(3)])
                  cssem = [K.dsem() for _ in range(3)]
                  pfx_r = Ring([sbuf(s2, f"pfx{i}", [128, 512], F32) for i in range(2)])
                  cs_halo = [sbuf(s2, f"cs_halo{d}", [128, 512], F32) for d in range(2)]
                  t_cs_halo = [T(), T()]
                  cs_q = {id(t): [T() for _ in range(4)] for t in list(cs_r.ts) + t_cs_halo}
                  edec = Ring([sbuf(s2, f"edec{i}", [128, 128], F32) for i in range(4)])
                  kdT = Ring([sbuf(s2, f"kdT{i}", [128, 128], BF16) for i in range(4)])
                  kd = Ring([sbuf(s2, f"kd{i}", [128, 128], BF16) for i in range(4)])
                  sm1 = Ring([sbuf(s2, f"sm1_{i}", [128, 8], F32) for i in range(2)])

                  for h in range(4):
                      load_wbf(wkv[:, :, 0:384], t_wkv, "in", OFF_G(h) + 128, 384)
                      for half in range(2):
                          pbuf, tp = pb.next()
                          mm_acc(pbuf[:, :], tp, [(wkv[:, c, 0:128], hT_gh[:, c, half * GH:(half + 1) * GH])
                                                  for c in range(8)], reads=[t_wkv] + t_hgh[half * 4:half * 4 + 4])
                          evac(kTh[:, half * GH:(half + 1) * GH], pbuf[:, :], [tp], [t_kTh])
                      for ti in range(8):
                          phb, tph = ph.next()
                          mm_acc(phb[:, :], tph, [(hT_gh[:, c, ti * 128:(ti + 1) * 128], wkv[:, c, 128:384])
                                                  for c in range(8)], reads=[t_wkv, t_hgh[ti]])
                          evac(vh[:, ti, :], phb[:, :], [tph], [t_vh])
                      pend = [[], []]

                      def emit_block(d, blk, h=h):
                          u = d * 4 + h
                          is_halo = (blk == 0) if d == 0 else (blk == 8)
                          pbuf, tp = pb.next()
                          K.op(PE, lambda e: e.matmul(pbuf[:, :], lhsT=gw[:, d, h * 128:(h + 1) * 128],
                                                      rhs=lrT[d][:, blk * 512:(blk + 1) * 512], start=True, stop=True),
                               reads=[t_gw, t_lrT[d][blk]], writes=[tp])
                          spb, tsp = sp_r.next()
                          K.op(ACT, lambda e: e.activation(out=spb[:], in_=pbuf[:, :], func=AF.Exp, scale=-1.0,
                                                           bias=negb[:, u:u + 1]), reads=[tp, t_negb], writes=[tsp])
                          K.op(ACT, lambda e: e.activation(out=spb[:], in_=spb[:], func=AF.Ln, scale=1.0,
                                                           bias=EPS_AP[:, 1:2]), reads=[tsp, t_eps], writes=[tsp])
                          if is_halo:
                              csb, tcs = cs_halo[d], t_cs_halo[d]
                              csx = None
                          else:
                              csb, tcs = cs_r.next()
                              csx = cssem[(cs_r.i - 1) % 3]
                          tcq = cs_q[id(tcs)]
                          if d == 0:
                              K.op(DVE, lambda e: e.tensor_tensor_scan(out=csb[:], data0=rmask[:], data1=spb[:], initial=0.0,
                                                                       op0=ALU.mult, op1=ALU.add),
                                   reads=[tsp, t_rmask], writes=tcq)
                          else:
                              pfb, tpf = pfx_r.next()
                              K.op(DVE, lambda e: e.tensor_tensor_scan(out=pfb[:], data0=rmask[:], data1=spb[:], initial=0.0,
                                                                       op0=ALU.mult, op1=ALU.add),
                                   reads=[tsp, t_rmask], writes=[tpf])
                              for c4 in range(4):
                                  K.op(DVE, lambda e, c4=c4: e.scalar_tensor_tensor(
                                      out=csb[:, c4 * 128:(c4 + 1) * 128], in0=spb[:, c4 * 128:(c4 + 1) * 128],
                                      scalar=pfb[:, c4 * 128 + 127:c4 * 128 + 128], in1=pfb[:, c4 * 128:(c4 + 1) * 128],
                                      op0=ALU.add, op1=ALU.subtract), reads=[tsp, tpf], writes=[tcq[c4]])
                          if not is_halo:
                              mblk = blk - 1 if d == 0 else blk
                              K.dma(SP, lambda e: e.dma_start(out=cs_d[u, :, mblk * 512:(mblk + 1) * 512], in_=csb[:]),
                                    csx, reads=tcq, writes=[t_cs_d[u][mblk]])
                              return
                          K.op(POOL, lambda e: e.memset(S0[:, u, :], 0.0), writes=[t_S0[u]])
                          hoff = 0 if d == 0 else GH
                          corder = list(range(4)) if d == 0 else list(range(3, -1, -1))
                          lc0 = 127 if d == 0 else 0

                          def batch():
                              smb, tsm = sm1.next()
                              K.op(DVE, lambda e: e.tensor_scalar(
                                  out=smb[:, 0:4], in0=csb[:].rearrange("p (c t) -> p c t", c=4)[:, :, lc0],
                                  scalar1=-1.0 / 16, scalar2=None, op0=ALU.mult), reads=tcq, writes=[tsm])
                              K.op(ACT, lambda e: e.activation(out=smb[:, 4:8], in_=smb[:, 0:4], func=AF.Exp),
                                   reads=[tsm], writes=[tsm])
                              kds = []
                              for c4 in corder:
                                  edb, ted = edec.next()
                                  K.op(ACT, lambda e, c4=c4, edb=edb: e.activation(
                                      out=edb[:], in_=csb[:, c4 * 128:(c4 + 1) * 128], func=AF.Exp, scale=1.0 / 16,
                                      bias=smb[:, c4:c4 + 1]), reads=[tcq[c4], tsm], writes=[ted])
                                  kdTb, tkdT = kdT.next()
                                  K.op(POOL, lambda e, c4=c4, edb=edb, kdTb=kdTb: e.tensor_tensor(
                                      out=kdTb[:], in0=kTh[:, hoff + c4 * 128:hoff + (c4 + 1) * 128], in1=edb[:],
                                      op=ALU.mult), reads=[t_kTh, ted], writes=[tkdT])
                                  kds.append((c4, kdTb, tkdT))
                              us = []
                              for (c4, kdTb, tkdT) in kds:
                                  ptt, tpt = pt.next()
                                  K.op(PE, lambda e, ptt=ptt, kdTb=kdTb: e.transpose(out=ptt, in_=kdTb[:], identity=ident[:]),
                                       reads=[tkdT, t_ident], writes=[tpt])
                                  kdb, tkd = kd.next()
                                  evac(kdb[:], ptt, [tpt], [tkd], eng=ACT)
                                  phb, tph = ph.next()
                                  vti = (0 if d == 0 else 4) + c4
                                  K.op(PE, lambda e, phb=phb, kdb=kdb, vti=vti: e.matmul(
                                      phb[:, :], lhsT=kdb[:], rhs=vh[:, vti, :], start=True, stop=True),
                                      reads=[tkd, t_vh], writes=[tph])
                                  K.op(DVE, lambda e, c4=c4, phb=phb: e.scalar_tensor_tensor(
                                      out=S0[:, u, :], in0=S0[:, u, :], scalar=smb[:, 4 + c4:5 + c4], in1=phb[:, :],
                                      op0=ALU.mult, op1=ALU.add), reads=[tsm, tph], writes=[t_S0[u]])
                          pend[d].append(batch)

                      for i in range(9):
                          for d in range(2):
                              emit_block(d, i if d == 0 else 8 - i)
                          conv_some(1)
                      for d in range(2):
                          while pend[d]:
                              pend[d].pop(0)()
                  dbg_dump("S0", S0[:], t_S0)
              conv_flush()
            K.barrier()
        K.barrier()

        def mix_store(stb, tst, sem, ch0, nch, blk):
            K.dma(SP, lambda e: e.dma_start(
                out=mixT_d[ch0:ch0 + nch, :, blk * 512:(blk + 1) * 512].rearrange("c p t -> p c t"),
                in_=stb[:, 0:nch, :]), sem, reads=mix_q[id(tst)], writes=[t_mixT[ch0 + k][blk] for k in range(nch)])

        zsem = K.dsem("zfill")

        def mix_zero(ch0, nch):
            with ExitStack() as sz:
                zt = sbuf(sz, f"zt{ch0}", [128, 512], BF16)
                t_zt = T()
                K.op(POOL, lambda e: e.memset(zt[:], 0.0), writes=[t_zt])
                for ch in range(ch0, ch0 + nch):
                    for blk in range(NBLK):
                        K.dma(SP, lambda e, ch=ch, blk=blk: e.dma_start(out=mixT_d[ch, :, blk * 512:(blk + 1) * 512],
                                                                        in_=zt[:]), zsem, reads=[t_zt],
                              writes=[t_mixT[ch][blk]])
                K.barrier()

        if dbg.get("skip_nat"):
            mix_zero(8, 4)
        if dbg.get("skip_mem"):
            mix_zero(12, 4)
        if dbg.get("skip_gla"):
            mix_zero(0, 8)
        if not dbg.get("skip_nat"):
          with ExitStack() as sn:
            hTx = sbuf(sn, "hTx", [128, 8, 2 * NH], BF16)
            t_hTx = [T() for _ in range(4)]
            with ExitStack() as s0:
                norm_tiles(s0, x_nh, 4, hTx, t_hTx, "n")
            K.barrier()

            def ext_src(tok0, n):
                out = []
                t = tok0
                end = tok0 + n
                while t < end:
                    if t < NH:
                        e2 = min(end, NH)
                        out.append((hTx[:, :, t:e2], [t_hTx[k] for k in range(t // 128, (e2 - 1) // 128 + 1)], e2 - t))
                    elif t < NH + NTOK:
                        e2 = min(end, NH + NTOK)
                        a, b = t - NH, e2 - NH
                        out.append((hT[:, :, a:b], [t_hT[k] for k in range(a // 128, (b - 1) // 128 + 1)], e2 - t))
                    else:
                        e2 = end
                        a, b = t - NTOK - NH + NH, e2 - NTOK - NH + NH
                        out.append((hTx[:, :, a:b], [t_hTx[k] for k in range(a // 128, (b - 1) // 128 + 1)], e2 - t))
                    t = e2
                return out

            Egen = sbuf(sn, "Egen", [128, 8, 512], BF16)
            t_Egen = T()
            Esp = sbuf(sn, "Esp", [128, 10, 8, 64], BF16)
            t_Esp = T()
            with ExitStack() as stb:
                tabf = sbuf(stb, "tabf", [128, 8, 512], F32)
                t_tabf = T()
                K.dma(SP, lambda e: e.dma_start(out=tabf[:], in_=nat_gen_d[:, :, :]), new_csem(), writes=[t_tabf])
                K.op(ACT, lambda e: e.activation(out=Egen[:], in_=tabf[:], func=AF.Copy), reads=[t_tabf], writes=[t_Egen])
                tabs = sbuf(stb, "tabs", [128, 10, 8, 64], F32)
                t_tabs = T()
                K.dma(SP, lambda e: e.dma_start(out=tabs[:], in_=nat_sp_d[:, :, :, :]), new_csem(), writes=[t_tabs])
                K.op(ACT, lambda e: e.activation(out=Esp[:], in_=tabs[:], func=AF.Copy), reads=[t_tabs], writes=[t_Esp])
                K.barrier()

            wn = sbuf(sn, "wn", [128, 8, 512], BF16)
            t_wn = T()
            t_wng = T()
            qTn = sbuf(sn, "qTn", [128, NTOK], BF16)
            t_qTn = [T() for _ in range(NBLK)]
            kTn = sbuf(sn, "kTn", [128, NEXT], BF16)
            t_kTn = [T() for _ in range(9)]
            vTn = sbuf(sn, "vTn", [128, NEXT], BF16)
            t_vTn = [T() for _ in range(9)]
            Ve = sbuf(sn, "Ve", [128, 36, 2, 65], BF16)
            t_Ve = [T() for _ in range(36)]
            Vo = sbuf(sn, "Vo", [128, 36, 2, 65], BF16)
            t_Vo = [T() for _ in range(36)]
            K.op(POOL, lambda e: e.memset(Ve[:, :, :, 64:65], 1.0), writes=t_Ve)
            K.op(POOL, lambda e: e.memset(Vo[:, :, :, 64:65], 1.0), writes=t_Vo)
            pex = Ring([sbuf(sn, f"pex{i}", [128, 512], F32) for i in range(3)])
            pT = Ring([sbuf(sn, f"pT{i}", [128, 2, 4, 64], BF16) for i in range(8)])
            gateTn = Ring([sbuf(sn, f"gateTn{i}", [128, 512], BF16) for i in range(3)])
            thn = Ring([sbuf(sn, f"thn{i}", [128, 512], F32) for i in range(2)])
            rden = Ring([sbuf(sn, f"rden{i}", [128, 2], F32) for i in range(3)])
            ogt = Ring([sbuf(sn, f"ogt{i}", [128, 128], BF16) for i in range(4)])
            og_q = {id(t): [T(), T()] for t in ogt.ts}
            pS = Ring([pb.bufs[0], pb.bufs[1], pb.bufs[2]], ts=[pb.ts[0], pb.ts[1], pb.ts[2]])
            pG = Ring([_phb[2]], ts=[ph.ts[2]])
            pV = Ring([_phb[0][:, 0:256], _phb[1][:, 0:256]], ts=[ph.ts[0], ph.ts[1]])

            load_wbf(wn[:, :, 0:384], t_wn, "in", OFF_N(0), 384)
            for j in range(4):
                load_wbf(wn[:, :, 384:512], t_wng, "in", OFF_N(j) + 384, 128)
                for blk in range(NBLK):
                    pbuf, tp = pS.next()
                    mm_acc(pbuf[:, :], tp, [(wn[:, c, 0:128], hT[:, c, blk * 512:(blk + 1) * 512]) for c in range(8)],
                           reads=[t_wn] + t_hT[blk * 4:blk * 4 + 4])
                    evac(qTn[:, blk * 512:(blk + 1) * 512], pbuf[:, :], [tp], [t_qTn[blk]])
                for (dstT, tdst, co) in ((kTn, t_kTn, 128), (vTn, t_vTn, 256)):
                    for blk in range(9):
                        pbuf, tp = pS.next()
                        pieces = ext_src(blk * 512, 512)
                        off = 0
                        for (src, ts, ln) in pieces:
                            mm_acc(pbuf[:, off:off + ln], tp, [(wn[:, c, co:co + 128], src[:, c, :]) for c in range(8)],
                                   reads=[t_wn] + ts)
                            off += ln
                        evac(dstT[:, blk * 512:(blk + 1) * 512], pbuf[:, :], [tp], [tdst[blk]])
                for (Vb, tV, n, base) in ((Ve, t_Ve, 36, 0), (Vo, t_Vo, 35, 64)):
                    for e0 in range(0, n, 4):
                        ne = min(4, n - e0)
                        pbk, tpbk = pt.next_bank()
                        for k in range(ne):
                            tok0 = base + (e0 + k) * 128
                            vb0 = tok0 // 512
                            vts = [t_vTn[vb0]] + ([t_vTn[vb0 + 1]] if (tok0 + 127) // 512 != vb0 else [])
                            K.op(PE, lambda e, pbk=pbk, k=k, tok0=tok0: e.transpose(
                                out=pbk[:, k, :], in_=vTn[:, tok0:tok0 + 128], identity=ident[:]),
                                reads=vts + [t_ident], writes=[tpbk])
                        evac(Vb[:, e0:e0 + ne, :, 0:64], pbk[:, 0:ne, :].rearrange("p e (a d) -> p e a d", a=2),
                             [tpbk], [tV[e0 + k] for k in range(ne)])

                if j + 1 < 4:
                    load_wbf(wn[:, :, 0:384], t_wn, "in", OFF_N(j + 1), 384)
                gate_blk = {}
                gate_done = set()

                def nat_gate_steps(blk):
                    thb, tth = thn.next()
                    gtb, tgt = gateTn.next()
                    gate_blk[blk] = (gtb, tgt)
                    gbank = {}
                    steps = []
                    for c0 in range(0, 8, 2):
                        def sg_(c0=c0):
                            if c0 == 0:
                                gbank["b"] = pG.next()
                            pbuf, tp = gbank["b"]
                            for c in (c0, c0 + 1):
                                K.op(PE, lambda e, c=c: e.matmul(pbuf[:, :], lhsT=wn[:, c, 384:512],
                                                                 rhs=hT[:, c, blk * 512:(blk + 1) * 512], start=(c == 0),
                                                                 stop=(c == 7)), reads=[t_wng] + t_hT[blk * 4:blk * 4 + 4],
                                     writes=[tp])
                            if c0 == 6:
                                K.op(ACT, lambda e: e.activation(out=thb[:], in_=pbuf[:, :], func=AF.Tanh, scale=0.5),
                                     reads=[tp], writes=[tth])
                                K.op(DVE, lambda e: e.scalar_tensor_tensor(out=gtb[:], in0=thb[:], scalar=1.0,
                                                                           in1=pbuf[:, :], op0=ALU.add, op1=ALU.mult),
                                     reads=[tth, tp], writes=[tgt])
                        steps.append(sg_)
                    return steps

                nat_pending = []

                stA = {}
                stB = {}
                stage_buf = {}

                def natA(p, j=j):
                    if p % 4 == 0:
                        for gb_ in (p // 4, p // 4 + 1):
                            if gb_ < NBLK and gb_ not in gate_done:
                                gate_done.add(gb_)
                                sts = nat_gate_steps(gb_)
                                if gb_ == 0:
                                    for st_ in sts:
                                        st_()
                                else:
                                    while nat_pending:
                                        nat_pending.pop(0)()
                                    nat_pending.extend(sts)
                    banks = [pS.next(), pS.next()]
                    for b in range(2):
                        for t in range(4):
                            k0 = (2 * p + b + 2 * t) * 64
                            kb = k0 // 512
                            kbs = [t_kTn[kb]] + ([t_kTn[kb + 1]] if (k0 + 127) // 512 != kb else [])
                            col = (b * 4 + t) * 64
                            for a in range(2):
                                pbuf, tp = banks[a]
                                pa = slice(a * 64, (a + 1) * 64)
                                K.op(PE, lambda e, pbuf=pbuf, pa=pa, k0=k0, col=col, b=b: e.matmul(
                                    pbuf[:, col:col + 64], lhsT=kTn[pa, k0:k0 + 128],
                                    rhs=qTn[pa, p * 128 + b * 64:p * 128 + b * 64 + 64], start=True, stop=True),
                                    reads=kbs + [t_qTn[p // 4]], writes=[tp])
                    outs = []
                    for a in range(2):
                        hd = 2 * j + a
                        pbuf, tp = banks[a]
                        pexb, tpex = pex.next()
                        pTb, tpT = pT.next()
                        spec = {}
                        for b in range(2):
                            for t in range(4):
                                if (2 * p + b, t) in SP_TILES:
                                    spec[(b, t)] = SP_TILES.index((2 * p + b, t))
                        if not spec:
                            K.op(DVE, lambda e, pexb=pexb, pbuf=pbuf, hd=hd: e.scalar_tensor_tensor(
                                out=pexb[:], in0=pbuf[:, :], scalar=0.125, in1=Egen[:, hd, :], op0=ALU.mult, op1=ALU.add),
                                reads=[tp, t_Egen], writes=[tpex])
                        else:
                            first = True
                            for b in range(2):
                                for t in range(4):
                                    col = (b * 4 + t) * 64
                                    if (b, t) in spec:
                                        tab = Esp[:, spec[(b, t)], hd, :]
                                    else:
                                        tab = Egen[:, hd, col:col + 64]
                                    K.op(DVE, lambda e, pexb=pexb, pbuf=pbuf, col=col, tab=tab: e.scalar_tensor_tensor(
                                        out=pexb[:, col:col + 64], in0=pbuf[:, col:col + 64], scalar=0.125, in1=tab,
                                        op0=ALU.mult, op1=ALU.add), reads=[tp, t_Egen, t_Esp], writes=[tpex],
                                        same_ok=(not first))
                                    first = False
                        K.op(ACT, lambda e, pexb=pexb, pTb=pTb: e.activation(
                            out=pTb[:].rearrange("p b t q -> p (b t q)"), in_=pexb[:], func=AF.Exp),
                            reads=[tpex], writes=[tpT])
                        outs.append((pTb, tpT))
                    stA[p] = outs

                def natB(p):
                    outs = stA.pop(p)
                    pho, tpho = pV.next()
                    for b in range(2):
                        for a in range(2):
                            for t in range(4):
                                pTb, tpT = outs[a]
                                if b == 0:
                                    vt, tv = Ve[:, p + t, a, :], t_Ve[p + t]
                                else:
                                    vt, tv = Vo[:, p + t, a, :], t_Vo[p + t]
                                K.op(PE, lambda e, pho=pho, pTb=pTb, b=b, t=t, vt=vt, a=a: e.matmul(
                                    pho[b * 64:(b + 1) * 64, a * 65:(a + 1) * 65], lhsT=pTb[:, b, t, :], rhs=vt,
                                    start=(t == 0), stop=(t == 3)), reads=[tpT, tv], writes=[tpho])
                    rdb, trd = rden.next()
                    pho3 = pho[:, 0:130].rearrange("p (a d) -> p a d", a=2)
                    K.op(DVE, lambda e: e.reciprocal(out=rdb[:], in_=pho3[:, :, 64]), reads=[tpho], writes=[trd])
                    ogb, tog = ogt.next()
                    togh = og_q[id(tog)]
                    for a in range(2):
                        K.op(DVE, lambda e, a=a: e.tensor_scalar(out=ogb[:, a * 64:(a + 1) * 64], in0=pho3[:, a, 0:64],
                                                                 scalar1=rdb[:, a:a + 1], scalar2=0.5, op0=ALU.mult,
                                                                 op1=ALU.mult), reads=[tpho, trd], writes=[togh[a]])
                    stB[p] = (ogb, togh)

                def natC(p, j=j):
                    ogb, tog = stB.pop(p)
                    blk = p // 4
                    if p % 4 == 0:
                        stb, tst = mix_stage.next()
                        stage_buf[blk] = (stb, tst, mixsem[(mix_stage.i - 1) % 2])
                    stb, tst, msem = stage_buf[blk]
                    gtb, tgt = gate_blk[blk]
                    q0 = (p % 4) * 128
                    ptt, tpt = pt.next()
                    K.op(PE, lambda e: e.transpose(out=ptt, in_=ogb[:], identity=ident[:]), reads=list(tog) + [t_ident],
                         writes=[tpt])
                    K.op(DVE, lambda e: e.tensor_tensor(out=stb[:, 0, q0:q0 + 128], in0=ptt, in1=gtb[:, q0:q0 + 128],
                                                        op=ALU.mult), reads=[tpt, tgt], writes=[mix_q[id(tst)][p % 4]])
                    if p % 4 == 3:
                        mix_store(stb, tst, msem, 8 + j, 1, blk)
                        del gate_blk[blk]

                LA_, LB_ = 2, 1
                for p in range(LA_):
                    natA(p)
                for p in range(NTILE + LA_ + LB_):
                    if p < NTILE:
                        natB(p)
                    if 0 <= p - LB_ < NTILE:
                        natC(p - LB_)
                    if p + LA_ < NTILE:
                        natA(p + LA_)
                    if nat_pending:
                        nat_pending.pop(0)()
                while nat_pending:
                    nat_pending.pop(0)()
            K.barrier()
        K.barrier()

        if not dbg.get("skip_mem"):
          with ExitStack() as sm_:
            mT = sbuf(sm_, "mT", [128, 8, 256], BF16)
            t_mT = [T() for _ in range(2)]
            with ExitStack() as s0:
                norm_tiles(s0, mem_d, 2, mT, t_mT, "mm")
            K.barrier()
            mem_g = sbuf(sm_, "mem_g_sb", [128, 8, 128], F32)
            t_mem_g = T()
            K.dma(SP, lambda e: e.dma_start(out=mem_g[:], in_=mem_g_d[:, :, :]), new_csem(), writes=[t_mem_g])
            K.barrier()
            mkT = sbuf(sm_, "mkT", [128, 4, 256], BF16)
            t_mkT = T()
            mva = sbuf(sm_, "mva", [128, 2, 4, 129], BF16)
            t_mva = T()
            K.op(POOL, lambda e: e.memset(mva[:, :, :, 128:129], 1.0), writes=[t_mva])
            with ExitStack() as sk:
                wkvm = sbuf(sk, "wkvm", [128, 8, 1024], BF16)
                t_wkvm = T()
                load_wbf(wkvm[:, :, :], t_wkvm, "kv", 0, 1024)
                for h in range(4):
                    phb, tph = ph.next()
                    mm_acc(phb[:, :], tph, [(wkvm[:, c, h * 128:(h + 1) * 128], mT[:, c, :]) for c in range(8)],
                           reads=[t_wkvm] + t_mT)
                    evac(mkT[:, h, :], phb[:, :], [tph], [t_mkT])
                for mt in range(2):
                    pbuf, tp = pb.next()
                    mm_acc(pbuf[:, :], tp, [(mT[:, c, mt * 128:(mt + 1) * 128], wkvm[:, c, 512:1024]) for c in range(8)],
                           reads=[t_wkvm, t_mT[mt]])
                    evac(mva[:, mt, :, 0:128], pbuf[:, :].rearrange("p (h d) -> p h d", h=4), [tp], [t_mva])
                K.barrier()
            wm = sbuf(sm_, "wm", [128, 8, 256], BF16)
            t_wm = T()
            t_wmg = T()
            qTm = sbuf(sm_, "qTm", [128, NTOK], BF16)
            t_qTm = [T() for _ in range(NBLK)]
            pTm = Ring([sbuf(sm_, f"pTm{i}", [128, 2, 512], BF16) for i in range(3)])
            gateTm = Ring([sbuf(sm_, f"gateTm{i}", [128, 512], BF16) for i in range(3)])
            thm = Ring([sbuf(sm_, f"thm{i}", [128, 512], F32) for i in range(2)])
            rden = Ring([sbuf(sm_, f"rdenm{i}", [128, 2], F32) for i in range(3)])
            ogt = Ring([sbuf(sm_, f"ogtm{i}", [128, 128], BF16) for i in range(4)])
            pS = Ring([pb.bufs[0], pb.bufs[1], pb.bufs[2], _phb[2]], ts=[pb.ts[0], pb.ts[1], pb.ts[2], ph.ts[2]])
            pV = Ring([_phb[0][:, 0:256], _phb[1][:, 0:256]], ts=[ph.ts[0], ph.ts[1]])
            load_wbf(wm[:, :, 0:128], t_wm, "in", OFF_M(0), 128)
            for h in range(4):
                load_wbf(wm[:, :, 128:256], t_wmg, "in", OFF_M(h) + 128, 128)
                for blk in range(NBLK):
                    pbuf, tp = pS.next()
                    mm_acc(pbuf[:, :], tp, [(wm[:, c, 0:128], hT[:, c, blk * 512:(blk + 1) * 512]) for c in range(8)],
                           reads=[t_wm] + t_hT[blk * 4:blk * 4 + 4])
                    evac(qTm[:, blk * 512:(blk + 1) * 512], pbuf[:, :], [tp], [t_qTm[blk]])
                if h + 1 < 4:
                    load_wbf(wm[:, :, 0:128], t_wm, "in", OFF_M(h + 1), 128)
                blkA = {}
                stB = {}
                stage_buf = {}

                def memA_steps(blk, h=h):
                    pTb, tpT = pTm.next()
                    gtb, tgt = gateTm.next()
                    thb, tth = thm.next()
                    blkA[blk] = (pTb, tpT, gtb, tgt)
                    steps = []
                    for mt in range(2):
                        def st(mt=mt):
                            pbuf, tp = pS.next()
                            K.op(PE, lambda e: e.matmul(
                                pbuf[:, :], lhsT=mkT[:, h, mt * 128:(mt + 1) * 128], rhs=qTm[:, blk * 512:(blk + 1) * 512],
                                start=True, stop=True), reads=[t_mkT, t_qTm[blk]], writes=[tp])
                            K.op(ACT, lambda e: e.activation(out=pTb[:, mt, :], in_=pbuf[:, :], func=AF.Exp,
                                                             scale=128.0 ** -0.5), reads=[tp], writes=[tpT])
                        steps.append(st)
                    gbank = {}
                    for c0 in range(0, 8, 2):
                        def sg_(c0=c0):
                            if c0 == 0:
                                gbank["b"] = pS.next()
                            pbuf, tp = gbank["b"]
                            for c in (c0, c0 + 1):
                                K.op(PE, lambda e, c=c: e.matmul(pbuf[:, :], lhsT=wm[:, c, 128:256],
                                                                 rhs=hT[:, c, blk * 512:(blk + 1) * 512], start=(c == 0),
                                                                 stop=(c == 7)), reads=[t_wmg] + t_hT[blk * 4:blk * 4 + 4],
                                     writes=[tp])
                            if c0 == 6:
                                K.op(ACT, lambda e: e.activation(out=thb[:], in_=pbuf[:, :], func=AF.Tanh, scale=0.5),
                                     reads=[tp], writes=[tth])
                                K.op(DVE, lambda e: e.scalar_tensor_tensor(out=gtb[:], in0=thb[:], scalar=1.0,
                                                                           in1=pbuf[:, :], op0=ALU.add, op1=ALU.mult),
                                     reads=[tth, tp], writes=[tgt])
                        steps.append(sg_)
                    return steps

                def memB(p, h=h):
                    blk = p // 4
                    pTb, tpT, gtb, tgt = blkA[blk]
                    q0 = (p % 4) * 128
                    for _ in range(dbg.get("mem_fill", 0)):
                        jb, tj = pS.next()
                        K.op(PE, lambda e, jb=jb: e.matmul(jb[:, :], lhsT=wm[:, 0, 0:128], rhs=qTm[:, 0:512],
                                                           start=True, stop=True), reads=[t_wm], writes=[tj])
                    pho, tpho = pV.next()
                    for mt in range(2):
                        K.op(PE, lambda e, mt=mt: e.matmul(pho[:, 0:129], lhsT=pTb[:, mt, q0:q0 + 128], rhs=mva[:, mt, h, :],
                                                           start=(mt == 0), stop=(mt == 1)), reads=[tpT, t_mva], writes=[tpho])
                    rdb, trd = rden.next()
                    K.op(DVE, lambda e: e.reciprocal(out=rdb[:, 0:1], in_=pho[:, 128:129]), reads=[tpho], writes=[trd])
                    ogb, tog = ogt.next()
                    K.op(DVE, lambda e: e.tensor_scalar(out=ogb[:], in0=pho[:, 0:128], scalar1=rdb[:, 0:1], scalar2=0.5,
                                                        op0=ALU.mult, op1=ALU.mult), reads=[tpho, trd], writes=[tog])
                    stB[p] = (ogb, tog)

                def memC(p, h=h):
                    ogb, tog = stB.pop(p)
                    blk = p // 4
                    if p % 4 == 0:
                        stb, tst = mix_stage.next()
                        stage_buf[blk] = (stb, tst, mixsem[(mix_stage.i - 1) % 2])
                    stb, tst, msem = stage_buf[blk]
                    pTb, tpT, gtb, tgt = blkA[blk]
                    q0 = (p % 4) * 128
                    ptt, tpt = pt.next()
                    K.op(PE, lambda e: e.transpose(out=ptt, in_=ogb[:], identity=ident[:]), reads=[tog, t_ident],
                         writes=[tpt])
                    K.op(DVE, lambda e: e.tensor_tensor(out=stb[:, 0, q0:q0 + 128], in0=ptt, in1=gtb[:, q0:q0 + 128],
                                                        op=ALU.mult), reads=[tpt, tgt], writes=[mix_q[id(tst)][p % 4]])
                    if p % 4 == 3:
                        mix_store(stb, tst, msem, 12 + h, 1, blk)
                        del blkA[blk]

                for st in memA_steps(0):
                    st()
                pending = []
                for p in range(NTILE + 1):
                    if p % 4 == 0 and p // 4 + 1 < NBLK:
                        pending = memA_steps(p // 4 + 1)
                    if p < NTILE:
                        memB(p)
                    if p >= 1:
                        memC(p - 1)
                    k = 2 if p % 4 < 2 else 1
                    for _ in range(k):
                        if pending:
                            pending.pop(0)()
                    if p % 4 == 3:
                        while pending:
                            pending.pop(0)()
            K.barrier()
        K.barrier()

        if not dbg.get("skip_gla"):
          with ExitStack() as sg:
            wg = sbuf(sg, "wg", [128, 8, 768], BF16)
            t_wg = T()
            qTg = sbuf(sg, "qTg", [128, NTOK], BF16)
            kTg = sbuf(sg, "kTg", [128, NTOK], BF16)
            t_qTg = [T() for _ in range(NBLK)]
            t_kTg = [T() for _ in range(NBLK)]
            vg = sbuf(sg, "vg", [128, NTILE, 256], BF16)
            t_vg = [T() for _ in range(NTILE)]
            oacc = sbuf(sg, "oacc", [128, NTILE, 256], F32)
            t_oacc = [T() for _ in range(NTILE)]
            gnfm = sbuf(sg, "gnfm", [128, 2], F32)
            t_gnfm = T()
            K.dma(SP, lambda e: e.dma_start(out=cf32[:, 0:2], in_=gnorm_fm_d[:, :]), new_csem(), writes=[t_cf32])
            K.op(DVE, lambda e: e.tensor_scalar(out=gnfm[:], in0=cf32[:, 0:2], scalar1=0.5, scalar2=None, op0=ALU.mult),
                 reads=[t_cf32], writes=[t_gnfm])
            csl = [Ring([sbuf(sg, f"csl{d}{i}", [128, 512], F32) for i in range(2)]) for d in range(2)]
            cslsem = [[K.dsem() for _ in range(2)] for _ in range(2)]
            exr = Ring([sbuf(sg, f"exr{i}", [128, 512], F32) for i in range(2)])
            qB = [Ring([sbuf(sg, f"qB{d}{i}", [128, 512], BF16) for i in range(2)]) for d in range(2)]
            kB = [Ring([sbuf(sg, f"kB{d}{i}", [128, 512], BF16) for i in range(2)]) for d in range(2)]
            nbl = [Ring([sbuf(sg, f"nbl{d}{i}", [128, 8], F32) for i in range(2)]) for d in range(2)]
            edec = Ring([sbuf(sg, f"gedec{i}", [128, 128], F32) for i in range(3)])
            kdT = Ring([sbuf(sg, f"gkdT{i}", [128, 128], BF16) for i in range(6)])
            kd = Ring([sbuf(sg, f"gkd{i}", [128, 128], BF16) for i in range(6)])
            attm = Ring([sbuf(sg, f"attm{i}", [128, 128], BF16) for i in range(6)])
            Sf = [sbuf(sg, f"Sf{d}", [128, 256], F32) for d in range(2)]
            t_Sf = [T(), T()]
            Sb = [Ring([sbuf(sg, f"Sb{d}{i}", [128, 256], BF16) for i in range(2)]) for d in range(2)]
            ssall = sbuf(sg, "ssall", [128, 2, NTILE], F32)
            t_ssall = T()
            t_ss = [T() for _ in range(NTILE)]
            gateAll = sbuf(sg, "gateAll", [128, 2, NTOK], BF16)
            t_gate = [[T(), T()] for _ in range(NBLK)]
            onb = Ring([sbuf(sg, f"onb{i}", [128, 256], BF16) for i in range(4)])
            pU = Ring([_phb[0][:, 0:256], _phb[1][:, 0:256]], ts=[ph.ts[0], ph.ts[1]])
            pA = Ring([pb.bufs[2][:, 0:128], _phb[2][:, 0:128]], ts=[pb.ts[2], ph.ts[2]])
            pO = Ring([pb.bufs[0][:, 0:256], pb.bufs[1][:, 0:256]], ts=[pb.ts[0], pb.ts[1]])

            load_wbf(wg[:, :, :], t_wg, "in", OFF_G(0), 768)
            for h in range(4):
                for blk in range(NBLK):
                    for (dst, tds, co) in ((qTg, t_qTg, 0), (kTg, t_kTg, 128)):
                        pbuf, tp = pb.next()
                        mm_acc(pbuf[:, :], tp, [(wg[:, c, co:co + 128], hT[:, c, blk * 512:(blk + 1) * 512])
                                                for c in range(8)], reads=[t_wg] + t_hT[blk * 4:blk * 4 + 4])
                        evac(dst[:, blk * 512:(blk + 1) * 512], pbuf[:, :], [tp], [tds[blk]])
                for blk in range(NBLK):
                    for cc in range(2):
                        pbuf, tp = pb.next()
                        mm_acc(pbuf[:, :], tp, [(wg[:, c, 512 + cc * 128:512 + (cc + 1) * 128],
                                                 hT[:, c, blk * 512:(blk + 1) * 512]) for c in range(8)],
                               reads=[t_wg] + t_hT[blk * 4:blk * 4 + 4])
                        thb, tth = exr.next()
                        K.op(ACT, lambda e, thb=thb, pbuf=pbuf: e.activation(out=thb[:], in_=pbuf[:, :], func=AF.Tanh,
                                                                             scale=0.5), reads=[tp], writes=[tth])
                        K.op(DVE, lambda e, thb=thb, pbuf=pbuf: e.scalar_tensor_tensor(
                            out=thb[:], in0=thb[:], scalar=1.0, in1=pbuf[:, :], op0=ALU.add, op1=ALU.mult),
                            reads=[tth, tp], writes=[tth])
                        K.op(ACT, lambda e, thb=thb, cc=cc, blk=blk: e.activation(
                            out=gateAll[:, cc, blk * 512:(blk + 1) * 512], in_=thb[:], func=AF.Copy,
                            scale=gnfm[:, cc:cc + 1]), reads=[tth, t_gnfm], writes=[t_gate[blk][cc]])
                for p in range(0, NTILE, 2):
                    pbuf, tp = pb.next()
                    for k2 in range(2):
                        mm_acc(pbuf[:, k2 * 256:(k2 + 1) * 256], tp,
                               [(hT[:, c, (p + k2) * 128:(p + k2 + 1) * 128], wg[:, c, 256:512]) for c in range(8)],
                               reads=[t_wg, t_hT[p + k2]])
                    evac(vg[:, p:p + 2, :], pbuf[:, :].rearrange("p (a d) -> p a d", a=2), [tp], [t_vg[p], t_vg[p + 1]])
                if h + 1 < 4:
                    load_wbf(wg[:, :, :], t_wg, "in", OFF_G(h + 1), 768)
                cur_Sb = [None, None]
                for d in range(2):
                    u = d * 4 + h
                    K.op(POOL, lambda e, d=d, u=u: e.tensor_copy(out=Sf[d][:], in_=S0[:, u, :]), reads=[t_S0[u]],
                         writes=[t_Sf[d]])
                    sbb, tsb = Sb[d].next()
                    K.op(POOL, lambda e, sbb=sbb, u=u: e.tensor_copy(out=sbb[:], in_=S0[:, u, :]), reads=[t_S0[u]],
                         writes=[tsb])
                    cur_Sb[d] = (sbb, tsb)
                first_written = [False] * NTILE

                blkstate = {}

                def prep_block(d, blk, h=h):
                    u = d * 4 + h
                    csb, tcs = csl[d].next()
                    sem = cslsem[d][(csl[d].i - 1) % 2]
                    K.dma(SP, lambda e: e.dma_start(out=csb[:], in_=cs_d[u, :, blk * 512:(blk + 1) * 512]), sem,
                          reads=[t_cs_d[u][blk]], writes=[tcs])
                    qb, tqb = qB[d].next()
                    kb, tkb = kB[d].next()
                    nb, tnb = nbl[d].next()
                    ex0, tex0 = exr.next()
                    K.op(ACT, lambda e: e.activation(out=ex0[:], in_=csb[:], func=AF.Exp, scale=-1.0 / 16),
                         reads=[tcs], writes=[tex0])
                    K.op(DVE, lambda e: e.scalar_tensor_tensor(
                        out=qb[:], in0=qTg[:, blk * 512:(blk + 1) * 512], scalar=128.0 ** -0.5, in1=ex0[:],
                        op0=ALU.mult, op1=ALU.mult), reads=[t_qTg[blk], tex0], writes=[tqb])
                    ex1, tex1 = exr.next()
                    K.op(ACT, lambda e: e.activation(out=ex1[:], in_=csb[:], func=AF.Exp, scale=1.0 / 16),
                         reads=[tcs], writes=[tex1])
                    K.op(POOL, lambda e: e.tensor_tensor(
                        out=kb[:], in0=kTg[:, blk * 512:(blk + 1) * 512], in1=ex1[:], op=ALU.mult),
                        reads=[t_kTg[blk], tex1], writes=[tkb])
                    lc0 = 127 if d == 0 else 0
                    K.op(DVE, lambda e: e.tensor_scalar(
                        out=nb[:, 0:4], in0=csb[:].rearrange("p (c t) -> p c t", c=4)[:, :, lc0], scalar1=-1.0 / 16,
                        scalar2=None, op0=ALU.mult), reads=[tcs], writes=[tnb])
                    K.op(ACT, lambda e: e.activation(out=nb[:, 4:8], in_=nb[:, 0:4], func=AF.Exp), reads=[tnb], writes=[tnb])
                    blkstate[(d, blk)] = (csb, tcs, qb, tqb, kb, tkb, nb, tnb)

                items = []
                for step in range(NBLK):
                    for k4 in range(4):
                        items.append((0, step, k4))
                        items.append((1, NBLK - 1 - step, 3 - k4))
                pre_out = {}

                stA_out = {}

                def stageA(i):
                    d, blk, c4 = items[i]
                    if (d, blk) not in blkstate:
                        prep_block(d, blk)
                    csb, tcs, qb, tqb, kb, tkb, nb, tnb = blkstate[(d, blk)]
                    p = blk * 4 + c4
                    cs_ = slice(c4 * 128, (c4 + 1) * 128)
                    edb, ted = edec.next()
                    K.op(ACT, lambda e: e.activation(out=edb[:], in_=csb[:, cs_], func=AF.Exp, scale=1.0 / 16,
                                                     bias=nb[:, c4:c4 + 1]), reads=[tcs, tnb], writes=[ted])
                    kdTb, tkdT = kdT.next()
                    K.op(POOL, lambda e: e.tensor_tensor(out=kdTb[:], in0=kTg[:, p * 128:(p + 1) * 128], in1=edb[:],
                                                         op=ALU.mult), reads=[t_kTg[blk], ted], writes=[tkdT])
                    stA_out[i] = (kdTb, tkdT)

                def stageB(i):
                    d, blk, c4 = items[i]
                    csb, tcs, qb, tqb, kb, tkb, nb, tnb = blkstate[(d, blk)]
                    kdTb, tkdT = stA_out.pop(i)
                    cs_ = slice(c4 * 128, (c4 + 1) * 128)
                    ptt, tpt = pt.next()
                    K.op(PE, lambda e: e.transpose(out=ptt, in_=kdTb[:], identity=ident[:]),
                         reads=[tkdT, t_ident], writes=[tpt])
                    kdb, tkd = kd.next()
                    evac(kdb[:], ptt, [tpt], [tkd], eng=ACT)
                    pha, tpha = pA.next()
                    K.op(PE, lambda e: e.matmul(pha, lhsT=kb[:, cs_], rhs=qb[:, cs_], start=True, stop=True),
                         reads=[tkb, tqb], writes=[tpha])
                    atb, tat = attm.next()
                    K.op(DVE, lambda e: e.tensor_tensor(out=atb[:], in0=pha, in1=cmask[:, d * 128:(d + 1) * 128],
                                                        op=ALU.mult), reads=[tpha, t_cmask], writes=[tat])
                    pre_out[i] = (atb, tat, kdb, tkd)

                def post(i):
                    d, blk, c4 = items[i]
                    csb, tcs, qb, tqb, kb, tkb, nb, tnb = blkstate[(d, blk)]
                    atb, tat, kdb, tkd = pre_out.pop(i)
                    p = blk * 4 + c4
                    cs_ = slice(c4 * 128, (c4 + 1) * 128)
                    sbb, tsb = cur_Sb[d]
                    phu, tphu = pU.next()
                    K.op(PE, lambda e: e.matmul(phu, lhsT=kdb[:], rhs=vg[:, p, :], start=True, stop=True),
                         reads=[tkd, t_vg[p]], writes=[tphu])
                    pho, tpho = pO.next()
                    K.op(PE, lambda e: e.matmul(pho, lhsT=atb[:], rhs=vg[:, p, :], start=True, stop=False),
                         reads=[tat, t_vg[p]], writes=[tpho])
                    K.op(PE, lambda e: e.matmul(pho, lhsT=qb[:, cs_], rhs=sbb[:], start=False, stop=True),
                         reads=[tqb, tsb], writes=[tpho])
                    K.op(DVE, lambda e: e.scalar_tensor_tensor(
                        out=Sf[d][:], in0=Sf[d][:], scalar=nb[:, 4 + c4:5 + c4], in1=phu, op0=ALU.mult, op1=ALU.add),
                        reads=[tnb, tphu], writes=[t_Sf[d]])
                    nsb, tnsb = Sb[d].next()
                    K.op(DVE, lambda e: e.tensor_copy(out=nsb[:], in_=Sf[d][:]), reads=[t_Sf[d]], writes=[tnsb])
                    cur_Sb[d] = (nsb, tnsb)
                    if not first_written[p]:
                        K.op(ACT, lambda e: e.activation(out=oacc[:, p, :], in_=pho, func=AF.Copy),
                             reads=[tpho], writes=[t_oacc[p]])
                        first_written[p] = True
                    else:
                        K.op(DVE, lambda e: e.tensor_tensor(out=oacc[:, p, :], in0=pho, in1=oacc[:, p, :], op=ALU.add),
                             reads=[tpho, t_oacc[p]], writes=[t_oacc[p]])
                        K.op(ACT, lambda e: e.activation(out=junk[:, 0:256], in_=oacc[:, p, :], func=AF.Square,
                                                         accum_out=ssall[:, 0, p:p + 1]),
                             reads=[t_oacc[p]], writes=[t_junk, t_ss[p]])

                n_it = len(items)
                LA, LB = 5, 2
                for i in range(min(LA, n_it)):
                    stageA(i)
                for i in range(min(LB, n_it)):
                    stageB(i)
                for i in range(n_it):
                    if i + LA < n_it:
                        stageA(i + LA)
                    if i + LB < n_it:
                        stageB(i + LB)
                    post(i)

                K.op(ACT, lambda e: e.activation(out=ssall[:, 1, :], in_=ssall[:, 0, :], func=AF.Ln, scale=1.0 / 256,
                                                 bias=EPS_AP[:, 0:1]), reads=t_ss + [t_eps], writes=[t_ssall])
                K.op(ACT, lambda e: e.activation(out=ssall[:, 1, :], in_=ssall[:, 1, :], func=AF.Exp, scale=-0.5),
                     reads=[t_ssall], writes=[t_ssall])

                fA = {}
                fB = {}
                fstage = {}

                def finA(p):
                    onbb, tonb = onb.next()
                    K.op(ACT, lambda e: e.activation(out=onbb[:], in_=oacc[:, p, :], func=AF.Copy,
                                                     scale=ssall[:, 1, p:p + 1]), reads=[t_oacc[p], t_ssall], writes=[tonb])
                    fA[p] = (onbb, tonb)

                def finB(p):
                    onbb, tonb = fA.pop(p)
                    pbk, tpbk = pt.next_bank()
                    for cc in range(2):
                        K.op(PE, lambda e, cc=cc: e.transpose(out=pbk[:, cc, :], in_=onbb[:, cc * 128:(cc + 1) * 128],
                                                              identity=ident[:]), reads=[tonb, t_ident], writes=[tpbk])
                    fB[p] = (pbk, tpbk)

                def finC(p, h=h):
                    pbk, tpbk = fB.pop(p)
                    blk = p // 4
                    if p % 4 == 0:
                        stb, tst = mix_stage.next()
                        fstage[blk] = (stb, tst, mixsem[(mix_stage.i - 1) % 2])
                    stb, tst, msem = fstage[blk]
                    q0 = (p % 4) * 128
                    K.op(DVE, lambda e: e.tensor_tensor(out=stb[:, :, q0:q0 + 128], in0=pbk[:, 0:2, :],
                                                        in1=gateAll[:, :, p * 128:(p + 1) * 128], op=ALU.mult),
                         reads=[tpbk] + t_gate[blk], writes=[mix_q[id(tst)][p % 4]])
                    if p % 4 == 3:
                        mix_store(stb, tst, msem, 2 * h, 2, blk)

                finA(0)
                finA(1)
                finB(0)
                for p in range(NTILE):
                    if p + 2 < NTILE:
                        finA(p + 2)
                    if p + 1 < NTILE:
                        finB(p + 1)
                    finC(p)
            K.barrier()
        K.barrier()

        with ExitStack() as so:
          if dbg.get("skip_out"):
            K.final_wait(SP)
          else:
              wo = sbuf(so, "wo", [128, 16, 1024], BF16)
              t_wo = T()
              for kh in range(2):
                  load_wbf(wo[:, kh * 8:(kh + 1) * 8, :], t_wo, "out", 0, 1024) if False else None
              load_wbf(wo[:, :, :], t_wo, "out", 0, 1024)
              gB = sbuf(so, "gB", [128, 1024], F32)
              t_gB = T()
              K.dma(SP, lambda e: e.dma_start(out=gB[:], in_=post_g_d[:, :]), new_csem(), writes=[t_gB])
              mx = Ring([sbuf(so, f"mx{i}", [128, 16, 512], BF16) for i in range(2)])
              mxsem = [K.dsem() for _ in range(2)]
              xr = Ring([sbuf(so, f"xr{i}", [128, 1024], F32) for i in range(3)])
              xrsem = [K.dsem() for _ in range(3)]
              yt = Ring([sbuf(so, f"yt{i}", [128, 1024], F32) for i in range(3)])
              ysem = [K.dsem() for _ in range(3)]
              smo = Ring([sbuf(so, f"smo{i}", [128, 4], F32) for i in range(3)])
              pO6 = Ring([pb.bufs[0], pb.bufs[1], pb.bufs[2], _phb[0], _phb[1], _phb[2]],
                         ts=[pb.ts[0], pb.ts[1], pb.ts[2], ph.ts[0], ph.ts[1], ph.ts[2]])
              mxblk = {}
              oA = {}

              def load_mx(blk):
                  mxb, tmx = mx.next()
                  sem = mxsem[(mx.i - 1) % 2]
                  for kh in range(2):
                      K.dma(SP, lambda e, kh=kh: e.dma_start(
                          out=mxb[:, kh * 8:(kh + 1) * 8, :],
                          in_=mixT_d[kh * 8:(kh + 1) * 8, :, blk * 512:(blk + 1) * 512].rearrange("c p t -> p c t")), sem,
                          reads=[t_mixT[c][blk] for c in range(kh * 8, kh * 8 + 8)], writes=[tmx])
                  mxblk[blk] = (mxb, tmx)

              def outA(p):
                  blk = p // 4
                  if blk not in mxblk:
                      load_mx(blk)
                  if p % 4 == 0 and blk + 1 < NBLK and (blk + 1) not in mxblk:
                      load_mx(blk + 1)
                  mxb, tmx = mxblk[blk]
                  q0 = (p % 4) * 128
                  xb, tx = xr.next()
                  xsm = xrsem[(xr.i - 1) % 3]
                  K.dma(SP, lambda e: e.dma_start(out=xb[:], in_=x_main[p * 128:(p + 1) * 128, :]), xsm, writes=[tx])
                  smb, tsm = smo.next()
                  pbs = []
                  for hf in range(2):
                      pbuf, tp = pO6.next()
                      mm_acc(pbuf[:, :], tp, [(mxb[:, kc, q0:q0 + 128], wo[:, kc, hf * 512:(hf + 1) * 512])
                                              for kc in range(16)], reads=[tmx, t_wo])
                      K.op(ACT, lambda e, pbuf=pbuf, hf=hf: e.activation(
                          out=junk[:, 0:512], in_=pbuf[:, :], func=AF.Square, accum_out=smb[:, hf:hf + 1]),
                          reads=[tp], writes=[t_junk, tsm])
                      pbs.append((pbuf, tp))
                  oA[p] = (xb, tx, smb, tsm, pbs)

              def outB(p):
                  xb, tx, smb, tsm, pbs = oA.pop(p)
                  K.op(DVE, lambda e: e.tensor_tensor(out=smb[:, 2:3], in0=smb[:, 0:1], in1=smb[:, 1:2], op=ALU.add),
                       reads=[tsm], writes=[tsm])
                  K.op(ACT, lambda e: e.activation(out=smb[:, 3:4], in_=smb[:, 2:3], func=AF.Ln, scale=1.0 / 1024,
                                                   bias=EPS_AP[:, 0:1]), reads=[tsm, t_eps], writes=[tsm])
                  K.op(ACT, lambda e: e.activation(out=smb[:, 3:4], in_=smb[:, 3:4], func=AF.Exp, scale=-0.5),
                       reads=[tsm], writes=[tsm])
                  yb, ty = yt.next()
                  ysm = ysem[(yt.i - 1) % 3]
                  for hf, (pbuf, tp) in enumerate(pbs):
                      K.op(DVE, lambda e, pbuf=pbuf, hf=hf: e.scalar_tensor_tensor(
                          out=yb[:, hf * 512:(hf + 1) * 512], in0=pbuf[:, :], scalar=smb[:, 3:4],
                          in1=gB[:, hf * 512:(hf + 1) * 512], op0=ALU.mult, op1=ALU.mult),
                          reads=[tp, tsm, t_gB], writes=[ty])
                  K.op(POOL, lambda e: e.tensor_tensor(out=yb[:], in0=yb[:], in1=xb[:], op=ALU.add),
                       reads=[tx, ty], writes=[ty])
                  K.dma(SP, lambda e: e.dma_start(out=y_out[p * 128:(p + 1) * 128, :], in_=yb[:]), ysm, reads=[ty])

              outA(0)
              for p in range(NTILE):
                  if p + 1 < NTILE:
                      outA(p + 1)
                  outB(p)
              K.final_wait(SP)
        with nc.Block() as block:
            K.replay(block)
    return nc


def _nat_tables(rpb, first, last):
    kc = np.arange(64)[:, None]
    qc = np.arange(64)[None, :]
    cs = np.clip(qc - 8, 0, 48)
    col_in = (kc >= cs) & (kc < cs + 16)
    dcol = np.clip(kc - qc + 15, 0, 30)

    def blockfor(drow):
        if drow < 0 or drow > 14:
            return np.full((8, 64, 64), MASKV, np.float32)
        b = rpb[:, drow][:, dcol]
        return np.where(col_in[None], b, np.float32(MASKV)).astype(np.float32)

    gen = np.empty((128, 8, 2, 4, 64), np.float32)
    for t in range(4):
        for a in range(2):
            blk = blockfor(3 + 2 * t + a)
            for b in range(2):
                gen[a * 64:(a + 1) * 64, :, b, t, :] = blk.transpose(1, 0, 2)
    sp = np.empty((128, 10, 8, 64), np.float32)
    for i, (r, t) in enumerate(SP_TILES):
        for a in range(2):
            slot = r - 4 + 2 * t + a
            drow = 3 + 2 * t + a
            if first and slot < 0:
                drow = 2 * t + a + 11
            if last and slot >= 64:
                drow = 2 * t + a - 5 if slot <= 66 else -1
            sp[a * 64:(a + 1) * 64, i, :, :] = blockfor(drow).transpose(1, 0, 2)
    return gen.reshape(128, 8, 512), sp


def _core_inputs(c, inp, consts):
    if c < 4:
        xs, mem, seg, nseg = inp["x_prompt"][0], inp["mem_prompt"][0], c, 4
    else:
        b = (c - 4) // 2
        xs, mem, seg, nseg = inp["x_sample"][b], inp["mem_sample"][b], (c - 4) % 2, 2
    t0 = seg * NTOK
    first, last = seg == 0, seg == nseg - 1
    x_main = xs[t0:t0 + NTOK]
    x_gh = np.zeros((2 * GH, 1024), np.float32)
    if not first:
        x_gh[:GH] = xs[t0 - GH:t0]
    if not last:
        x_gh[GH:] = xs[t0 + NTOK:t0 + NTOK + GH]
    x_nh = np.zeros((2 * NH, 1024), np.float32)
    if first:
        x_nh[:NH] = x_main[4 * 64:8 * 64]
    else:
        x_nh[:NH] = xs[t0 - NH:t0]
    if last:
        x_nh[NH:NH + 192] = x_main[56 * 64:59 * 64]
    else:
        x_nh[NH:] = xs[t0 + NTOK:t0 + NTOK + NH]
    gen, sp = _nat_tables(inp["nat_rpb"][0], first, last)
    d = dict(consts)
    d.update(x_main=np.ascontiguousarray(x_main), x_gh=x_gh, x_nh=x_nh, mem=np.ascontiguousarray(mem),
             nat_gen=gen, nat_sp=sp)
    return d


def _consts(inp):
    f = np.float32
    j = np.arange(128)[:, None]
    i = np.arange(128)[None, :]
    cmask = np.concatenate([(j <= i).astype(f), (j >= i).astype(f)], axis=1)
    rmask = np.ones((128, 512), f)
    rmask[:, ::128] = 0.0
    gb = np.concatenate([inp["gla_b_fwd"][0].reshape(4, 128).T, inp["gla_b_bwd"][0].reshape(4, 128).T], axis=1)
    return dict(
        w_in=np.ascontiguousarray(inp["w_in"][0]), w_out=np.ascontiguousarray(inp["w_out"][0]),
        w_mkv=np.ascontiguousarray(inp["w_mem_kv"][0]),
        pre_g=np.ascontiguousarray(np.broadcast_to(inp["pre_norm_g"][0].reshape(8, 128).T[:, :, None], (128, 8, 128))),
        mem_g=np.ascontiguousarray(np.broadcast_to(inp["mem_norm_g"][0].reshape(8, 128).T[:, :, None], (128, 8, 128))),
        post_g=np.ascontiguousarray(np.broadcast_to(inp["post_norm_g"][0][None, :], (128, 1024))),
        gnorm=np.ascontiguousarray(np.broadcast_to(inp["gla_norm_g"][0][None, :], (128, 256))),
        gnorm_fm=np.ascontiguousarray(inp["gla_norm_g"][0].reshape(2, 128).T),
        gw_f=np.ascontiguousarray(inp["gla_w_fwd"][0]), gw_b=np.ascontiguousarray(inp["gla_w_bwd"][0]),
        gb=np.ascontiguousarray(gb.astype(f)),
        ident=np.eye(128, dtype=f), cmask=cmask, rmask=rmask)


def kernel(**inputs):
    inp = {k: np.asarray(v) for k, v in inputs.items()}
    consts = _consts(inp)
    in_maps = [_core_inputs(c, inp, consts) for c in range(8)]
    nc = build_nc()
    res = run_bass_kernel_spmd(nc, in_maps, core_ids=list(range(8)))
    ys = [np.asarray(r["y"], dtype=np.float32) for r in res.results]
    y_prompt = np.concatenate(ys[0:4], axis=0)[None]
    y_sample = np.stack([np.concatenate(ys[4:6], axis=0), np.concatenate(ys[6:8], axis=0)], axis=0)
    return (y_prompt, y_sample)
```
